# Optimizing a Trainium2 kernel written in Bass

```python
import math
import jax, jax.numpy as jnp
from jax import lax
import numpy as np

D_MODEL = 1024
BATCH = 8
SEQ = 2048
DEPTH = 4
DEC_BATCH = 128
DEC_SEQ = 4
PAST_LEN = 8192
PAGE_SIZE = 128

W_A = D_MODEL // 2
POOL_WINDOWS = (2, 4, 8, 16)
POOL_GROUP = W_A // len(POOL_WINDOWS)
POOL_BUF = max(POOL_WINDOWS) - 1
HEAD_DIM = 64
N_HEADS = (D_MODEL // 2) // HEAD_DIM
KV_HEADS = 2
Q_PER_KV = N_HEADS // KV_HEADS
W_B = N_HEADS * HEAD_DIM
KV_W = KV_HEADS * HEAD_DIM
WINDOW = 128
BLOCK = WINDOW
ROT_DIM = HEAD_DIM // 4
ROPE_THETA = 500000.0
W_C = D_MODEL // 2
SSM_CH = 16
SSM_GROUPS = W_C // SSM_CH
SSM_STATE = 64
N_BRANCH = 3
D_FF = -(-(8 * D_MODEL // 3) // 256) * 256
EPS = 1e-6
IN_SPLITS = (W_A, W_A + W_B, W_A + W_B + KV_W, W_A + W_B + 2 * KV_W, W_A + W_B + 2 * KV_W + W_C)
IN_COLS = W_A + W_B + 2 * KV_W + W_C + N_BRANCH * D_MODEL

kernel_name = "gated_hybrid_pool_swa_s5_decoder_step"


def _rmsnorm(x, g):
    xf = x.astype(jnp.float32)
    r = lax.rsqrt(jnp.mean(xf * xf, axis=-1, keepdims=True) + EPS)
    return (xf * r).astype(x.dtype) * g


def _rope(x, pos):
    f32 = jnp.float32
    inv = ROPE_THETA ** (-jnp.arange(0, ROT_DIM, 2, dtype=f32) / ROT_DIM)
    ang = pos.astype(f32)[:, None] * inv[None, :]
    cos = jnp.cos(ang)[None, :, None, :]
    sin = jnp.sin(ang)[None, :, None, :]
    xr = x[..., :ROT_DIM].astype(f32)
    x1, x2 = xr[..., : ROT_DIM // 2], xr[..., ROT_DIM // 2:]
    rot = jnp.concatenate([x1 * cos - x2 * sin, x2 * cos + x1 * sin], axis=-1).astype(x.dtype)
    return jnp.concatenate([rot, x[..., ROT_DIM:]], axis=-1)


def _sink_attend(s, mask, sinks, v, eq_pv):
    s = jnp.where(mask, s, -jnp.inf)
    sk = sinks.astype(jnp.float32).reshape(KV_HEADS, Q_PER_KV, 1, 1)
    m = jnp.maximum(jnp.max(s, axis=-1, keepdims=True), sk)
    p = jnp.exp(s - m)
    den = jnp.sum(p, axis=-1, keepdims=True) + jnp.exp(sk - m)
    return jnp.einsum(eq_pv, (p / den).astype(v.dtype), v)


def _swa_prompt(q, k, v, sinks):
    bsz, seq = q.shape[:2]
    nb = seq // BLOCK
    qb = q.reshape(bsz, nb, BLOCK, KV_HEADS, Q_PER_KV, HEAD_DIM)

    def band(t):
        tb = t.reshape(bsz, nb, BLOCK, KV_HEADS, HEAD_DIM)
        prev = jnp.concatenate([jnp.zeros_like(tb[:, :1]), tb[:, :-1]], axis=1)
        return jnp.concatenate([prev, tb], axis=2)

    kb, vb = band(k), band(v)
    s = jnp.einsum("bnqgrd,bnkgd->bngrqk", qb, kb, preferred_element_type=jnp.float32) * (HEAD_DIM ** -0.5)
    qi = jnp.arange(BLOCK)[:, None]
    kj = jnp.arange(2 * BLOCK)[None, :]
    diff = BLOCK + qi - kj
    in_band = (diff >= 0) & (diff < WINDOW)
    blk = jnp.arange(nb)[:, None, None]
    mask = in_band[None] & ((blk > 0) | (kj >= BLOCK)[None])
    mask = mask[None, :, None, None]
    o = _sink_attend(s, mask, sinks, vb, "bngrqk,bnkgd->bnqgrd")
    return o.reshape(bsz, seq, W_B), k[:, -WINDOW:], v[:, -WINDOW:]


def _swa_sample(q, k, v, ck, cv, pos, sinks):
    bsz, t = q.shape[:2]
    wb = ck.shape[1]
    ke = jnp.concatenate([ck, k], axis=1)
    ve = jnp.concatenate([cv, v], axis=1)
    kpos = PAST_LEN - wb + jnp.arange(wb + t, dtype=jnp.int32)
    diff = pos[:, None] - kpos[None, :]
    mask = (diff >= 0) & (diff < WINDOW)
    qg = q.reshape(bsz, t, KV_HEADS, Q_PER_KV, HEAD_DIM)
    s = jnp.einsum("bqgrd,bkgd->bgrqk", qg, ke, preferred_element_type=jnp.float32) * (HEAD_DIM ** -0.5)
    o = _sink_attend(s, mask, sinks, ve, "bgrqk,bkgd->bqgrd")
    return o.reshape(bsz, t, W_B), ke[:, -wb:], ve[:, -wb:]


def _pool_mix(xa, prev, pos, pool_w, pool_scale):
    t = xa.shape[1]
    xe = jnp.concatenate([prev, xa], axis=1)
    cs = jnp.cumsum(xe.astype(jnp.float32), axis=1)
    cs = jnp.concatenate([jnp.zeros_like(cs[:, :1]), cs], axis=1)
    end = cs[:, POOL_BUF + 1:]
    outs = []
    for g, w in enumerate(POOL_WINDOWS):
        lo, hi = g * POOL_GROUP, (g + 1) * POOL_GROUP
        start = cs[:, POOL_BUF + 1 - w: POOL_BUF + 1 - w + t, lo:hi]
        cnt = jnp.minimum(pos + 1, w).astype(jnp.float32)[None, :, None]
        d = ((end[..., lo:hi] - start) / cnt - xa[..., lo:hi].astype(jnp.float32)).astype(xa.dtype)
        outs.append(d @ pool_w[g])
    y = jnp.concatenate(outs, axis=-1) * pool_scale
    return y, xe[:, -POOL_BUF:]


def _lin_combine(left, right):
    return (left[0] * right[0], right[0] * left[1] + right[1])


def _s5(u, h0_re, h0_im, a_re, a_im, log_dt, b_re, b_im, c_re, c_im, d, w_glu):
    f32 = jnp.float32
    bsz, t = u.shape[:2]
    uf = u.astype(f32).reshape(bsz, t, SSM_GROUPS, SSM_CH)
    a = lax.complex(a_re.astype(f32), a_im.astype(f32))
    dt = jnp.exp(log_dt.astype(f32))[:, None]
    a_bar = jnp.exp(a * dt)
    b = lax.complex(b_re.astype(f32), b_im.astype(f32))
    b_bar = ((a_bar - 1.0) / a)[..., None] * b
    drive = jnp.einsum("gpc,btgc->btgp", b_bar, uf.astype(jnp.complex64))
    h0 = lax.complex(h0_re.astype(f32), h0_im.astype(f32))
    drive = drive.at[:, 0].add(a_bar[None] * h0)
    decay = jnp.broadcast_to(a_bar, drive.shape)
    _, h = lax.associative_scan(_lin_combine, (decay, drive), axis=1)
    cm = lax.complex(c_re.astype(f32), c_im.astype(f32))
    y = jnp.einsum("gcp,btgp->btgc", cm, h).real + d.astype(f32).reshape(SSM_GROUPS, SSM_CH) * uf
    y = jax.nn.gelu(y.reshape(bsz, t, W_C)).astype(u.dtype)
    y = y * jax.nn.sigmoid(y @ w_glu)
    h_last = h[:, -1]
    return y, h_last.real.astype(h0_re.dtype), h_last.imag.astype(h0_re.dtype)


def _layer(x, c, pos, win_k, win_v, pool_prev, h0_re, h0_im, lp):
    bsz, t, _ = x.shape
    if pool_prev is None:
        pool_prev = jnp.zeros((bsz, POOL_BUF, W_A), x.dtype)
        h0_re = jnp.zeros((bsz, SSM_GROUPS, SSM_STATE), x.dtype)
        h0_im = jnp.zeros((bsz, SSM_GROUPS, SSM_STATE), x.dtype)
    mod = jax.nn.silu(c) @ lp["w_ada"] + lp["b_ada"]
    sh1, sc1, g1, sh2, sc2, g2 = [m[:, None, :] for m in jnp.split(mod, 6, axis=-1)]
    h = _rmsnorm(x, lp["norm1_g"]) * (1.0 + sc1) + sh1
    z = h @ lp["w_in"]
    xa, q, k, v, u, gates = jnp.split(z, list(IN_SPLITS), axis=-1)
    q = _rope(q.reshape(bsz, t, N_HEADS, HEAD_DIM), pos)
    k = _rope(k.reshape(bsz, t, KV_HEADS, HEAD_DIM), pos)
    v = v.reshape(bsz, t, KV_HEADS, HEAD_DIM)
    ya, pool_new = _pool_mix(xa, pool_prev, pos, lp["pool_w"], lp["pool_scale"])
    if win_k is None:
        yb, k_new, v_new = _swa_prompt(q, k, v, lp["attn_sinks"])
    else:
        yb, k_new, v_new = _swa_sample(q, k, v, win_k, win_v, pos, lp["attn_sinks"])
    yc, hre, him = _s5(u, h0_re, h0_im, lp["ssm_a_re"], lp["ssm_a_im"], lp["ssm_log_dt"], lp["ssm_b_re"],
                       lp["ssm_b_im"], lp["ssm_c_re"], lp["ssm_c_im"], lp["ssm_d"], lp["w_glu"])
    ga, gb, gc = jnp.split(jax.nn.sigmoid(gates), N_BRANCH, axis=-1)
    merged = ga * (ya @ lp["w_branch_a"]) + gb * (yb @ lp["w_branch_b"]) + gc * (yc @ lp["w_branch_c"])
    x = x + g1 * (merged @ lp["w_out"])
    h2 = _rmsnorm(x, lp["norm2_g"]) * (1.0 + sc2) + sh2
    a_up, b_up = jnp.split(h2 @ lp["w_ffn_in"], 2, axis=-1)
    x = x + g2 * ((jax.nn.silu(a_up) * b_up) @ lp["w_ffn_out"])
    return x, k_new, v_new, pool_new, hre, him


def setup_inputs(seed: int = 0) -> dict:
    key = jax.random.key(seed)
    ks = iter(jax.random.split(key, 48))
    f32 = jnp.float32

    def nrm(shape, s=1.0):
        return s * jax.random.normal(next(ks), shape, f32)

    L, D = DEPTH, D_MODEL
    win = min(WINDOW, PAST_LEN)
    n_idx = jnp.arange(SSM_STATE, dtype=f32)
    return {
        "x_prompt": nrm((BATCH, SEQ, D)),
        "x_sample": nrm((DEC_BATCH, DEC_SEQ, D)),
        "cache_win_k": nrm((L, DEC_BATCH, win, KV_HEADS, HEAD_DIM)),
        "cache_win_v": nrm((L, DEC_BATCH, win, KV_HEADS, HEAD_DIM)),
        "state_pool": nrm((L, DEC_BATCH, POOL_BUF, W_A)),
        "state_ssm_re": nrm((L, DEC_BATCH, SSM_GROUPS, SSM_STATE), 0.5),
        "state_ssm_im": nrm((L, DEC_BATCH, SSM_GROUPS, SSM_STATE), 0.5),
        "c_prompt": nrm((BATCH, D)),
        "c_sample": nrm((DEC_BATCH, D)),
        "norm1_g": 1.0 + nrm((L, D), 0.05),
        "norm2_g": 1.0 + nrm((L, D), 0.05),
        "w_ada": nrm((L, D, 6 * D), 0.5 * D ** -0.5),
        "b_ada": nrm((L, 6 * D), 0.02),
        "w_in": nrm((L, D, IN_COLS), D ** -0.5),
        "pool_w": nrm((L, len(POOL_WINDOWS), POOL_GROUP, POOL_GROUP), POOL_GROUP ** -0.5),
        "pool_scale": 1.0 + nrm((L, W_A), 0.1),
        "attn_sinks": nrm((L, N_HEADS), 0.5),
        "ssm_a_re": -0.5 * jnp.exp(nrm((L, SSM_GROUPS, SSM_STATE), 0.05)),
        "ssm_a_im": math.pi * n_idx + nrm((L, SSM_GROUPS, SSM_STATE), 0.05),
        "ssm_log_dt": jax.random.uniform(next(ks), (L, SSM_GROUPS), f32, math.log(1e-3), math.log(1e-1)),
        "ssm_b_re": nrm((L, SSM_GROUPS, SSM_STATE, SSM_CH), (2 * SSM_CH) ** -0.5),
        "ssm_b_im": nrm((L, SSM_GROUPS, SSM_STATE, SSM_CH), (2 * SSM_CH) ** -0.5),
        "ssm_c_re": nrm((L, SSM_GROUPS, SSM_CH, SSM_STATE), SSM_STATE ** -0.5),
        "ssm_c_im": nrm((L, SSM_GROUPS, SSM_CH, SSM_STATE), SSM_STATE ** -0.5),
        "ssm_d": nrm((L, W_C)),
        "w_glu": nrm((L, W_C, W_C), W_C ** -0.5),
        "w_branch_a": nrm((L, W_A, D), W_A ** -0.5),
        "w_branch_b": nrm((L, W_B, D), W_B ** -0.5),
        "w_branch_c": nrm((L, W_C, D), W_C ** -0.5),
        "w_out": nrm((L, D, D), D ** -0.5),
        "w_ffn_in": nrm((L, D, 2 * D_FF), D ** -0.5),
        "w_ffn_out": nrm((L, D_FF, D), D_FF ** -0.5),
        "final_norm_g": 1.0 + nrm((D,), 0.05),
    }


def reference(x_prompt, x_sample, cache_win_k, cache_win_v, state_pool, state_ssm_re, state_ssm_im,
              c_prompt, c_sample, norm1_g, norm2_g, w_ada, b_ada, w_in, pool_w, pool_scale, attn_sinks,
              ssm_a_re, ssm_a_im, ssm_log_dt, ssm_b_re, ssm_b_im, ssm_c_re, ssm_c_im, ssm_d, w_glu,
              w_branch_a, w_branch_b, w_branch_c, w_out, w_ffn_in, w_ffn_out, final_norm_g):
    pos_p = jnp.arange(x_prompt.shape[1], dtype=jnp.int32)
    pos_s = PAST_LEN + jnp.arange(x_sample.shape[1], dtype=jnp.int32)
    hp, hs = x_prompt, x_sample
    pk, pv, ppool, pre, pim = [], [], [], [], []
    sk, sv, spool, sre, sim = [], [], [], [], []
    for l in range(DEPTH):
        lp = dict(norm1_g=norm1_g[l], norm2_g=norm2_g[l], w_ada=w_ada[l], b_ada=b_ada[l], w_in=w_in[l],
                  pool_w=pool_w[l], pool_scale=pool_scale[l], attn_sinks=attn_sinks[l],
                  ssm_a_re=ssm_a_re[l], ssm_a_im=ssm_a_im[l], ssm_log_dt=ssm_log_dt[l],
                  ssm_b_re=ssm_b_re[l], ssm_b_im=ssm_b_im[l], ssm_c_re=ssm_c_re[l], ssm_c_im=ssm_c_im[l],
                  ssm_d=ssm_d[l], w_glu=w_glu[l], w_branch_a=w_branch_a[l], w_branch_b=w_branch_b[l],
                  w_branch_c=w_branch_c[l], w_out=w_out[l], w_ffn_in=w_ffn_in[l], w_ffn_out=w_ffn_out[l])
        hp, k1, v1, po1, r1, i1 = _layer(hp, c_prompt, pos_p, None, None, None, None, None, lp)
        pk.append(k1); pv.append(v1); ppool.append(po1); pre.append(r1); pim.append(i1)
        hs, k2, v2, po2, r2, i2 = _layer(hs, c_sample, pos_s, cache_win_k[l], cache_win_v[l], state_pool[l],
                                         state_ssm_re[l], state_ssm_im[l], lp)
        sk.append(k2); sv.append(v2); spool.append(po2); sre.append(r2); sim.append(i2)
    y_prompt = _rmsnorm(hp, final_norm_g)
    y_sample = _rmsnorm(hs, final_norm_g)
    return (y_prompt, y_sample,
            jnp.stack(pk), jnp.stack(pv), jnp.stack(ppool), jnp.stack(pre), jnp.stack(pim),
            jnp.stack(sk), jnp.stack(sv), jnp.stack(spool), jnp.stack(sre), jnp.stack(sim))
```

```python
import math
import numpy as np
import concourse.bass as bass
import concourse.mybir as mybir
from concourse.bass_utils import run_bass_kernel_spmd

F32 = mybir.dt.float32
BF16 = mybir.dt.bfloat16
I32 = mybir.dt.int32
AF = mybir.ActivationFunctionType
ALU = mybir.AluOpType
AX = mybir.AxisListType

NL = 4
D = 1024
KC = 8
T = 2048
TP = 512
NTILE = 4
NSEQ = 16
NS = 64
TOT = T + NS
WMAX = TP + NS
PAST = 8192
EPS = 1e-6
INC = 1792 + 3072
DFF = 2816
HC = 22
NSEM = 26
BLK = 64
TWO_PI_S = 6.28318


class Prog:
    QS = ('pe', 'act', 'dve', 'pool', 'sp')

    def __init__(self, nc):
        self.nc = nc
        self.dry = False
        self.ins = []
        self.qins = {q: [] for q in self.QS}
        self.last_w = {}
        self.rd_c = {}
        self.rd_d = {}
        self.dma_cnt = {}
        self.dma_last = {}
        self.sem_pool = {'pool': list(range(0, 14)), 'sp': list(range(14, NSEM))}
        self.out_dmas = []
        self.base = {}
        self.n_t = 0
        self.retired = {}
        self.dependents = {}
        self.dsem = None

    def sbt(self, name, shape, dt=F32):
        t = self.nc.alloc_sbuf_tensor(name, list(shape), dt)
        return t

    def sbt_at(self, name, shape, dt, off):
        self.n_t += 1
        return self.nc.alloc_sbuf_tensor_at("%s_%d" % (name, self.n_t), list(shape), dt, offset=off)

    def pst(self, name, shape, dt=F32):
        return self.nc.alloc_psum_tensor(name, list(shape), dt)

    def _ranges(self, ap):
        sp = str(ap.space)
        if 'DRAM' in sp:
            return None
        name = ap.tensor.name
        key = self.base.get(name)
        if key is None:
            m = self.nc.lookup_mloc(ap.tensor)
            if 'PSUM' in sp:
                key = ('P', m.bank * 2048 + m.addr)
            else:
                key = ('S', m.addr)
            self.base[name] = key
        spc, b0 = key
        es = mybir.dt.size(ap.dtype)
        dims = list(ap.ap)
        pstride = dims[0][0]
        off = ap.offset
        foff = off % pstride if pstride > 0 else off
        free = [(abs(s), n) for (s, n) in dims[1:] if n > 1 and s != 0]
        free.sort(reverse=True)
        out = []

        def rec(base, ds):
            if not ds:
                out.append((base, base + 1))
                return
            ext = sum(s * (n - 1) for s, n in ds) + 1
            if len(ds) == 1 or ext * es <= 2 * BLK or ds[0][1] > 64:
                out.append((base, base + ext))
                return
            s, n = ds[0]
            inner = sum(s2 * (n2 - 1) for s2, n2 in ds[1:]) + 1
            if inner >= s:
                out.append((base, base + ext))
                return
            for i in range(n):
                rec(base + i * s, ds[1:])
        rec(foff, free)
        blocks = set()
        blk = 2048 if spc == 'P' else BLK
        for lo, hi in out:
            a = (b0 + lo * es) // blk
            b = (b0 + hi * es - 1) // blk
            for k in range(a, b + 1):
                blocks.add((spc, k))
        return blocks

    def _add(self, q, fn, deps, dma=False):
        iid = len(self.ins)
        rec = dict(id=iid, q=q, fn=fn, dma=dma, signal=False, qidx=len(self.qins[q]))
        nd = []
        for d in set(deps):
            d = self.retired.get(d, d)
            dr = self.ins[d]
            if (not dma) and (not dr['dma']) and dr['q'] == q:
                if q == 'pe':
                    continue
            nd.append(d)
            if dr['dma']:
                self.dependents.setdefault(d, []).append(iid)
        rec['deps'] = nd
        self.ins.append(rec)
        self.qins[q].append(rec)
        return iid

    def op(self, q, fn, outs=(), ins=(), dma=False, out=False):
        if self.dry:
            return None
        rb = set()
        wb = set()
        for a in ins:
            if a is None or isinstance(a, (int, float)):
                continue
            r = self._ranges(a)
            if r:
                rb |= r
        for a in outs:
            r = self._ranges(a)
            if r:
                wb |= r
        pr = set(k for k in rb if k[0] == 'P')
        if pr:
            rb -= pr
            wb |= pr
        deps = set()
        for k in rb:
            w = self.last_w.get(k)
            if w is not None:
                deps.add(w)
        for k in wb:
            w = self.last_w.get(k)
            if w is not None:
                deps.add(w)
            rc = self.rd_c.get(k)
            if rc:
                deps.update(rc.values())
            rd = self.rd_d.get(k)
            if rd:
                deps.update(rd)
        sem = None
        if dma:
            pool_ = self.sem_pool[q]
            cnt = self.dma_cnt.get(q, 0)
            self.dma_cnt[q] = cnt + 1
            sem = pool_[cnt % len(pool_)]
            prev = self.dma_last.get(sem)
            if prev is not None:
                deps.add(prev)
        iid = self._add(q, fn, deps, dma=dma)
        rec = self.ins[iid]
        if dma:
            rec['sem'] = sem
            rec['semval'] = 16 * (cnt // len(pool_) + 1)
            self.dma_last[sem] = iid
            if out:
                self.out_dmas.append(iid)
        for k in rb:
            if dma:
                self.rd_d.setdefault(k, []).append(iid)
            else:
                self.rd_c.setdefault(k, {})[q] = iid
        for k in wb:
            self.last_w[k] = iid
            self.rd_c[k] = {}
            self.rd_d[k] = []
        return iid

    def finish(self, es):
        fdeps = [self.retired.get(d, d) for d in self.out_dmas]
        fin = dict(id=len(self.ins), q='sp', fn=None, dma=False, signal=False,
                   qidx=len(self.qins['sp']), deps=list(set(fdeps)))
        self.ins.append(fin)
        self.qins['sp'].append(fin)
        for r in self.ins:
            for d in r['deps']:
                self.ins[d]['signal'] = True
        for q in self.QS:
            c = 0
            for r in self.qins[q]:
                if (not r['dma']) and r['signal']:
                    c += 1
                r['cnt'] = c
        nc = self.nc
        csem = {q: es.enter_context(nc.semaphore('cs_' + q)) for q in self.QS}
        dsem = [es.enter_context(nc.semaphore('ds%d' % i)) for i in range(NSEM)]
        self.dsem = dsem
        block = es.enter_context(nc.Block())

        def emit(eng, q):
            waited = {}
            for r in self.qins[q]:
                need = {}
                for d in r['deps']:
                    dr = self.ins[d]
                    if dr['dma']:
                        key = ('d', dr['sem'])
                        val = dr['semval']
                        sem = dsem[dr['sem']]
                    else:
                        key = ('c', dr['q'])
                        val = dr['cnt']
                        sem = csem[dr['q']]
                    if need.get(key, (0, None))[0] < val:
                        need[key] = (val, sem)
                for key, (val, sem) in need.items():
                    if waited.get(key, 0) >= val:
                        continue
                    eng.wait_ge(sem, val)
                    waited[key] = val
                if r['fn'] is None:
                    continue
                bi = r['fn'](eng)
                if r['dma']:
                    bi.then_inc(dsem[r['sem']], 16)
                elif r['signal']:
                    bi.then_inc(csem[q], 1)
        block.tensor(lambda e: emit(e, 'pe'))
        block.scalar(lambda e: emit(e, 'act'))
        block.vector(lambda e: emit(e, 'dve'))
        block.gpsimd(lambda e: emit(e, 'pool'))
        block.sync(lambda e: emit(e, 'sp'))

    def dma(self, q, out, in_, is_out=False, **kw):
        return self.op(q, lambda e: e.dma_start(out=out, in_=in_, **kw), [out], [in_], dma=True, out=is_out)

    def mm(self, out, lhsT, rhs, start, stop):
        return self.op('pe', lambda e: e.matmul(out, lhsT, rhs, start=start, stop=stop), [out], [lhsT, rhs])

    def tr(self, out, in_, ident):
        return self.op('pe', lambda e: e.transpose(out, in_, ident), [out], [in_, ident])

    def act(self, out, in_, func, scale=1.0, bias=0.0, accum_out=None, q='act'):
        ins = [in_]
        outs = [out]
        if not isinstance(scale, (int, float)):
            ins.append(scale)
        if not isinstance(bias, (int, float)):
            ins.append(bias)
        kw = {}
        if accum_out is not None:
            outs.append(accum_out)
            kw['accum_out'] = accum_out
        return self.op(q, lambda e: e.activation(out, in_, func, bias=bias, scale=scale, **kw), outs, ins)

    def tt(self, out, a, b, op, q='dve'):
        return self.op(q, lambda e: e.tensor_tensor(out, a, b, op), [out], [a, b])

    def ts(self, out, a, s1, s2, op0, op1=None, q='dve'):
        ins = [a]
        if not isinstance(s1, (int, float)):
            ins.append(s1)
        if s2 is not None and not isinstance(s2, (int, float)):
            ins.append(s2)
        if op1 is None:
            return self.op(q, lambda e: e.tensor_scalar(out, a, s1, None, op0), [out], ins)
        return self.op(q, lambda e: e.tensor_scalar(out, a, s1, s2, op0, op1), [out], ins)

    def stt(self, out, in0, scalar, in1, op0, op1, q='dve'):
        ins = [in0, in1]
        if not isinstance(scalar, (int, float)):
            ins.append(scalar)
        return self.op(q, lambda e: e.scalar_tensor_tensor(out, in0, scalar, in1, op0, op1), [out], ins)

    def copy(self, out, in_, q='dve'):
        if q == 'act':
            return self.op(q, lambda e: e.copy(out, in_), [out], [in_])
        return self.op(q, lambda e: e.tensor_copy(out, in_), [out], [in_])

    def memset(self, ap, v, q='pool'):
        return self.op(q, lambda e: e.memset(ap, v), [ap], [])

    def recip(self, out, in_):
        return self.op('dve', lambda e: e.reciprocal(out, in_), [out], [in_])

    def scan(self, out, d0, d1, init):
        ins = [d0, d1]
        if not isinstance(init, (int, float)):
            ins.append(init)
        return self.op('dve', lambda e: e.tensor_tensor_scan(out, d0, d1, init, ALU.mult, ALU.add), [out], ins)

    def rmax(self, out, in_):
        return self.op('dve', lambda e: e.tensor_reduce(out, in_, AX.X, ALU.max), [out], [in_])


class StopBuild(Exception):
    pass


class WStream:
    NB = 3
    LOOK = 2

    def __init__(self, P, bufs):
        self.P = P
        self.bufs = bufs
        self.plan = []
        self.i = 0
        self.issued = 0

    def get(self, issue):
        P = self.P
        if P.dry:
            self.plan.append(issue)
            return self.bufs[(len(self.plan) - 1) % self.NB]
        i = self.i
        while self.issued <= min(i + self.LOOK, len(self.plan) - 1):
            j = self.issued
            self.plan[j](self.bufs[j % self.NB])
            self.issued += 1
        self.i += 1
        return self.bufs[i % self.NB]


def bc(ap, shape):
    return ap.to_broadcast(list(shape))


def build(nc, cfg):
    P = Prog(nc)
    dbg = cfg.get('dbg', False)
    nlayers = cfg.get('nlayers', NL)

    def DI(name, shape, dt=F32):
        return nc.dram_tensor(name, list(shape), dt, kind="ExternalInput").ap()

    def DO(name, shape, dt=F32):
        return nc.dram_tensor(name, list(shape), dt, kind="ExternalOutput").ap()

    xp = DI('xp', [T, D]); xs = DI('xs', [NS, D])
    ck = DI('ck', [NL, NSEQ, 128, 128]); cv = DI('cv', [NL, NSEQ, 128, 128])
    spool = DI('spool', [NL, NSEQ, 15, 512])
    sre = DI('sre', [NL, NSEQ, 2048]); sim = DI('sim', [NL, NSEQ, 2048])
    c17 = DI('c17', [17, D])
    n1g = DI('norm1_g', [NL, D]); n2g = DI('norm2_g', [NL, D])
    w_ada = DI('w_ada', [NL, D, 6 * D]); b_ada = DI('b_ada', [NL, 6 * D])
    w_in = DI('w_in', [NL, D, INC])
    pool_w = DI('pool_w', [NL, 4, 128, 128]); pool_scale = DI('pool_scale', [NL, 512])
    sinks = DI('attn_sinks', [NL, 8])
    a_re = DI('ssm_a_re', [NL, 32, 64]); a_im = DI('ssm_a_im', [NL, 32, 64]); log_dt = DI('ssm_log_dt', [NL, 32])
    b_re = DI('ssm_b_re', [NL, 32, 64, 16]); b_im = DI('ssm_b_im', [NL, 32, 64, 16])
    c_re = DI('ssm_c_re', [NL, 32, 16, 64]); c_im = DI('ssm_c_im', [NL, 32, 16, 64])
    ssm_d = DI('ssm_d', [NL, 512])
    w_glu = DI('w_glu', [NL, 512, 512])
    wbr = [DI('w_branch_a', [NL, 512, D]), DI('w_branch_b', [NL, 512, D]), DI('w_branch_c', [NL, 512, D])]
    w_out = DI('w_out', [NL, D, D])
    w_ffn_in = DI('w_ffn_in', [NL, D, 2 * DFF]); w_ffn_out = DI('w_ffn_out', [NL, DFF, D])
    fng = DI('final_norm_g', [D])
    k_ident = DI('k_ident', [128, 128]); k_rotm = DI('k_rotm', [128, 128])
    k_mask = DI('k_mask', [2, 128, 256]); k_rope = DI('k_rope', [2, 128, TOT])
    k_prep = DI('k_prep', [32, 800]); k_sel = DI('k_sel', [128, 8]); k_invc = DI('k_invc', [128, 64]); k_thl = DI('k_thl', [1, 1024])

    y_p = DO('y_p', [T, D]); y_s = DO('y_s', [NS, D])
    nk_p = DO('nk_p', [NL, 128, 128]); nv_p = DO('nv_p', [NL, 128, 128])
    npool_p = DO('npool_p', [NL, 15, 512])
    nre_p = DO('nre_p', [NL, 16, 128]); nim_p = DO('nim_p', [NL, 16, 128])
    nk_s = DO('nk_s', [NL, NSEQ, 128, 128]); nv_s = DO('nv_s', [NL, NSEQ, 128, 128])
    npool_s = DO('npool_s', [NL, NSEQ, 15, 512])
    nre_s = DO('nre_s', [NL, NSEQ, 2048]); nim_s = DO('nim_s', [NL, NSEQ, 2048])
    dbg_outs = {}

    def dump(name, ap, shape, dt=F32):
        if not dbg or P.dry:
            return
        o = DO('dbg_' + name, shape, dt)
        P.dma('sp', o, ap, is_out=True)

    x = P.sbt('x', [128, KC, TOT])
    wbufs = [P.sbt('wbuf%d' % i, [128, 4096], BF16) for i in range(3)]
    modT = P.sbt('modT', [128, 48, 17]); A1 = P.sbt('A1', [128, 8, 17]); A2 = P.sbt('A2', [128, 8, 17])
    scT = P.sbt('scT', [128, 8, 17], BF16)
    vecs = P.sbt('vecs', [128, 72]); v_fng = P.sbt('v_fng', [128, 8])
    v_n1g = vecs[:, 0:8]; v_n2g = vecs[:, 8:16]; v_bada = vecs[:, 16:64]; v_psc = vecs[:, 64:68]; v_ssmd = vecs[:, 68:72]
    BT_re = P.sbt('BT_re', [128, 4, 2, 128], BF16); BT_im = P.sbt('BT_im', [128, 4, 2, 128], BF16)
    CT_re = P.sbt('CT_re', [128, 16, 128], BF16); CT_imn = P.sbt('CT_imn', [128, 16, 128], BF16)
    c_rho = P.sbt('c_rho', [128, 16]); c_ft = P.sbt('c_ft', [128, 16]); c_g64 = P.sbt('c_g64', [128, 16])
    c_abr = P.sbt('c_abr', [128, 16]); c_abi = P.sbt('c_abi', [128, 16]); c_ct = P.sbt('c_ct', [128, 16])
    pw = P.sbt('pw', [128, 4, 128], BF16)
    kcar = P.sbt('kcar', [128, 128], BF16); vcar = P.sbt('vcar', [128, 128], BF16)
    pcar = P.sbt('pcar', [128, 4, 15]); gcar = P.sbt('gcar', [128, 16, 2])
    ident = P.sbt('ident', [128, 128]); identb = P.sbt('identb', [128, 128], BF16)
    rotm = P.sbt('rotm', [128, 128]); onesb = P.sbt('onesb', [128, 128], BF16)
    maskA = P.sbt('maskA', [128, 256], BF16); maskB = P.sbt('maskB', [128, 256], BF16)
    sel = P.sbt('sel', [128, 8]); invc = P.sbt('invc', [128, 4, 16]); sinkb = P.sbt('sinkb', [128, 8])
    thl = P.sbt('thl', [128, 2, 512], BF16)
    h = P.sbt('h', [128, KC, WMAX], BF16)
    merged = P.sbt('merged', [128, KC, WMAX])
    otok = P.sbt('otok', [128, 512])
    halfpi = P.sbt('halfpi', [128, 1]); ctmp = P.sbt('ctmp', [128, 16]); ctmpi = P.sbt('ctmpi', [128, 16], I32)
    arena_sz = (nc.sbuf_bytes_remaining - 64) // 64 * 64
    arena = P.sbt('arena', [128, arena_sz // 4])
    abase = nc.lookup_mloc(arena).addr
    cfg['arena'] = arena_sz

    class Ar:
        def __init__(self):
            self.off = 0

        def a(self, name, shape, dt=F32):
            n = 1
            for s_ in shape[1:]:
                n *= s_
            nb = (n * mybir.dt.size(dt) + 63) // 64 * 64
            assert self.off + nb <= arena_sz, (name, self.off, nb, arena_sz)
            t = P.sbt_at(name, shape, dt, abase + self.off)
            self.off += nb
            return t

    PS = [P.pst('ps%d' % i, [128, 1024]) for i in range(4)]
    psrr = [0]

    def nextps():
        psrr[0] = (psrr[0] + 1) % 3
        return PS[psrr[0]]
    psX = PS[3]

    W_ = WStream(P, wbufs)

    def wv(buf, kc, n):
        return buf[:, 0:kc * n].rearrange("p (k m) -> p k m", k=kc)

    def wload_std(src, kc, n):
        def issue(buf):
            P.dma('pool', wv(buf, kc, n), src.rearrange("(k p) m -> p k m", p=128))
        return issue

    ar = Ar()
    sq = ar.a('sq', [128, 2, WMAX], BF16); rb = ar.a('rb', [128, WMAX]); ntmp = ar.a('ntmp', [128, 2, WMAX])
    norm_end = ar.off
    ar.off = 0
    sig = ar.a('sig', [128, 2, WMAX]); mtmp = ar.a('mtmp', [128, 2, WMAX])
    br_base = max(ar.off, norm_end)
    ar.off = br_base
    xa = ar.a('xa', [128, 4, 15 + TP]); xes = ar.a('xes', [128, 4, NSEQ, 19])
    scr = ar.a('scr', [128, 2, 15 + TP]); scrs = ar.a('scrs', [128, 2, NSEQ, 19])
    dbuf = ar.a('dbuf', [128, 4, WMAX], BF16); ya = ar.a('ya', [128, 4, WMAX], BF16)
    sptok = ar.a('sptok', [128, 2, 512]); xsn = ar.a('xsn', [128, 4, NS])
    pool_end = ar.off
    ar.off = br_base
    qT = ar.a('qT', [128, 4, WMAX], BF16); kT = ar.a('kT', [128, 128 + WMAX], BF16)
    vtok = ar.a('vtok', [128, 6, 128], BF16)
    yb = ar.a('yb', [128, 4, WMAX], BF16)
    cv_tok = ar.a('cv_tok', [128, NSEQ, 128], BF16); kcT = ar.a('kcT', [128, NSEQ, 128], BF16)
    vnew = ar.a('vnew', [64, 128], BF16); vnew32 = ar.a('vnew32', [64, 128])
    smal = ar.a('smal', [128, 2, 8, 4])
    pz = ar.a('pz', [4, 2, 4, 64], BF16)
    al0 = ar.off
    pbuf = ar.a('pbuf', [128, 2, 4, 256]); pn = ar.a('pn', [128, 2, 4, 256], BF16)
    pT = ar.a('pT', [128, 2, 8, 128], BF16)
    attn_end = ar.off
    ar.off = al0
    q32 = ar.a('q32', [128, WMAX]); rt1 = ar.a('rt1', [128, WMAX]); rt2 = ar.a('rt2', [128, WMAX])
    cosT = ar.a('cosT', [128, WMAX]); sinT = ar.a('sinT', [128, WMAX]); kr32 = ar.a('kr32', [128, WMAX])
    ck_tok = ar.a('ck_tok', [128, 8, 128], BF16)
    attn_end = max(attn_end, ar.off)
    ar.off = br_base
    u32 = ar.a('u32', [128, 4, WMAX]); ub = ar.a('ub', [128, 4, WMAX], BF16)
    yg = ar.a('yg', [128, 4, WMAX], BF16)
    mb = P.sbt_at('mb', [128, KC, WMAX], BF16, abase + br_base)
    sl0 = ar.off
    pl = [ar.a('pl%d' % i, [128, TP]) for i in range(8)]
    pli = ar.a('pli', [128, 2, TP], I32)
    tb_s = ar.a('tb_s', [128, 2, TP]); tb_c = ar.a('tb_c', [128, 2, TP])
    hre = ar.a('hre', [128, 2, TP], BF16); him = ar.a('him', [128, 2, TP], BF16)
    sl1 = ar.off
    hl = ar.a('hl', [128, 16, 2])
    hsb_re = ar.a('hsb_re', [128, 16, NS], BF16); hsb_im = ar.a('hsb_im', [128, 16, NS], BF16)
    ubs = ar.a('ubs', [128, 2, 4, NS], BF16)
    ssm_end = ar.off
    ar.off = sl0
    ysm = ar.a('ysm', [128, 3, WMAX]); yc = ar.a('yc', [128, 4, WMAX], BF16)
    assert ar.off <= sl1
    ar.off = sl0
    htok = ar.a('htok', [16, 512])
    dsr = ar.a('dsr', [128, 16, NSEQ, 4]); dsi = ar.a('dsi', [128, 16, NSEQ, 4])
    hq_re = ar.a('hq_re', [128, 16, NSEQ, 5]); hq_im = ar.a('hq_im', [128, 16, NSEQ, 5])
    st1 = ar.a('st1', [128, 16, NSEQ]); st2 = ar.a('st2', [128, 16, NSEQ])
    assert ar.off <= sl1, (ar.off, sl1)
    ar.off = norm_end
    hid = ar.a('hid', [128, HC, WMAX], BF16); sa = ar.a('sa', [128, 2, WMAX])
    ffn_end = ar.off
    ar.off = norm_end
    yf = ar.a('yf', [128, KC, WMAX]); iotok = ar.a('iotok', [128, D])
    fin_end = ar.off
    ar.off = 0
    pp = {}
    for nm in ['are', 'aim', 'ldt', 'dt', 'lre', 'lim', 'rho', 'ft', 'fr', 'afr', 'sn', 'cs', 'abr', 'abi',
               'nr', 'den', 'qre', 'qim', 't1', 't2', 'bre', 'bim']:
        pp[nm] = ar.a('pp_' + nm, [128, 256])
    ppi = ar.a('pp_i', [128, 256], I32)
    craw_re = ar.a('craw_re', [128, 16, 16]); craw_im = ar.a('craw_im', [128, 16, 16])
    vst = ar.a('vst', [72, 128]); kp = ar.a('kp', [32, 800]); araw = ar.a('araw', [32, 2, 64])
    a2 = ar.a('a2', [32, 2, 2, 128]); ldtc = ar.a('ldtc', [32, 1]); rs01 = ar.a('rs01', [32, 2, 16])
    lb = ar.a('lb', [32, 64]); btile = ar.a('btile', [64, 4, 128]); ctile = ar.a('ctile', [16, 2048])
    c17sb = ar.a('c17sb', [17, D])
    cfg['arena_used'] = dict(pool=pool_end, attn=attn_end, ssm=ssm_end, ffn=ffn_end, fin=fin_end, prep=ar.off)

    def body():
        P.dma('sp', ident[:], k_ident)
        P.dma('sp', rotm[:], k_rotm)
        P.dma('pool', identb[:], k_ident)
        P.dma('pool', maskA[:], k_mask[0])
        P.dma('pool', maskB[:], k_mask[1])
        P.memset(onesb[:], 1.0)
        P.dma('sp', sel[:], k_sel)
        P.dma('sp', invc[:].rearrange("p a b -> p (a b)"), k_invc)
        P.dma('pool', thl[:].rearrange("p a b -> p (a b)"), k_thl.partition_broadcast(128))
        P.dma('sp', vst[0:8, :], fng.rearrange("(k p) -> k p", p=128))
        P.mm(psX[:, 512:520], vst[0:8, :], ident[0:8, 0:8], True, True)
        P.copy(v_fng[:], psX[:, 512:520])
        P.dma('sp', c17sb[:], c17)
        for kc in range(KC):
            P.mm(psX[:, kc * 17:(kc + 1) * 17], c17sb[0:17, kc * 128:(kc + 1) * 128], ident[0:17, 0:17], True, True)
        P.act(scT[:].rearrange("p a b -> p (a b)"), psX[:, 0:136], AF.Silu)
        for blk in range(TOT // 128 + 1):
            r0 = blk * 128
            nr = 128 if blk < 16 else NS
            if blk < 16:
                P.dma('sp', iotok[0:nr, :], xp[r0:r0 + nr, :])
            else:
                P.dma('sp', iotok[0:nr, :], xs)
            ps = nextps()
            for kc in range(KC):
                P.mm(ps[:, kc * 128:kc * 128 + nr], iotok[0:nr, kc * 128:(kc + 1) * 128], ident[0:nr, 0:nr], True, True)
            P.copy(x[:, :, r0:r0 + nr], ps[:, :].rearrange("p (k m) -> p k m", k=KC)[:, :, 0:nr], q='act')

        cfg['_stopfn']('setup')
        for l in range(nlayers):
            layer(l)

    def cexp(shape_n, ldt_ap, are_ap, aim_ap):
        n = shape_n
        g = lambda nm: pp[nm][:, 0:n]
        P.act(g('dt'), ldt_ap, AF.Exp)
        P.tt(g('lre'), are_ap, g('dt'), ALU.mult)
        P.tt(g('lim'), aim_ap, g('dt'), ALU.mult)
        P.act(g('rho'), g('lre'), AF.Exp)
        P.ts(g('ft'), g('lim'), 1.0 / (2 * math.pi), None, ALU.mult)
        P.copy(ppi[:, 0:n], g('ft'))
        P.tt(g('fr'), g('ft'), ppi[:, 0:n], ALU.subtract)
        P.stt(g('afr'), g('fr'), -1.0, g('fr'), ALU.mult, ALU.max)
        P.act(g('sn'), g('fr'), AF.Sin, scale=TWO_PI_S)
        P.act(g('cs'), g('afr'), AF.Sin, scale=-TWO_PI_S, bias=halfpi[:, 0:1])
        P.tt(g('abr'), g('rho'), g('cs'), ALU.mult)
        P.tt(g('abi'), g('rho'), g('sn'), ALU.mult)

    def layer_prep(l):
        r0 = 0
        for (src_, n_) in ((n1g[l], 8), (n2g[l], 8), (b_ada[l], 48), (pool_scale[l], 4), (ssm_d[l], 4)):
            P.dma('sp', vst[r0:r0 + n_, :], src_.rearrange("(k p) -> k p", p=128))
            r0 += n_
        P.mm(psX[:, 0:72], vst[0:72, :], ident[0:72, 0:72], True, True)
        P.copy(vecs[:], psX[:, 0:72])
        P.dma('sp', sinkb[:], sinks[l:l + 1, :].partition_broadcast(128))
        P.dma('pool', pw[:], pool_w[l].rearrange("g i o -> i g o"))
        P.memset(halfpi[:], math.pi / 2)
        for bk in range(12):
            wt = wv(W_.get(wload_std(w_ada[l][:, bk * 512:(bk + 1) * 512], 8, 512)), 8, 512)
            for o4 in range(4):
                for kc in range(KC):
                    P.mm(psX[:, o4 * 17:(o4 + 1) * 17], wt[:, kc, o4 * 128:(o4 + 1) * 128], scT[:, kc, :], kc == 0, kc == KC - 1)
            P.tt(modT[:, bk * 4:(bk + 1) * 4, :], psX[:, 0:68].rearrange("p (a b) -> p a b", a=4),
                 bc(v_bada[:, bk * 4:(bk + 1) * 4].unsqueeze(2), [128, 4, 17]), ALU.add)
        P.ts(A1[:], modT[:, 8:16, :], 1.0, None, ALU.add)
        P.tt(A1[:], A1[:], bc(v_n1g.unsqueeze(2), [128, 8, 17]), ALU.mult)
        P.ts(A2[:], modT[:, 32:40, :], 1.0, None, ALU.add)
        P.tt(A2[:], A2[:], bc(v_n2g.unsqueeze(2), [128, 8, 17]), ALU.mult)
        P.dma('sp', kp[:], k_prep)
        P.dma('sp', araw[:, 0, :], a_re[l]); P.dma('sp', araw[:, 1, :], a_im[l])
        P.dma('sp', ldtc[:], log_dt[l].rearrange("(g o) -> g o", o=1))
        S0 = kp[:, 0:16]; S1 = kp[:, 16:32]; E_lo = kp[:, 32:160]; E_hi = kp[:, 160:288]
        Esel = lambda j: kp[:, 288 + j * 128:288 + (j + 1) * 128]
        P.memset(a2[:], 0.0, q='dve')
        for r_ in range(2):
            P.copy(a2[:, r_, 0, 0:64], araw[:, r_, :])
            P.copy(a2[:, r_, 1, 64:128], araw[:, r_, :])
        P.ts(rs01[:, 0, :], S0, ldtc[:, 0:1], None, ALU.mult)
        P.ts(rs01[:, 1, :], S1, ldtc[:, 0:1], None, ALU.mult)
        psc_ = nextps()
        for r_ in range(2):
            P.mm(psc_[:, r_ * 16:(r_ + 1) * 16], a2[:, r_, 0, :], S0, True, False)
            P.mm(psc_[:, r_ * 16:(r_ + 1) * 16], a2[:, r_, 1, :], S1, False, True)
        P.mm(psc_[:, 32:48], E_lo, rs01[:, 0, :], True, False)
        P.mm(psc_[:, 32:48], E_hi, rs01[:, 1, :], False, True)
        P.copy(pp['are'][:, 0:16], psc_[:, 0:16]); P.copy(pp['aim'][:, 0:16], psc_[:, 16:32])
        P.copy(pp['ldt'][:, 0:16], psc_[:, 32:48])
        cexp(16, pp['ldt'][:, 0:16], pp['are'][:, 0:16], pp['aim'][:, 0:16])
        P.copy(c_rho[:], pp['rho'][:, 0:16]); P.copy(c_ft[:], pp['ft'][:, 0:16])
        P.copy(c_abr[:], pp['abr'][:, 0:16]); P.copy(c_abi[:], pp['abi'][:, 0:16])
        P.ts(pp['t1'][:, 0:16], pp['ft'][:, 0:16], 64.0, None, ALU.mult)
        P.copy(ppi[:, 0:16], pp['t1'][:, 0:16])
        P.tt(c_g64[:], pp['t1'][:, 0:16], ppi[:, 0:16], ALU.subtract)
        P.memset(lb[:], 1.0, q='dve')
        P.ts(lb[:], lb[:], ldtc[:, 0:1], None, ALU.mult)
        for (rhs_, nm) in ((araw[:, 0, :], 'are'), (araw[:, 1, :], 'aim'), (lb[:], 'ldt')):
            ps_ = nextps()
            for j in range(4):
                P.mm(ps_[:, j * 64:(j + 1) * 64], Esel(j), rhs_, True, True)
            P.copy(pp[nm][:, :], ps_[:, 0:256])
        for (arr, nm) in ((b_re, 'bre'), (b_im, 'bim')):
            ps_ = nextps()
            for j in range(4):
                P.dma('sp', btile[:, j, :].rearrange("p (m c) -> p m c", c=16), arr[l, 8 * j:8 * j + 8].rearrange("m p c -> p m c"))
                P.mm(ps_[:, j * 64:(j + 1) * 64], btile[:, j, :], ident[0:64, 0:64], True, True)
            P.copy(pp[nm][:, :], ps_[:, 0:256])
        cexp(256, pp['ldt'][:, :], pp['are'][:, :], pp['aim'][:, :])
        g = lambda nm: pp[nm][:, :]
        P.ts(g('nr'), g('abr'), -1.0, None, ALU.add)
        P.tt(g('t1'), g('are'), g('are'), ALU.mult)
        P.tt(g('t2'), g('aim'), g('aim'), ALU.mult)
        P.tt(g('den'), g('t1'), g('t2'), ALU.add)
        P.recip(g('den'), g('den'))
        P.tt(g('t1'), g('nr'), g('are'), ALU.mult)
        P.tt(g('t2'), g('abi'), g('aim'), ALU.mult)
        P.tt(g('qre'), g('t1'), g('t2'), ALU.add)
        P.tt(g('qre'), g('qre'), g('den'), ALU.mult)
        P.tt(g('t1'), g('abi'), g('are'), ALU.mult)
        P.tt(g('t2'), g('nr'), g('aim'), ALU.mult)
        P.tt(g('qim'), g('t1'), g('t2'), ALU.subtract)
        P.tt(g('qim'), g('qim'), g('den'), ALU.mult)
        P.tt(g('t1'), g('qre'), g('bre'), ALU.mult)
        P.tt(g('t2'), g('qim'), g('bim'), ALU.mult)
        P.tt(g('lre'), g('t1'), g('t2'), ALU.subtract)
        P.tt(g('t1'), g('qre'), g('bim'), ALU.mult)
        P.tt(g('t2'), g('qim'), g('bre'), ALU.mult)
        P.tt(g('lim'), g('t1'), g('t2'), ALU.add)
        for mm_ in range(2):
            for gq in range(2):
                sc_ = sel[:, mm_ * 2 + gq:mm_ * 2 + gq + 1]
                P.ts(BT_re[:, :, mm_, gq * 64:(gq + 1) * 64], pp['lre'][:, :].rearrange("p (j q) -> p j q", j=4), sc_, None, ALU.mult)
                P.ts(BT_im[:, :, mm_, gq * 64:(gq + 1) * 64], pp['lim'][:, :].rearrange("p (j q) -> p j q", j=4), sc_, None, ALU.mult)
        for (arr, craw) in ((c_re, craw_re), (c_im, craw_im)):
            P.dma('sp', ctile[:, :].rearrange("c (g p) -> c g p", p=64), arr[l].rearrange("g c p -> c g p"))
            ps_ = nextps()
            for sc in range(16):
                P.mm(ps_[:, sc * 16:(sc + 1) * 16], ctile[0:16, sc * 128:(sc + 1) * 128], ident[0:16, 0:16], True, True)
            P.copy(craw[:].rearrange("p a b -> p (a b)"), ps_[:, 0:256])
        P.memset(CT_re[:], 0.0)
        P.memset(CT_imn[:], 0.0)
        for m in range(4):
            for gq in range(2):
                sc_ = sel[:, 4 + gq:5 + gq]
                c0_ = m * 32 + gq * 16
                P.ts(CT_re[:, m::4, c0_:c0_ + 16], craw_re[:, m::4, :], sc_, None, ALU.mult)
                P.ts(CT_imn[:, m::4, c0_:c0_ + 16], craw_im[:, m::4, :], sel[:, 6 + gq:7 + gq], None, ALU.mult)
        P.memset(kcar[:], 0.0); P.memset(vcar[:], 0.0); P.memset(pcar[:], 0.0); P.memset(gcar[:], 0.0)

    def layer(l):
        layer_prep(l)
        cfg['_stopfn']('prep')
        for t in range(NTILE):
            tile_layer(l, t)
            if l == nlayers - 1:
                final_tile(t)

    def tile_layer(l, t):
        c0 = t * TP
        last = (t == NTILE - 1)
        W = WMAX if last else TP
        segs = [(0, TP)] + ([(TP, NS)] if last else [])
        cfg['_stopfn']('tile%d' % t)
        LS = (lambda n: cfg['_stopfn']('L_' + n)) if last else (lambda n: None)

        def lin(ps, wfn, kcs, rfn):
            kcs = list(kcs)
            for i, kc in enumerate(kcs):
                for (s0, sn) in segs:
                    P.mm(ps[:, s0:s0 + sn], wfn(kc), rfn(kc, s0, sn), i == 0, i == len(kcs) - 1)

        def s3(ap):
            return ap.rearrange("p (b t) -> p b t", t=4)

        def norm(Am, Bfn):
            for kc in range(KC):
                P.act(sq[:, kc % 2, 0:W], x[:, kc, c0:c0 + W], AF.Square)
                for (s0, sn) in segs:
                    P.mm(psX[:, s0:s0 + sn], onesb[:], sq[:, kc % 2, s0:s0 + sn], kc == 0, kc == KC - 1)
            P.ts(rb[:, 0:W], psX[:, 0:W], 1.0 / D, EPS, ALU.mult, ALU.add)
            P.recip(rb[:, 0:W], rb[:, 0:W])
            P.act(rb[:, 0:W], rb[:, 0:W], AF.Sqrt)
            for kc in range(KC):
                tm = ntmp[:, kc % 2, :]
                P.tt(tm[:, 0:W], x[:, kc, c0:c0 + W], rb[:, 0:W], ALU.mult)
                P.act(h[:, kc, 0:TP], tm[:, 0:TP], AF.Identity, scale=Am[:, kc, 0:1], bias=Bfn(kc)[:, 0:1])
                if last:
                    v = s3(tm[:, TP:W])
                    P.tt(v, v, bc(Am[:, kc, 1:17].unsqueeze(2), [128, 16, 4]), ALU.mult)
                    P.tt(s3(h[:, kc, TP:W]), v, bc(Bfn(kc)[:, 1:17].unsqueeze(2), [128, 16, 4]), ALU.add)

        def resid(ps, oc, gch):
            P.stt(x[:, oc, c0:c0 + TP], ps[:, 0:TP], modT[:, gch + oc, 0:1], x[:, oc, c0:c0 + TP], ALU.mult, ALU.add)
            if last:
                tmv = s3(rb[:, 0:NS])
                P.tt(tmv, s3(ps[:, TP:W]), bc(modT[:, gch + oc, 1:17].unsqueeze(2), [128, 16, 4]), ALU.mult)
                P.tt(s3(x[:, oc, T:TOT]), s3(x[:, oc, T:TOT]), tmv, ALU.add)

        def gate_block(br, qtr, perm_b=False):
            gc0 = 1792 + br * 1024 + qtr * 256

            def issue(buf):
                P.dma('pool', buf[:, 0:2048].rearrange("p (k m) -> p k m", k=8),
                      w_in[l][:, gc0:gc0 + 256].rearrange("(k p) m -> p k m", p=128))
                dst = buf[:, 2048:3072].rearrange("p (k m) -> p k m", k=4)
                if not perm_b:
                    P.dma('pool', dst, wbr[br][l][:, qtr * 256:(qtr + 1) * 256].rearrange("(k p) m -> p k m", p=128))
                else:
                    for g_ in range(2):
                        P.dma('pool', dst[g_ * 64:(g_ + 1) * 64],
                              wbr[br][l][g_ * 256:(g_ + 1) * 256, qtr * 256:(qtr + 1) * 256].rearrange("(c d) m -> d c m", d=64))
            return issue

        def merge(br, src):
            for qtr in range(4):
                buf = W_.get(gate_block(br, qtr, perm_b=(br == 1)))
                gt = buf[:, 0:2048].rearrange("p (k m) -> p k m", k=8)
                bt = buf[:, 2048:3072].rearrange("p (k m) -> p k m", k=4)
                for o in range(2):
                    oc = qtr * 2 + o
                    psB = nextps(); psG = nextps()
                    lin(psB, lambda kc: bt[:, kc, o * 128:(o + 1) * 128], range(4), lambda kc, s0, sn: src[:, kc, s0:s0 + sn])
                    lin(psG, lambda kc: gt[:, kc, o * 128:(o + 1) * 128], range(8), lambda kc, s0, sn: h[:, kc, s0:s0 + sn])
                    P.act(sig[:, oc % 2, 0:W], psG[:, 0:W], AF.Sigmoid)
                    if br == 0:
                        P.tt(merged[:, oc, 0:W], sig[:, oc % 2, 0:W], psB[:, 0:W], ALU.mult)
                    else:
                        P.tt(mtmp[:, oc % 2, 0:W], sig[:, oc % 2, 0:W], psB[:, 0:W], ALU.mult)
                        dst = mb if br == 2 else merged
                        P.tt(dst[:, oc, 0:W], merged[:, oc, 0:W], mtmp[:, oc % 2, 0:W], ALU.add)

        norm(A1, lambda kc: modT[:, kc, :])
        if dbg and l == 0 and t == 3:
            dump('h', h[:], [128, KC, WMAX], BF16)

        cfg['_stopfn']('n1')
        LS('n1')
        wt = wv(W_.get(wload_std(w_in[l][:, 0:512], 8, 512)), 8, 512)
        if t == 0:
            P.memset(xa[:, :, 0:15], 0.0)
        else:
            P.copy(xa[:, :, 0:15], pcar[:])
        for oc in range(4):
            ps = nextps()
            lin(ps, lambda kc: wt[:, kc, oc * 128:(oc + 1) * 128], range(8), lambda kc, s0, sn: h[:, kc, s0:s0 + sn])
            P.copy(xa[:, oc, 15:15 + TP], ps[:, 0:TP], q='act')
            if last:
                P.copy(xes[:, oc, :, 15:19], s3(ps[:, TP:W]), q='act')
        P.copy(pcar[:], xa[:, :, TP:TP + 15])
        if last:
            for hh in range(2):
                P.dma('sp', sptok[0:120, hh, :], spool[l, hh * 8:(hh + 1) * 8].rearrange("b j f -> (b j) f"))
                for g_ in range(4):
                    P.mm(psX[:, g_ * 128:g_ * 128 + 120], sptok[0:120, hh, g_ * 128:(g_ + 1) * 128], ident[0:120, 0:120], True, True)
                for g_ in range(4):
                    P.copy(xes[:, g_, hh * 8:(hh + 1) * 8, 0:15],
                           psX[:, g_ * 128:g_ * 128 + 120].rearrange("p (b j) -> p b j", j=15), q='act')
        L_ = 15 + TP
        for g_ in range(4):
            w_ = 2 << g_
            cur = xa[:, g_, :]
            lo = 0
            for si, step in enumerate([1, 2, 4, 8][:g_ + 1]):
                nxt = scr[:, si % 2, :]
                P.tt(nxt[:, lo + step:L_], cur[:, lo + step:L_], cur[:, lo:L_ - step], ALU.add)
                cur = nxt
                lo += step
            P.stt(dbuf[:, g_, 0:TP], cur[:, 15:L_], 1.0 / w_, xa[:, g_, 15:L_], ALU.mult, ALU.subtract)
            if t == 0:
                P.tt(rb[:, 0:16], cur[:, 15:31], invc[:, g_, :], ALU.mult)
                P.tt(dbuf[:, g_, 0:16], rb[:, 0:16], xa[:, g_, 15:31], ALU.subtract)
            if last:
                cur = xes[:, g_, :, :]
                lo = 0
                for si, step in enumerate([1, 2, 4, 8][:g_ + 1]):
                    nxt = scrs[:, si % 2, :, :]
                    P.tt(nxt[:, :, lo + step:19], cur[:, :, lo + step:19], cur[:, :, lo:19 - step], ALU.add)
                    cur = nxt
                    lo += step
                P.stt(s3(dbuf[:, g_, TP:W]), cur[:, :, 15:19], 1.0 / w_, xes[:, g_, :, 15:19], ALU.mult, ALU.subtract)
        for g_ in range(4):
            ps = nextps()
            lin(ps, lambda kc: pw[:, g_, :], [0], lambda kc, s0, sn: dbuf[:, g_, s0:s0 + sn])
            P.act(ya[:, g_, 0:W], ps[:, 0:W], AF.Identity, scale=v_psc[:, g_:g_ + 1])
        if last:
            for g_ in range(4):
                P.mm(psX[0:15, g_ * 128:(g_ + 1) * 128], xa[:, g_, TP:TP + 15], ident[:], True, True)
            P.copy(otok[0:15, :], psX[0:15, 0:512], q='act')
            P.dma('sp', npool_p[l], otok[0:15, :], is_out=True)
            P.dma('sp', npool_s[l, :, 0:11, :], spool[l, :, 4:15, :], is_out=True)
            for g_ in range(4):
                P.copy(xsn[:, g_, :].rearrange("p (b t) -> p b t", t=4), xes[:, g_, :, 15:19])
                P.mm(psX[0:64, 512 + g_ * 128:512 + (g_ + 1) * 128], xsn[:, g_, :], ident[:], True, True)
            P.copy(otok[0:64, :], psX[0:64, 512:1024], q='act')
            for b in range(NSEQ):
                P.dma('sp', npool_s[l, b, 11:15, :], otok[b * 4:(b + 1) * 4, :], is_out=True)
        merge(0, ya)
        if dbg and l == 0 and t == 3:
            dump('ya', ya[:], [128, 4, WMAX], BF16)
            dump('mergedA', merged[:], [128, KC, WMAX])

        cfg['_stopfn']('pool')
        LS('pool')
        P.dma('sp', cosT[:, 0:W], k_rope[0][:, c0:c0 + W])
        P.dma('sp', sinT[:, 0:W], k_rope[1][:, c0:c0 + W])

        def issue_q(buf):
            dst = buf[:, 0:4096].rearrange("p (k c g d) -> p k c g d", k=8, c=4, g=2)
            for g_ in range(2):
                for c_ in range(4):
                    cq = 512 + g_ * 256 + c_ * 64
                    P.dma('pool', dst[:, :, c_, g_, :], w_in[l][:, cq:cq + 64].rearrange("(k p) d -> p k d", p=128))
        wt = wv(W_.get(issue_q), 8, 512)

        def rope(ps, dst_ap, dst32=None):
            P.copy(q32[:, 0:W], ps[:, 0:W], q='act')
            psr = nextps()
            for (s0, sn) in segs:
                P.mm(psr[:, s0:s0 + sn], rotm[:], q32[:, s0:s0 + sn], True, True)
            P.tt(rt1[:, 0:W], q32[:, 0:W], cosT[:, 0:W], ALU.mult)
            P.tt(rt2[:, 0:W], psr[:, 0:W], sinT[:, 0:W], ALU.mult)
            if dst32 is None:
                P.tt(dst_ap, rt1[:, 0:W], rt2[:, 0:W], ALU.add)
            else:
                P.tt(dst32, rt1[:, 0:W], rt2[:, 0:W], ALU.add)
                P.copy(dst_ap, dst32, q='act')
        for c_ in range(4):
            ps = nextps()
            lin(ps, lambda kc: wt[:, kc, c_ * 128:(c_ + 1) * 128], range(8), lambda kc, s0, sn: h[:, kc, s0:s0 + sn])
            rope(ps, qT[:, c_, 0:W])
        wt = wv(W_.get(wload_std(w_in[l][:, 1024:1280], 8, 256)), 8, 256)
        ps = nextps()
        lin(ps, lambda kc: wt[:, kc, 0:128], range(8), lambda kc, s0, sn: h[:, kc, s0:s0 + sn])
        rope(ps, kT[:, 128:128 + W], dst32=kr32[:, 0:W])
        P.copy(kT[:, 0:128], kcar[:])
        P.copy(vtok[:, 0, :], vcar[:])
        psv = nextps()
        for bi in range(4):
            for kc in range(KC):
                P.mm(psv[:, bi * 128:(bi + 1) * 128], h[:, kc, bi * 128:(bi + 1) * 128], wt[:, kc, 128:256], kc == 0, kc == KC - 1)
        P.copy(vtok[:, 1:5, :], psv[:, 0:512].rearrange("p (b f) -> p b f", b=4), q='act')
        if last:
            P.copy(otok[:, 0:128], psv[:, 384:512])
            P.dma('sp', nv_p[l], otok[:, 0:128], is_out=True)
            P.mm(psX[:, 0:128], kr32[:, 384:512], ident[:], True, True)
            P.copy(otok[:, 128:256], psX[:, 0:128])
            P.dma('sp', nk_p[l], otok[:, 128:256], is_out=True)
            for kc in range(KC):
                P.mm(psX[0:NS, 256:384], h[:, kc, TP:W], wt[:, kc, 128:256], kc == 0, kc == KC - 1)
            P.copy(vnew32[:], psX[0:NS, 256:384], q='act')
            P.copy(vnew[:], vnew32[:])
            for b in range(NSEQ):
                P.dma('sp', nv_s[l, b, 124:128, :], vnew32[b * 4:(b + 1) * 4, :], is_out=True)
            P.dma('sp', nv_s[l, :, 0:124, :], cv[l, :, 4:128, :], is_out=True)
            P.dma('sp', nk_s[l, :, 0:124, :], ck[l, :, 4:128, :], is_out=True)
            P.mm(psX[0:NS, 384:512], kr32[:, TP:W], ident[:], True, True)
            P.copy(otok[0:NS, 256:384], psX[0:NS, 384:512])
            for b in range(NSEQ):
                P.dma('sp', nk_s[l, b, 124:128, :], otok[b * 4:(b + 1) * 4, 256:384], is_out=True)
            P.dma('pool', cv_tok[:], cv[l].rearrange("b k f -> k b f"))
            for hh in range(2):
                P.dma('pool', ck_tok[:], ck[l, hh * 8:(hh + 1) * 8].rearrange("b k f -> k b f"))
                pst_ = PS[3][:, 512:1024].bitcast(BF16)
                for b8 in range(8):
                    P.tr(pst_[:, b8 * 128:(b8 + 1) * 128], ck_tok[:, b8, :], identb[:])
                P.copy(kcT[:, hh * 8:(hh + 1) * 8, :], pst_[:, :].rearrange("p (b f) -> p b f", b=8))
        P.copy(kcar[:], kT[:, TP:TP + 128])
        P.copy(vcar[:], vtok[:, 4, :])
        LS('attnin')

        def attn_unit(u, nq, g_, q_fn, k_parts, mask, nk, norm_fn, v_parts, out_fn):
            psS = PS[u]
            psTO = PS[2 + u]
            psT_ = psTO[:, 0:512].bitcast(BF16)
            psO = psTO[:, 512:1024]
            sm = smal[0:nq, u, :, :]
            sk = sinkb[0:nq, g_ * 4:(g_ + 1) * 4]
            S3 = psS[0:nq, :].rearrange("p (c k) -> p c k", c=4)[:, :, 0:nk]
            rows0 = v_parts[0][0]
            steps = []

            def s_qk():
                for c_ in range(4):
                    for ki, (k0, kn, kap) in enumerate(k_parts):
                        P.mm(psS[0:nq, c_ * 256 + k0:c_ * 256 + k0 + kn], q_fn(c_), kap, ki == 0, False)
                    P.mm(psS[0:nq, c_ * 256:c_ * 256 + nk], identb[0:nq, 0:nq], mask[0:nq, 0:nk], False, True)
            steps.append(s_qk)
            steps.append(lambda: P.rmax(sm[:, 0, :], S3))
            steps.append(lambda: P.ts(sm[:, 0, :], sm[:, 0, :], 0.125, None, ALU.mult))
            steps.append(lambda: P.tt(sm[:, 1, :], sm[:, 0, :], sk, ALU.max))
            steps.append(lambda: P.ts(sm[:, 2, :], sm[:, 1, :], -1.0, None, ALU.mult))
            steps.append(lambda: P.tt(sm[:, 4, :], sk, sm[:, 1, :], ALU.subtract))

            def s_exp():
                for c_ in range(4):
                    P.act(pbuf[0:nq, u, c_, 0:nk], psS[0:nq, c_ * 256:c_ * 256 + nk], AF.Exp, scale=0.125,
                          bias=sm[:, 2, c_:c_ + 1], accum_out=sm[:, 3, c_:c_ + 1])
                P.act(sm[:, 4, :], sm[:, 4, :], AF.Exp)
            steps.append(s_exp)
            steps.append(lambda: P.tt(sm[:, 5, :], sm[:, 3, :], sm[:, 4, :], ALU.add))
            steps.append(lambda: P.recip(sm[:, 6, :], sm[:, 5, :]))
            steps.append(lambda: norm_fn(u, sm[:, 6, :]))

            def s_tr():
                for c_ in range(4):
                    for vi, (rows, vap, src_fn) in enumerate(v_parts):
                        P.tr(psT_[0:rows, (c_ * 2 + vi) * 128:(c_ * 2 + vi) * 128 + nq], src_fn(u, c_), identb[0:nq, 0:nq])
            steps.append(s_tr)
            steps.append(lambda: P.copy(pT[0:rows0, u, :, 0:nq], psT_[0:rows0, :].rearrange("p (s q) -> p s q", s=8)[:, :, 0:nq], q='act'))

            def s_pv():
                for c_ in range(4):
                    for vi, (rows, vap, src_fn) in enumerate(v_parts):
                        P.mm(psO[:, c_ * 128:c_ * 128 + nq], vap, pT[0:rows, u, c_ * 2 + vi, 0:nq], vi == 0, vi == len(v_parts) - 1)
            steps.append(s_pv)
            steps.append(lambda: out_fn(psO))
            return steps

        def run_pair(ua, ub_):
            for fa, fb in zip(ua, ub_):
                fa()
                fb()

        for bi in range(4):
            gb = t * 4 + bi
            units = []
            for g_ in range(2):
                gs = slice(g_ * 64, (g_ + 1) * 64)

                def outp(psO, bi=bi, gs=gs):
                    P.copy(yb[gs, :, bi * 128:(bi + 1) * 128], psO[gs, :].rearrange("p (c q) -> p c q", c=4))

                def normp(u, rinv):
                    P.tt(pn[:, u, :, :], pbuf[:, u, :, :], bc(rinv.unsqueeze(2), [128, 4, 256]), ALU.mult)
                units.append(attn_unit(g_, 128, g_, lambda c_, bi=bi, gs=gs: qT[gs, c_, bi * 128:(bi + 1) * 128],
                                       [(0, 256, kT[gs, bi * 128:bi * 128 + 256])],
                                       maskB if gb == 0 else maskA, 256, normp,
                                       [(128, vtok[:, bi, :], lambda u, c_: pn[:, u, c_, 0:128]),
                                        (128, vtok[:, bi + 1, :], lambda u, c_: pn[:, u, c_, 128:256])], outp))
            run_pair(units[0], units[1])
        LS('attnp')
        if last:
            for b in range(NSEQ):
                units = []
                for g_ in range(2):
                    gs = slice(g_ * 64, (g_ + 1) * 64)
                    cs_ = slice(TP + 4 * b, TP + 4 * b + 4)

                    def outp(psO, gs=gs, cs_=cs_):
                        P.copy(yb[gs, :, cs_], psO[gs, :].rearrange("p (c q) -> p c q", c=4)[:, :, 0:4])

                    def norms(u, rinv, b=b):
                        P.tt(pn[0:4, u, :, 0:128], pbuf[0:4, u, :, 0:128], bc(rinv.unsqueeze(2), [4, 4, 128]), ALU.mult)
                        P.memset(pz[0:4, u, :, :], 0.0, q='dve')
                        P.tt(pz[0:4, u, :, 4 * b:4 * b + 4], pbuf[0:4, u, :, 128:132], bc(rinv.unsqueeze(2), [4, 4, 4]), ALU.mult)
                    units.append(attn_unit(g_, 4, g_, lambda c_, gs=gs, cs_=cs_: qT[gs, c_, cs_],
                                           [(0, 128, kcT[gs, b, :]), (128, 4, kT[gs, 128 + TP + 4 * b:128 + TP + 4 * b + 4])],
                                           maskA, 132, norms,
                                           [(128, cv_tok[:, b, :], lambda u, c_: pn[0:4, u, c_, 0:128]),
                                            (64, vnew[:, :], lambda u, c_: pz[0:4, u, c_, :])], outp))
                run_pair(units[0], units[1])
        if dbg and l == 0 and t == 3:
            dump('yb', yb[:], [128, 4, WMAX], BF16)
            dump('qT', qT[:], [128, 4, WMAX], BF16)
        merge(1, yb)

        cfg['_stopfn']('attn')
        LS('attn')
        wt = wv(W_.get(wload_std(w_in[l][:, 1280:1792], 8, 512)), 8, 512)
        cfg['_stopfn']('ssm_a')
        for j in range(4):
            ps = nextps()
            lin(ps, lambda kc: wt[:, kc, j * 128:(j + 1) * 128], range(8), lambda kc, s0, sn: h[:, kc, s0:s0 + sn])
            P.copy(u32[:, j, 0:W], ps[:, 0:W], q='act')
            P.copy(ub[:, j, 0:W], ps[:, 0:W])
        cfg['_stopfn']('ssm_b')
        LS('s0')
        P.ts(ctmp[:], c_g64[:], float(8 * t), None, ALU.mult)
        P.copy(ctmpi[:], ctmp[:])
        P.tt(c_ct[:], ctmp[:], ctmpi[:], ALU.subtract)
        cfg['_stopfn']('ssm_u')
        psYs = None
        if last:
            psdr = nextps(); psdi = nextps()
            for hf in range(2):
                P.ts(ubs[:, hf, :, :], ub[:, :, TP:W], sel[:, 4 + hf:5 + hf], None, ALU.mult)
            for sc in range(16):
                j, m = sc // 4, sc % 4
                P.mm(psdr[:, sc * NS:(sc + 1) * NS], BT_re[:, j, m % 2, :], ubs[:, m // 2, j, :], True, True)
                P.mm(psdi[:, sc * NS:(sc + 1) * NS], BT_im[:, j, m % 2, :], ubs[:, m // 2, j, :], True, True)
            LS('s0b')
            P.copy(dsr[:].rearrange("p a b c -> p (a b c)"), psdr[:, :], q='act')
            LS('s0c')
            P.copy(dsi[:].rearrange("p a b c -> p (a b c)"), psdi[:, :])
            LS('s1')
            for (src_, dstq) in ((sre, hq_re), (sim, hq_im)):
                for r4 in range(4):
                    P.dma('sp', htok[:], src_[l][:, r4 * 512:(r4 + 1) * 512])
                    for s_ in range(4):
                        sc = r4 * 4 + s_
                        P.mm(psX[:, sc * 16:(sc + 1) * 16], htok[0:16, s_ * 128:(s_ + 1) * 128], ident[0:16, 0:16], True, True)
                P.copy(dstq[:, :, :, 0], psX[:, 0:256].rearrange("p (a b) -> p a b", a=16), q='act')
            LS('s2')
            ar_b = bc(c_abr[:].unsqueeze(2), [128, 16, NSEQ])
            ai_b = bc(c_abi[:].unsqueeze(2), [128, 16, NSEQ])
            for tt_ in range(4):
                pr = hq_re[:, :, :, tt_]; pi_ = hq_im[:, :, :, tt_]
                P.tt(st1[:], ar_b, pr, ALU.mult)
                P.tt(st2[:], ai_b, pi_, ALU.mult)
                P.tt(st1[:], st1[:], st2[:], ALU.subtract)
                P.tt(hq_re[:, :, :, tt_ + 1], st1[:], dsr[:, :, :, tt_], ALU.add)
                P.tt(st1[:], ar_b, pi_, ALU.mult)
                P.tt(st2[:], ai_b, pr, ALU.mult)
                P.tt(st1[:], st1[:], st2[:], ALU.add)
                P.tt(hq_im[:, :, :, tt_ + 1], st1[:], dsi[:, :, :, tt_], ALU.add)
            LS('s3')
            P.copy(hsb_re[:].rearrange("p a (b t) -> p a b t", t=4), hq_re[:, :, :, 1:5])
            P.copy(hsb_im[:].rearrange("p a (b t) -> p a b t", t=4), hq_im[:, :, :, 1:5])
            LS('s4')
            for (srcq, dsto) in ((hq_re, nre_s), (hq_im, nim_s)):
                for r4 in range(4):
                    for s_ in range(4):
                        sc = r4 * 4 + s_
                        P.mm(psX[0:16, s_ * 128:(s_ + 1) * 128], srcq[:, sc, :, 4], ident[:], True, True)
                    P.copy(htok[0:16, :], psX[0:16, 0:512], q='act')
                    P.dma('sp', dsto[l][:, r4 * 512:(r4 + 1) * 512], htok[:], is_out=True)
        LS('ssms')
        def tabs(ci, q):
            if q % 2 == 0:
                return tb_s[:, ci, :], tb_c[:, ci, :]
            base_ = sig if ci == 0 else mtmp
            return base_[:, 0, 0:TP], base_[:, 1, 0:TP]

        def prep_steps(j, m, ci, q):
            sc = 4 * j + m
            sn_, cs_ = tabs(ci, q)
            pi_ = pli[:, ci, :]
            steps = []
            steps.append(lambda: P.act(sn_, thl[:, 1, :], AF.Identity, scale=c_ft[:, sc:sc + 1], bias=c_ct[:, sc:sc + 1]))
            steps.append(lambda: P.stt(sn_, thl[:, 0, :], c_g64[:, sc:sc + 1], sn_, ALU.mult, ALU.add))
            steps.append(lambda: P.copy(pi_, sn_))
            steps.append(lambda: P.tt(sn_, sn_, pi_, ALU.subtract))
            steps.append(lambda: P.stt(cs_, sn_, -1.0, sn_, ALU.mult, ALU.max))

            def s_sin():
                P.act(sn_, sn_, AF.Sin, scale=TWO_PI_S)
                P.act(cs_, cs_, AF.Sin, scale=-TWO_PI_S, bias=halfpi[:, 0:1])
            steps.append(s_sin)
            return steps

        def main_steps(j, m, ci, q, psY):
            sc = 4 * j + m
            u = ci
            hs_ = slice((m // 2) * 64, (m // 2) * 64 + 64)
            A_, B_, C_, D_ = pl[4 * ci][:, :], pl[4 * ci + 1][:, :], pl[4 * ci + 2][:, :], pl[4 * ci + 3][:, :]
            sn_, cs_ = tabs(ci, q)
            st = {}
            rho_b = bc(c_rho[:, sc:sc + 1], [128, TP])
            steps = []

            def s_drive():
                st['psD'] = nextps()
                P.mm(st['psD'][:, 0:TP], BT_re[hs_, j, m % 2, :], ub[hs_, j, 0:TP], True, True)
                P.mm(st['psD'][:, TP:2 * TP], BT_im[hs_, j, m % 2, :], ub[hs_, j, 0:TP], True, True)
            steps.append(('drive', s_drive))
            dre = lambda: st['psD'][:, 0:TP]
            dim_ = lambda: st['psD'][:, TP:2 * TP]
            steps.append(('pre', lambda: P.tt(A_, dre(), cs_, ALU.mult)))
            steps.append(('pre', lambda: P.tt(B_, dim_(), sn_, ALU.mult)))
            steps.append(('pre', lambda: P.tt(A_, A_, B_, ALU.add)))
            steps.append(('pre', lambda: P.tt(B_, dim_(), cs_, ALU.mult)))
            steps.append(('pre', lambda: P.tt(C_, dre(), sn_, ALU.mult)))
            steps.append(('pre', lambda: P.tt(B_, B_, C_, ALU.subtract)))
            steps.append(('scan', lambda: P.scan(C_, rho_b, A_, gcar[:, sc, 0:1])))
            steps.append(('scan', lambda: P.scan(D_, rho_b, B_, gcar[:, sc, 1:2])))

            def s_carry():
                P.copy(gcar[:, sc, 0:1], C_[:, TP - 1:TP], q='act')
                P.copy(gcar[:, sc, 1:2], D_[:, TP - 1:TP], q='act')
            steps.append(('carry', s_carry))
            steps.append(('post', lambda: P.tt(A_, cs_, C_, ALU.mult)))
            steps.append(('post', lambda: P.tt(B_, sn_, D_, ALU.mult)))

            def s_hre():
                P.tt(hre[:, u, :], A_, B_, ALU.subtract)
                if last:
                    P.tt(hl[:, sc, 0:1], A_[:, TP - 1:TP], B_[:, TP - 1:TP], ALU.subtract)
            steps.append(('post', s_hre))
            steps.append(('post', lambda: P.tt(A_, cs_, D_, ALU.mult)))
            steps.append(('post', lambda: P.tt(B_, sn_, C_, ALU.mult)))

            def s_him():
                P.tt(him[:, u, :], A_, B_, ALU.add)
                if last:
                    P.tt(hl[:, sc, 1:2], A_[:, TP - 1:TP], B_[:, TP - 1:TP], ALU.add)
            steps.append(('post', s_him))

            def s_cmm():
                P.mm(psY[:, 0:TP], CT_re[:, sc, :], hre[:, u, :], m == 0, False)
                P.mm(psY[:, 0:TP], CT_imn[:, sc, :], him[:, u, :], False, m == 3)
                if last:
                    P.mm(psY[:, TP:W], CT_re[:, sc, :], hsb_re[:, sc, :], m == 0, False)
                    P.mm(psY[:, TP:W], CT_imn[:, sc, :], hsb_im[:, sc, :], False, m == 3)
            steps.append(('cmm', s_cmm))
            return steps

        def emit_prep(jq, pq, q):
            for fa, fb in zip(prep_steps(jq, 2 * pq, 0, q), prep_steps(jq, 2 * pq + 1, 1, q)):
                fa()
                fb()
        pair_list = [(j_, p_) for j_ in range(4) for p_ in range(2)]
        emit_prep(0, 0, 0)
        for j in range(4):
            psY = PS[3]
            for pr in range(2):
                q = 2 * j + pr
                ca = main_steps(j, 2 * pr, 0, q, psY)
                cb = main_steps(j, 2 * pr + 1, 1, q, psY)
                for (ta, fa), (tb_, fb) in zip(ca, cb):
                    fa()
                    fb()
                    if ta == 'carry' and q + 1 < len(pair_list):
                        emit_prep(pair_list[q + 1][0], pair_list[q + 1][1], q + 1)
            cfg['_stopfn']('ssm_post')
            Y1 = ysm[:, 0, 0:W]; Y2 = ysm[:, 1, 0:W]; SG = ysm[:, 2, 0:W]
            P.stt(Y1, u32[:, j, 0:W], v_ssmd[:, j:j + 1], psY[:, 0:W], ALU.mult, ALU.add)
            P.tt(Y2, Y1, Y1, ALU.mult)
            P.ts(Y2, Y2, 0.044715, 1.0, ALU.mult, ALU.add)
            P.tt(Y2, Y2, Y1, ALU.mult)
            P.act(SG, Y2, AF.Sigmoid, scale=1.5957691216)
            P.tt(yg[:, j, 0:W], Y1, SG, ALU.mult)
        if last:
            for (cidx, dsto) in ((0, nre_p), (1, nim_p)):
                P.mm(psX[0:16, 0:128], hl[:, :, cidx], ident[:], True, True)
                P.copy(otok[0:16, cidx * 128:(cidx + 1) * 128], psX[0:16, 0:128])
                P.dma('sp', dsto[l], otok[0:16, cidx * 128:(cidx + 1) * 128], is_out=True)
        wt = wv(W_.get(wload_std(w_glu[l], 4, 512)), 4, 512)
        for oc in range(4):
            ps = nextps()
            lin(ps, lambda kc: wt[:, kc, oc * 128:(oc + 1) * 128], range(4), lambda kc, s0, sn: yg[:, kc, s0:s0 + sn])
            P.act(ysm[:, oc % 2, 0:W], ps[:, 0:W], AF.Sigmoid)
            P.tt(yc[:, oc, 0:W], yg[:, oc, 0:W], ysm[:, oc % 2, 0:W], ALU.mult)
        if dbg and l == 0 and t == 3:
            dump('yc', yc[:], [128, 4, WMAX], BF16)
        merge(2, yc)

        cfg['_stopfn']('ssm')
        LS('ssm')
        for hb in range(2):
            wt = wv(W_.get(wload_std(w_out[l][:, hb * 512:(hb + 1) * 512], 8, 512)), 8, 512)
            for o in range(4):
                oc = hb * 4 + o
                ps = nextps()
                lin(ps, lambda kc: wt[:, kc, o * 128:(o + 1) * 128], range(8), lambda kc, s0, sn: mb[:, kc, s0:s0 + sn])
                resid(ps, oc, 16)
        cfg['_stopfn']('outp')
        norm(A2, lambda kc: modT[:, 24 + kc, :])
        for hp in range(HC // 2):
            def issue(buf, hp=hp):
                dst = buf[:, 0:4096].rearrange("p (k a m) -> p k a m", k=8, a=2)
                for a_ in range(2):
                    cw = a_ * DFF + hp * 256
                    P.dma('pool', dst[:, :, a_, :], w_ffn_in[l][:, cw:cw + 256].rearrange("(k p) n -> p k n", p=128))
            wt = W_.get(issue)[:, 0:4096].rearrange("p (k a m) -> p k a m", k=8, a=2)
            for o in range(2):
                hc = hp * 2 + o
                psA = nextps(); psB = nextps()
                lin(psA, lambda kc: wt[:, kc, 0, o * 128:(o + 1) * 128], range(8), lambda kc, s0, sn: h[:, kc, s0:s0 + sn])
                lin(psB, lambda kc: wt[:, kc, 1, o * 128:(o + 1) * 128], range(8), lambda kc, s0, sn: h[:, kc, s0:s0 + sn])
                P.act(sa[:, hc % 2, 0:W], psA[:, 0:W], AF.Silu)
                P.tt(hid[:, hc, 0:W], sa[:, hc % 2, 0:W], psB[:, 0:W], ALU.mult)
        for oc in range(KC):
            wt = wv(W_.get(wload_std(w_ffn_out[l][:, oc * 128:(oc + 1) * 128], HC, 128)), HC, 128)
            ps = nextps()
            lin(ps, lambda kc: wt[:, kc, :], range(HC), lambda kc, s0, sn: hid[:, kc, s0:s0 + sn])
            resid(ps, oc, 40)
        cfg['_stopfn']('ffn')
        LS('end')

    def final_tile(t):
        c0 = t * TP
        last = (t == NTILE - 1)
        W = WMAX if last else TP
        segs = [(0, TP)] + ([(TP, NS)] if last else [])
        for kc in range(KC):
            P.act(sq[:, kc % 2, 0:W], x[:, kc, c0:c0 + W], AF.Square)
            for (s0, sn) in segs:
                P.mm(psX[:, s0:s0 + sn], onesb[:], sq[:, kc % 2, s0:s0 + sn], kc == 0, kc == KC - 1)
        P.ts(rb[:, 0:W], psX[:, 0:W], 1.0 / D, EPS, ALU.mult, ALU.add)
        P.recip(rb[:, 0:W], rb[:, 0:W])
        P.act(rb[:, 0:W], rb[:, 0:W], AF.Sqrt)
        for kc in range(KC):
            P.stt(yf[:, kc, 0:W], x[:, kc, c0:c0 + W], v_fng[:, kc:kc + 1], rb[:, 0:W], ALU.mult, ALU.mult)
        nblk = 5 if last else 4
        for blk in range(nblk):
            nr = 128 if blk < 4 else NS
            ps = nextps()
            for kc in range(KC):
                P.mm(ps[0:nr, kc * 128:(kc + 1) * 128], yf[:, kc, blk * 128:blk * 128 + nr], ident[:], True, True)
            P.copy(iotok[0:nr, :], ps[0:nr, :], q='act')
            if blk < 4:
                P.dma('sp', y_p[c0 + blk * 128:c0 + (blk + 1) * 128, :], iotok[0:nr, :], is_out=True)
            else:
                P.dma('sp', y_s, iotok[0:nr, :], is_out=True)

    def stop(name):
        if cfg.get('stop') == name:
            raise StopBuild()
    cfg['_stopfn'] = stop
    for dry in (True, False):
        P.dry = dry
        psrr[0] = 0
        try:
            body()
        except StopBuild:
            pass
    return P


def _consts():
    ident = np.eye(128, dtype=np.float32)
    rotm = np.zeros((128, 128), np.float32)
    for d in range(128):
        dd = d % 64
        if dd < 8:
            rotm[d + 8, d] = -1.0
        elif dd < 16:
            rotm[d - 8, d] = 1.0
    qi = np.arange(128)[:, None]
    kj = np.arange(256)[None, :]
    diff = 128 + qi - kj
    band = (diff >= 0) & (diff < 128)
    mA = np.where(band, 0.0, -30000.0).astype(np.float32)
    mB = np.where(band & (kj >= 128), 0.0, -30000.0).astype(np.float32)
    mask = np.stack([mA, mB])
    pos = np.concatenate([np.arange(T), np.tile(PAST + np.arange(4), NSEQ)]).astype(np.float32)
    inv = (500000.0 ** (-np.arange(0, 16, 2, dtype=np.float32) / 16)).astype(np.float32)
    ang = pos[:, None] * inv[None, :]
    cos = np.cos(ang).astype(np.float32)
    sin = np.sin(ang).astype(np.float32)
    rope = np.zeros((2, 128, TOT), np.float32)
    rope[0] = 1.0
    for p in range(128):
        dd = p % 64
        if dd < 16:
            rope[0, p] = cos[:, dd % 8]
            rope[1, p] = sin[:, dd % 8]
    sel = np.zeros((128, 8), np.float32)
    for p in range(128):
        for mm_ in range(2):
            for gq in range(2):
                sel[p, mm_ * 2 + gq] = 1.0 if ((p % 64) // 32 == mm_ and (p % 32) // 16 == gq) else 0.0
        for gq in range(2):
            sel[p, 4 + gq] = 1.0 if p // 64 == gq else 0.0
            sel[p, 6 + gq] = -sel[p, 4 + gq]
    invc = np.zeros((128, 4, 16), np.float32)
    for g in range(4):
        w = 2 << g
        invc[:, g, :] = 1.0 / np.minimum(np.arange(16) + 1, w)
    tt_ = np.arange(512)
    thl = np.concatenate([(tt_ // 64), (tt_ % 64)]).astype(np.float32)[None, :]
    kprep = np.zeros((32, 800), np.float32)
    for g in range(32):
        if g % 2 == 0:
            kprep[g, g // 2] = 1.0
        else:
            kprep[g, 16 + g // 2] = 1.0
        kprep[g, 32:96] = 1.0
        kprep[g, 160 + 64:160 + 128] = 1.0
        j, mg = g // 8, g % 8
        kprep[g, 288 + j * 128 + mg * 16:288 + j * 128 + mg * 16 + 16] = 1.0
    return dict(k_prep=kprep, k_ident=ident, k_rotm=rotm, k_mask=mask, k_rope=rope, k_sel=sel,
                k_invc=invc.reshape(128, 64), k_thl=thl)


_WNAMES = ['norm1_g', 'norm2_g', 'w_ada', 'b_ada', 'w_in', 'pool_w', 'pool_scale', 'attn_sinks', 'ssm_a_re',
           'ssm_a_im', 'ssm_log_dt', 'ssm_b_re', 'ssm_b_im', 'ssm_c_re', 'ssm_c_im', 'ssm_d', 'w_glu',
           'w_branch_a', 'w_branch_b', 'w_branch_c', 'w_out', 'w_ffn_in', 'w_ffn_out', 'final_norm_g']


def make_in_maps(inputs, ncores=8):
    f = lambda a: np.ascontiguousarray(np.asarray(a, dtype=np.float32))
    shared = {n: f(inputs[n]) for n in _WNAMES}
    shared.update(_consts())
    maps = []
    for c in range(ncores):
        sl = slice(c * NSEQ, (c + 1) * NSEQ)
        m = dict(shared)
        m['xp'] = f(inputs['x_prompt'][c])
        m['xs'] = f(np.asarray(inputs['x_sample'])[sl].reshape(NS, D))
        m['ck'] = f(np.asarray(inputs['cache_win_k'])[:, sl].reshape(NL, NSEQ, 128, 128))
        m['cv'] = f(np.asarray(inputs['cache_win_v'])[:, sl].reshape(NL, NSEQ, 128, 128))
        m['spool'] = f(np.asarray(inputs['state_pool'])[:, sl])
        m['sre'] = f(np.asarray(inputs['state_ssm_re'])[:, sl].reshape(NL, NSEQ, 2048))
        m['sim'] = f(np.asarray(inputs['state_ssm_im'])[:, sl].reshape(NL, NSEQ, 2048))
        m['c17'] = f(np.concatenate([np.asarray(inputs['c_prompt'])[c:c + 1], np.asarray(inputs['c_sample'])[sl]], 0))
        maps.append(m)
    return maps


def assemble(results):
    R = results
    n = len(R)
    cat = lambda k, ax: np.concatenate([r[k] for r in R], axis=ax)
    y_prompt = np.stack([r['y_p'] for r in R])
    y_sample = cat('y_s', 0).reshape(n * NSEQ, 4, D)
    nk_p = np.stack([r['nk_p'] for r in R], 1).reshape(NL, n, 128, 2, 64)
    nv_p = np.stack([r['nv_p'] for r in R], 1).reshape(NL, n, 128, 2, 64)
    npool_p = np.stack([r['npool_p'] for r in R], 1)
    nre_p = np.stack([r['nre_p'] for r in R], 1).reshape(NL, n, 32, 64)
    nim_p = np.stack([r['nim_p'] for r in R], 1).reshape(NL, n, 32, 64)
    nk_s = cat('nk_s', 1).reshape(NL, n * NSEQ, 128, 2, 64)
    nv_s = cat('nv_s', 1).reshape(NL, n * NSEQ, 128, 2, 64)
    npool_s = cat('npool_s', 1)
    nre_s = cat('nre_s', 1).reshape(NL, n * NSEQ, 32, 64)
    nim_s = cat('nim_s', 1).reshape(NL, n * NSEQ, 32, 64)
    outs = (y_prompt, y_sample, nk_p, nv_p, npool_p, nre_p, nim_p, nk_s, nv_s, npool_s, nre_s, nim_s)
    return tuple(np.ascontiguousarray(o, dtype=np.float32) for o in outs)


def kernel(**inputs):
    from contextlib import ExitStack
    nc = bass.Bass("TRN2", target_bir_lowering=False)
    cfg = {}
    with ExitStack() as es:
        P = build(nc, cfg)
        P.finish(es)
    in_maps = make_in_maps(inputs, 8)
    res = run_bass_kernel_spmd(nc, in_maps, core_ids=list(range(8)))
    return assemble(res.results)
```

```python
import math
import numpy as np
import concourse.bass as bass
import concourse.mybir as mybir
from concourse.bass_utils import run_bass_kernel_spmd

F32 = mybir.dt.float32
BF16 = mybir.dt.bfloat16
I32 = mybir.dt.int32
AF = mybir.ActivationFunctionType
ALU = mybir.AluOpType
AX = mybir.AxisListType

NL = 4
D = 1024
KC = 8
T = 2048
TP = 512
NTILE = 4
NSEQ = 16
NS = 64
TOT = T + NS
WMAX = TP + NS
PAST = 8192
EPS = 1e-6
INC = 1792 + 3072
DFF = 2816
HC = 22
NSEM = 26
BLK = 64
TWO_PI_S = 6.28318


class Prog:
    QS = ('pe', 'act', 'dve', 'pool', 'sp')

    def __init__(self, nc):
        self.nc = nc
        self.dry = False
        self.ins = []
        self.qins = {q: [] for q in self.QS}
        self.last_w = {}
        self.rd_c = {}
        self.rd_d = {}
        self.dma_cnt = {}
        self.dma_last = {}
        self.sem_pool = {'pool': list(range(0, 14)), 'sp': list(range(14, NSEM))}
        self.out_dmas = []
        self.base = {}
        self.n_t = 0
        self.retired = {}
        self.dependents = {}
        self.dsem = None

    def sbt(self, name, shape, dt=F32):
        t = self.nc.alloc_sbuf_tensor(name, list(shape), dt)
        return t

    def sbt_at(self, name, shape, dt, off):
        self.n_t += 1
        return self.nc.alloc_sbuf_tensor_at("%s_%d" % (name, self.n_t), list(shape), dt, offset=off)

    def pst(self, name, shape, dt=F32):
        return self.nc.alloc_psum_tensor(name, list(shape), dt)

    def _ranges(self, ap):
        sp = str(ap.space)
        if 'DRAM' in sp:
            return None
        name = ap.tensor.name
        key = self.base.get(name)
        if key is None:
            m = self.nc.lookup_mloc(ap.tensor)
            if 'PSUM' in sp:
                key = ('P', m.bank * 2048 + m.addr)
            else:
                key = ('S', m.addr)
            self.base[name] = key
        spc, b0 = key
        es = mybir.dt.size(ap.dtype)
        dims = list(ap.ap)
        pstride = dims[0][0]
        off = ap.offset
        foff = off % pstride if pstride > 0 else off
        free = [(abs(s), n) for (s, n) in dims[1:] if n > 1 and s != 0]
        free.sort(reverse=True)
        out = []

        def rec(base, ds):
            if not ds:
                out.append((base, base + 1))
                return
            ext = sum(s * (n - 1) for s, n in ds) + 1
            if len(ds) == 1 or ext * es <= 2 * BLK or ds[0][1] > 64:
                out.append((base, base + ext))
                return
            s, n = ds[0]
            inner = sum(s2 * (n2 - 1) for s2, n2 in ds[1:]) + 1
            if inner >= s:
                out.append((base, base + ext))
                return
            for i in range(n):
                rec(base + i * s, ds[1:])
        rec(foff, free)
        blocks = set()
        blk = 2048 if spc == 'P' else BLK
        for lo, hi in out:
            a = (b0 + lo * es) // blk
            b = (b0 + hi * es - 1) // blk
            for k in range(a, b + 1):
                blocks.add((spc, k))
        return blocks

    def _add(self, q, fn, deps, dma=False):
        iid = len(self.ins)
        rec = dict(id=iid, q=q, fn=fn, dma=dma, signal=False, qidx=len(self.qins[q]))
        nd = []
        for d in set(deps):
            d = self.retired.get(d, d)
            dr = self.ins[d]
            if (not dma) and (not dr['dma']) and dr['q'] == q:
                if q == 'pe':
                    continue
            nd.append(d)
            if dr['dma']:
                self.dependents.setdefault(d, []).append(iid)
        rec['deps'] = nd
        self.ins.append(rec)
        self.qins[q].append(rec)
        return iid

    def op(self, q, fn, outs=(), ins=(), dma=False, out=False):
        if self.dry:
            return None
        rb = set()
        wb = set()
        for a in ins:
            if a is None or isinstance(a, (int, float)):
                continue
            r = self._ranges(a)
            if r:
                rb |= r
        for a in outs:
            r = self._ranges(a)
            if r:
                wb |= r
        pr = set(k for k in rb if k[0] == 'P')
        if pr:
            rb -= pr
            wb |= pr
        deps = set()
        for k in rb:
            w = self.last_w.get(k)
            if w is not None:
                deps.add(w)
        for k in wb:
            w = self.last_w.get(k)
            if w is not None:
                deps.add(w)
            rc = self.rd_c.get(k)
            if rc:
                deps.update(rc.values())
            rd = self.rd_d.get(k)
            if rd:
                deps.update(rd)
        sem = None
        if dma:
            pool_ = self.sem_pool[q]
            cnt = self.dma_cnt.get(q, 0)
            self.dma_cnt[q] = cnt + 1
            sem = pool_[cnt % len(pool_)]
            prev = self.dma_last.get(sem)
            if prev is not None:
                deps.add(prev)
        iid = self._add(q, fn, deps, dma=dma)
        rec = self.ins[iid]
        if dma:
            rec['sem'] = sem
            rec['semval'] = 16 * (cnt // len(pool_) + 1)
            self.dma_last[sem] = iid
            if out:
                self.out_dmas.append(iid)
        for k in rb:
            if dma:
                self.rd_d.setdefault(k, []).append(iid)
            else:
                self.rd_c.setdefault(k, {})[q] = iid
        for k in wb:
            self.last_w[k] = iid
            self.rd_c[k] = {}
            self.rd_d[k] = []
        return iid

    def finish(self, es):
        fdeps = [self.retired.get(d, d) for d in self.out_dmas]
        fin = dict(id=len(self.ins), q='sp', fn=None, dma=False, signal=False,
                   qidx=len(self.qins['sp']), deps=list(set(fdeps)))
        self.ins.append(fin)
        self.qins['sp'].append(fin)
        for r in self.ins:
            for d in r['deps']:
                self.ins[d]['signal'] = True
        for q in self.QS:
            c = 0
            for r in self.qins[q]:
                if (not r['dma']) and r['signal']:
                    c += 1
                r['cnt'] = c
        nc = self.nc
        csem = {q: es.enter_context(nc.semaphore('cs_' + q)) for q in self.QS}
        dsem = [es.enter_context(nc.semaphore('ds%d' % i)) for i in range(NSEM)]
        self.dsem = dsem
        block = es.enter_context(nc.Block())

        def emit(eng, q):
            waited = {}
            for r in self.qins[q]:
                need = {}
                for d in r['deps']:
                    dr = self.ins[d]
                    if dr['dma']:
                        key = ('d', dr['sem'])
                        val = dr['semval']
                        sem = dsem[dr['sem']]
                    else:
                        key = ('c', dr['q'])
                        val = dr['cnt']
                        sem = csem[dr['q']]
                    if need.get(key, (0, None))[0] < val:
                        need[key] = (val, sem)
                for key, (val, sem) in need.items():
                    if waited.get(key, 0) >= val:
                        continue
                    eng.wait_ge(sem, val)
                    waited[key] = val
                if r['fn'] is None:
                    continue
                bi = r['fn'](eng)
                if r['dma']:
                    bi.then_inc(dsem[r['sem']], 16)
                elif r['signal']:
                    bi.then_inc(csem[q], 1)
        block.tensor(lambda e: emit(e, 'pe'))
        block.scalar(lambda e: emit(e, 'act'))
        block.vector(lambda e: emit(e, 'dve'))
        block.gpsimd(lambda e: emit(e, 'pool'))
        block.sync(lambda e: emit(e, 'sp'))

    def dma(self, q, out, in_, is_out=False, **kw):
        return self.op(q, lambda e: e.dma_start(out=out, in_=in_, **kw), [out], [in_], dma=True, out=is_out)

    def mm(self, out, lhsT, rhs, start, stop):
        return self.op('pe', lambda e: e.matmul(out, lhsT, rhs, start=start, stop=stop), [out], [lhsT, rhs])

    def tr(self, out, in_, ident):
        return self.op('pe', lambda e: e.transpose(out, in_, ident), [out], [in_, ident])

    def act(self, out, in_, func, scale=1.0, bias=0.0, accum_out=None, q='act'):
        ins = [in_]
        outs = [out]
        if not isinstance(scale, (int, float)):
            ins.append(scale)
        if not isinstance(bias, (int, float)):
            ins.append(bias)
        kw = {}
        if accum_out is not None:
            outs.append(accum_out)
            kw['accum_out'] = accum_out
        return self.op(q, lambda e: e.activation(out, in_, func, bias=bias, scale=scale, **kw), outs, ins)

    def tt(self, out, a, b, op, q='dve'):
        return self.op(q, lambda e: e.tensor_tensor(out, a, b, op), [out], [a, b])

    def ts(self, out, a, s1, s2, op0, op1=None, q='dve'):
        ins = [a]
        if not isinstance(s1, (int, float)):
            ins.append(s1)
        if s2 is not None and not isinstance(s2, (int, float)):
            ins.append(s2)
        if op1 is None:
            return self.op(q, lambda e: e.tensor_scalar(out, a, s1, None, op0), [out], ins)
        return self.op(q, lambda e: e.tensor_scalar(out, a, s1, s2, op0, op1), [out], ins)

    def stt(self, out, in0, scalar, in1, op0, op1, q='dve'):
        ins = [in0, in1]
        if not isinstance(scalar, (int, float)):
            ins.append(scalar)
        return self.op(q, lambda e: e.scalar_tensor_tensor(out, in0, scalar, in1, op0, op1), [out], ins)

    def copy(self, out, in_, q='dve'):
        if q == 'act':
            return self.op(q, lambda e: e.copy(out, in_), [out], [in_])
        return self.op(q, lambda e: e.tensor_copy(out, in_), [out], [in_])

    def memset(self, ap, v, q='pool'):
        return self.op(q, lambda e: e.memset(ap, v), [ap], [])

    def recip(self, out, in_):
        return self.op('dve', lambda e: e.reciprocal(out, in_), [out], [in_])

    def scan(self, out, d0, d1, init):
        ins = [d0, d1]
        if not isinstance(init, (int, float)):
            ins.append(init)
        return self.op('dve', lambda e: e.tensor_tensor_scan(out, d0, d1, init, ALU.mult, ALU.add), [out], ins)

    def rmax(self, out, in_):
        return self.op('dve', lambda e: e.tensor_reduce(out, in_, AX.X, ALU.max), [out], [in_])


class StopBuild(Exception):
    pass


class WStream:
    NB = 3
    LOOK = 2

    def __init__(self, P, bufs):
        self.P = P
        self.bufs = bufs
        self.plan = []
        self.i = 0
        self.issued = 0

    def get(self, issue):
        P = self.P
        if P.dry:
            self.plan.append(issue)
            return self.bufs[(len(self.plan) - 1) % self.NB]
        i = self.i
        while self.issued <= min(i + self.LOOK, len(self.plan) - 1):
            j = self.issued
            self.plan[j](self.bufs[j % self.NB])
            self.issued += 1
        self.i += 1
        return self.bufs[i % self.NB]


def bc(ap, shape):
    return ap.to_broadcast(list(shape))


def build(nc, cfg):
    P = Prog(nc)
    dbg = cfg.get('dbg', False)
    nlayers = cfg.get('nlayers', NL)

    def DI(name, shape, dt=F32):
        return nc.dram_tensor(name, list(shape), dt, kind="ExternalInput").ap()

    def DO(name, shape, dt=F32):
        return nc.dram_tensor(name, list(shape), dt, kind="ExternalOutput").ap()

    xp = DI('xp', [T, D]); xs = DI('xs', [NS, D])
    ck = DI('ck', [NL, NSEQ, 128, 128]); cv = DI('cv', [NL, NSEQ, 128, 128])
    spool = DI('spool', [NL, NSEQ, 15, 512])
    sre = DI('sre', [NL, NSEQ, 2048]); sim = DI('sim', [NL, NSEQ, 2048])
    c17 = DI('c17', [17, D])
    n1g = DI('norm1_g', [NL, D]); n2g = DI('norm2_g', [NL, D])
    w_ada = DI('w_ada', [NL, D, 6 * D]); b_ada = DI('b_ada', [NL, 6 * D])
    w_in = DI('w_in', [NL, D, INC])
    pool_w = DI('pool_w', [NL, 4, 128, 128]); pool_scale = DI('pool_scale', [NL, 512])
    sinks = DI('attn_sinks', [NL, 8])
    a_re = DI('ssm_a_re', [NL, 32, 64]); a_im = DI('ssm_a_im', [NL, 32, 64]); log_dt = DI('ssm_log_dt', [NL, 32])
    b_re = DI('ssm_b_re', [NL, 32, 64, 16]); b_im = DI('ssm_b_im', [NL, 32, 64, 16])
    c_re = DI('ssm_c_re', [NL, 32, 16, 64]); c_im = DI('ssm_c_im', [NL, 32, 16, 64])
    ssm_d = DI('ssm_d', [NL, 512])
    w_glu = DI('w_glu', [NL, 512, 512])
    wbr = [DI('w_branch_a', [NL, 512, D]), DI('w_branch_b', [NL, 512, D]), DI('w_branch_c', [NL, 512, D])]
    w_out = DI('w_out', [NL, D, D])
    w_ffn_in = DI('w_ffn_in', [NL, D, 2 * DFF]); w_ffn_out = DI('w_ffn_out', [NL, DFF, D])
    fng = DI('final_norm_g', [D])
    k_ident = DI('k_ident', [128, 128]); k_rotm = DI('k_rotm', [128, 128])
    k_mask = DI('k_mask', [2, 128, 256]); k_rope = DI('k_rope', [2, 128, TOT])
    k_prep = DI('k_prep', [32, 800]); k_sel = DI('k_sel', [128, 8]); k_invc = DI('k_invc', [128, 64]); k_thl = DI('k_thl', [1, 1024])

    y_p = DO('y_p', [T, D]); y_s = DO('y_s', [NS, D])
    nk_p = DO('nk_p', [NL, 128, 128]); nv_p = DO('nv_p', [NL, 128, 128])
    npool_p = DO('npool_p', [NL, 15, 512])
    nre_p = DO('nre_p', [NL, 16, 128]); nim_p = DO('nim_p', [NL, 16, 128])
    nk_s = DO('nk_s', [NL, NSEQ, 128, 128]); nv_s = DO('nv_s', [NL, NSEQ, 128, 128])
    npool_s = DO('npool_s', [NL, NSEQ, 15, 512])
    nre_s = DO('nre_s', [NL, NSEQ, 2048]); nim_s = DO('nim_s', [NL, NSEQ, 2048])
    dbg_outs = {}

    def dump(name, ap, shape, dt=F32):
        if not dbg or P.dry:
            return
        o = DO('dbg_' + name, shape, dt)
        P.dma('sp', o, ap, is_out=True)

    x = P.sbt('x', [128, KC, TOT])
    wbufs = [P.sbt('wbuf%d' % i, [128, 4096], BF16) for i in range(3)]
    modT = P.sbt('modT', [128, 48, 17]); A1 = P.sbt('A1', [128, 8, 17]); A2 = P.sbt('A2', [128, 8, 17])
    scT = P.sbt('scT', [128, 8, 17], BF16)
    vecs = P.sbt('vecs', [128, 72]); v_fng = P.sbt('v_fng', [128, 8])
    v_n1g = vecs[:, 0:8]; v_n2g = vecs[:, 8:16]; v_bada = vecs[:, 16:64]; v_psc = vecs[:, 64:68]; v_ssmd = vecs[:, 68:72]
    BT_re = P.sbt('BT_re', [128, 4, 2, 128], BF16); BT_im = P.sbt('BT_im', [128, 4, 2, 128], BF16)
    CT_re = P.sbt('CT_re', [128, 16, 128], BF16); CT_imn = P.sbt('CT_imn', [128, 16, 128], BF16)
    c_rho = P.sbt('c_rho', [128, 16]); c_ft = P.sbt('c_ft', [128, 16]); c_g64 = P.sbt('c_g64', [128, 16])
    c_abr = P.sbt('c_abr', [128, 16]); c_abi = P.sbt('c_abi', [128, 16]); c_ct = P.sbt('c_ct', [128, 16])
    pw = P.sbt('pw', [128, 4, 128], BF16)
    kcar = P.sbt('kcar', [128, 128], BF16); vcar = P.sbt('vcar', [128, 128], BF16)
    pcar = P.sbt('pcar', [128, 4, 15]); gcar = P.sbt('gcar', [128, 16, 2])
    ident = P.sbt('ident', [128, 128]); identb = P.sbt('identb', [128, 128], BF16)
    rotm = P.sbt('rotm', [128, 128]); onesb = P.sbt('onesb', [128, 128], BF16)
    maskA = P.sbt('maskA', [128, 256], BF16); maskB = P.sbt('maskB', [128, 256], BF16)
    sel = P.sbt('sel', [128, 8]); invc = P.sbt('invc', [128, 4, 16]); sinkb = P.sbt('sinkb', [128, 8])
    thl = P.sbt('thl', [128, 2, 512], BF16)
    h = P.sbt('h', [128, KC, WMAX], BF16)
    merged = P.sbt('merged', [128, KC, WMAX])
    otok = P.sbt('otok', [128, 512])
    halfpi = P.sbt('halfpi', [128, 1]); ctmp = P.sbt('ctmp', [128, 16]); ctmpi = P.sbt('ctmpi', [128, 16], I32)
    arena_sz = (nc.sbuf_bytes_remaining - 64) // 64 * 64
    arena = P.sbt('arena', [128, arena_sz // 4])
    abase = nc.lookup_mloc(arena).addr
    cfg['arena'] = arena_sz

    class Ar:
        def __init__(self):
            self.off = 0

        def a(self, name, shape, dt=F32):
            n = 1
            for s_ in shape[1:]:
                n *= s_
            nb = (n * mybir.dt.size(dt) + 63) // 64 * 64
            assert self.off + nb <= arena_sz, (name, self.off, nb, arena_sz)
            t = P.sbt_at(name, shape, dt, abase + self.off)
            self.off += nb
            return t

    PS = [P.pst('ps%d' % i, [128, 1024]) for i in range(4)]
    psrr = [0]

    def nextps():
        psrr[0] = (psrr[0] + 1) % 3
        return PS[psrr[0]]
    psX = PS[3]

    W_ = WStream(P, wbufs)

    def wv(buf, kc, n):
        return buf[:, 0:kc * n].rearrange("p (k m) -> p k m", k=kc)

    def wload_std(src, kc, n):
        def issue(buf):
            P.dma('pool', wv(buf, kc, n), src.rearrange("(k p) m -> p k m", p=128))
        return issue

    ar = Ar()
    sq = ar.a('sq', [128, 2, WMAX], BF16); rb = ar.a('rb', [128, WMAX]); ntmp = ar.a('ntmp', [128, 2, WMAX])
    norm_end = ar.off
    ar.off = 0
    sig = ar.a('sig', [128, 2, WMAX]); mtmp = ar.a('mtmp', [128, 2, WMAX])
    br_base = max(ar.off, norm_end)
    ar.off = br_base
    xa = ar.a('xa', [128, 4, 15 + TP]); xes = ar.a('xes', [128, 4, NSEQ, 19])
    scr = ar.a('scr', [128, 2, 15 + TP]); scrs = ar.a('scrs', [128, 2, NSEQ, 19])
    dbuf = ar.a('dbuf', [128, 4, WMAX], BF16); ya = ar.a('ya', [128, 4, WMAX], BF16)
    sptok = ar.a('sptok', [128, 2, 512]); xsn = ar.a('xsn', [128, 4, NS])
    pool_end = ar.off
    ar.off = br_base
    qT = ar.a('qT', [128, 4, WMAX], BF16); kT = ar.a('kT', [128, 128 + WMAX], BF16)
    vtok = ar.a('vtok', [128, 6, 128], BF16)
    yb = ar.a('yb', [128, 4, WMAX], BF16)
    cv_tok = ar.a('cv_tok', [128, NSEQ, 128], BF16); kcT = ar.a('kcT', [128, NSEQ, 128], BF16)
    vnew = ar.a('vnew', [64, 128], BF16); vnew32 = ar.a('vnew32', [64, 128])
    smal = ar.a('smal', [128, 2, 8, 4])
    pz = ar.a('pz', [4, 2, 4, 64], BF16)
    al0 = ar.off
    pbuf = ar.a('pbuf', [128, 2, 4, 256]); pn = ar.a('pn', [128, 2, 4, 256], BF16)
    pT = ar.a('pT', [128, 2, 8, 128], BF16)
    attn_end = ar.off
    ar.off = al0
    q32 = ar.a('q32', [128, WMAX]); rt1 = ar.a('rt1', [128, WMAX]); rt2 = ar.a('rt2', [128, WMAX])
    cosT = ar.a('cosT', [128, WMAX]); sinT = ar.a('sinT', [128, WMAX]); kr32 = ar.a('kr32', [128, WMAX])
    ck_tok = ar.a('ck_tok', [128, 8, 128], BF16)
    attn_end = max(attn_end, ar.off)
    ar.off = br_base
    u32 = ar.a('u32', [128, 4, WMAX]); ub = ar.a('ub', [128, 4, WMAX], BF16)
    yg = ar.a('yg', [128, 4, WMAX], BF16)
    mb = P.sbt_at('mb', [128, KC, WMAX], BF16, abase + br_base)
    sl0 = ar.off
    pl = [ar.a('pl%d' % i, [128, TP]) for i in range(8)]
    pli = ar.a('pli', [128, 2, TP], I32)
    tb_s = ar.a('tb_s', [128, 2, TP]); tb_c = ar.a('tb_c', [128, 2, TP])
    hre = ar.a('hre', [128, 2, TP], BF16); him = ar.a('him', [128, 2, TP], BF16)
    sl1 = ar.off
    hl = ar.a('hl', [128, 16, 2])
    hsb_re = ar.a('hsb_re', [128, 16, NS], BF16); hsb_im = ar.a('hsb_im', [128, 16, NS], BF16)
    ubs = ar.a('ubs', [128, 2, 4, NS], BF16)
    ssm_end = ar.off
    ar.off = sl0
    ysm = ar.a('ysm', [128, 3, WMAX]); yc = ar.a('yc', [128, 4, WMAX], BF16)
    assert ar.off <= sl1
    ar.off = sl0
    htok = ar.a('htok', [16, 512])
    dsr = ar.a('dsr', [128, 16, NSEQ, 4]); dsi = ar.a('dsi', [128, 16, NSEQ, 4])
    hq_re = ar.a('hq_re', [128, 16, NSEQ, 5]); hq_im = ar.a('hq_im', [128, 16, NSEQ, 5])
    st1 = ar.a('st1', [128, 16, NSEQ]); st2 = ar.a('st2', [128, 16, NSEQ])
    assert ar.off <= sl1, (ar.off, sl1)
    ar.off = norm_end
    hid = ar.a('hid', [128, HC, WMAX], BF16); sa = ar.a('sa', [128, 2, WMAX])
    ffn_end = ar.off
    ar.off = norm_end
    yf = ar.a('yf', [128, KC, WMAX]); iotok = ar.a('iotok', [128, D])
    fin_end = ar.off
    ar.off = 0
    pp = {}
    for nm in ['are', 'aim', 'ldt', 'dt', 'lre', 'lim', 'rho', 'ft', 'fr', 'afr', 'sn', 'cs', 'abr', 'abi',
               'nr', 'den', 'qre', 'qim', 't1', 't2', 'bre', 'bim']:
        pp[nm] = ar.a('pp_' + nm, [128, 256])
    ppi = ar.a('pp_i', [128, 256], I32)
    craw_re = ar.a('craw_re', [128, 16, 16]); craw_im = ar.a('craw_im', [128, 16, 16])
    vst = ar.a('vst', [72, 128]); kp = ar.a('kp', [32, 800]); araw = ar.a('araw', [32, 2, 64])
    a2 = ar.a('a2', [32, 2, 2, 128]); ldtc = ar.a('ldtc', [32, 1]); rs01 = ar.a('rs01', [32, 2, 16])
    lb = ar.a('lb', [32, 64]); btile = ar.a('btile', [64, 4, 128]); ctile = ar.a('ctile', [16, 2048])
    c17sb = ar.a('c17sb', [17, D])
    cfg['arena_used'] = dict(pool=pool_end, attn=attn_end, ssm=ssm_end, ffn=ffn_end, fin=fin_end, prep=ar.off)

    def body():
        P.dma('sp', ident[:], k_ident)
        P.dma('sp', rotm[:], k_rotm)
        P.dma('pool', identb[:], k_ident)
        P.dma('pool', maskA[:], k_mask[0])
        P.dma('pool', maskB[:], k_mask[1])
        P.memset(onesb[:], 1.0)
        P.dma('sp', sel[:], k_sel)
        P.dma('sp', invc[:].rearrange("p a b -> p (a b)"), k_invc)
        P.dma('pool', thl[:].rearrange("p a b -> p (a b)"), k_thl.partition_broadcast(128))
        P.dma('sp', vst[0:8, :], fng.rearrange("(k p) -> k p", p=128))
        P.mm(psX[:, 512:520], vst[0:8, :], ident[0:8, 0:8], True, True)
        P.copy(v_fng[:], psX[:, 512:520])
        P.dma('sp', c17sb[:], c17)
        for kc in range(KC):
            P.mm(psX[:, kc * 17:(kc + 1) * 17], c17sb[0:17, kc * 128:(kc + 1) * 128], ident[0:17, 0:17], True, True)
        P.act(scT[:].rearrange("p a b -> p (a b)"), psX[:, 0:136], AF.Silu)
        for blk in range(TOT // 128 + 1):
            r0 = blk * 128
            nr = 128 if blk < 16 else NS
            if blk < 16:
                P.dma('sp', iotok[0:nr, :], xp[r0:r0 + nr, :])
            else:
                P.dma('sp', iotok[0:nr, :], xs)
            ps = nextps()
            for kc in range(KC):
                P.mm(ps[:, kc * 128:kc * 128 + nr], iotok[0:nr, kc * 128:(kc + 1) * 128], ident[0:nr, 0:nr], True, True)
            P.copy(x[:, :, r0:r0 + nr], ps[:, :].rearrange("p (k m) -> p k m", k=KC)[:, :, 0:nr], q='act')

        cfg['_stopfn']('setup')
        for l in range(nlayers):
            layer(l)

    def cexp(shape_n, ldt_ap, are_ap, aim_ap):
        n = shape_n
        g = lambda nm: pp[nm][:, 0:n]
        P.act(g('dt'), ldt_ap, AF.Exp)
        P.tt(g('lre'), are_ap, g('dt'), ALU.mult)
        P.tt(g('lim'), aim_ap, g('dt'), ALU.mult)
        P.act(g('rho'), g('lre'), AF.Exp)
        P.ts(g('ft'), g('lim'), 1.0 / (2 * math.pi), None, ALU.mult)
        P.copy(ppi[:, 0:n], g('ft'))
        P.tt(g('fr'), g('ft'), ppi[:, 0:n], ALU.subtract)
        P.stt(g('afr'), g('fr'), -1.0, g('fr'), ALU.mult, ALU.max)
        P.act(g('sn'), g('fr'), AF.Sin, scale=TWO_PI_S)
        P.act(g('cs'), g('afr'), AF.Sin, scale=-TWO_PI_S, bias=halfpi[:, 0:1])
        P.tt(g('abr'), g('rho'), g('cs'), ALU.mult)
        P.tt(g('abi'), g('rho'), g('sn'), ALU.mult)

    def layer_prep(l):
        r0 = 0
        for (src_, n_) in ((n1g[l], 8), (n2g[l], 8), (b_ada[l], 48), (pool_scale[l], 4), (ssm_d[l], 4)):
            P.dma('sp', vst[r0:r0 + n_, :], src_.rearrange("(k p) -> k p", p=128))
            r0 += n_
        P.mm(psX[:, 0:72], vst[0:72, :], ident[0:72, 0:72], True, True)
        P.copy(vecs[:], psX[:, 0:72])
        P.dma('sp', sinkb[:], sinks[l:l + 1, :].partition_broadcast(128))
        P.dma('pool', pw[:], pool_w[l].rearrange("g i o -> i g o"))
        P.memset(halfpi[:], math.pi / 2)
        for bk in range(12):
            wt = wv(W_.get(wload_std(w_ada[l][:, bk * 512:(bk + 1) * 512], 8, 512)), 8, 512)
            for o4 in range(4):
                for kc in range(KC):
                    P.mm(psX[:, o4 * 17:(o4 + 1) * 17], wt[:, kc, o4 * 128:(o4 + 1) * 128], scT[:, kc, :], kc == 0, kc == KC - 1)
            P.tt(modT[:, bk * 4:(bk + 1) * 4, :], psX[:, 0:68].rearrange("p (a b) -> p a b", a=4),
                 bc(v_bada[:, bk * 4:(bk + 1) * 4].unsqueeze(2), [128, 4, 17]), ALU.add)
        P.ts(A1[:], modT[:, 8:16, :], 1.0, None, ALU.add)
        P.tt(A1[:], A1[:], bc(v_n1g.unsqueeze(2), [128, 8, 17]), ALU.mult)
        P.ts(A2[:], modT[:, 32:40, :], 1.0, None, ALU.add)
        P.tt(A2[:], A2[:], bc(v_n2g.unsqueeze(2), [128, 8, 17]), ALU.mult)
        P.dma('sp', kp[:], k_prep)
        P.dma('sp', araw[:, 0, :], a_re[l]); P.dma('sp', araw[:, 1, :], a_im[l])
        P.dma('sp', ldtc[:], log_dt[l].rearrange("(g o) -> g o", o=1))
        S0 = kp[:, 0:16]; S1 = kp[:, 16:32]; E_lo = kp[:, 32:160]; E_hi = kp[:, 160:288]
        Esel = lambda j: kp[:, 288 + j * 128:288 + (j + 1) * 128]
        P.memset(a2[:], 0.0, q='dve')
        for r_ in range(2):
            P.copy(a2[:, r_, 0, 0:64], araw[:, r_, :])
            P.copy(a2[:, r_, 1, 64:128], araw[:, r_, :])
        P.ts(rs01[:, 0, :], S0, ldtc[:, 0:1], None, ALU.mult)
        P.ts(rs01[:, 1, :], S1, ldtc[:, 0:1], None, ALU.mult)
        psc_ = nextps()
        for r_ in range(2):
            P.mm(psc_[:, r_ * 16:(r_ + 1) * 16], a2[:, r_, 0, :], S0, True, False)
            P.mm(psc_[:, r_ * 16:(r_ + 1) * 16], a2[:, r_, 1, :], S1, False, True)
        P.mm(psc_[:, 32:48], E_lo, rs01[:, 0, :], True, False)
        P.mm(psc_[:, 32:48], E_hi, rs01[:, 1, :], False, True)
        P.copy(pp['are'][:, 0:16], psc_[:, 0:16]); P.copy(pp['aim'][:, 0:16], psc_[:, 16:32])
        P.copy(pp['ldt'][:, 0:16], psc_[:, 32:48])
        cexp(16, pp['ldt'][:, 0:16], pp['are'][:, 0:16], pp['aim'][:, 0:16])
        P.copy(c_rho[:], pp['rho'][:, 0:16]); P.copy(c_ft[:], pp['ft'][:, 0:16])
        P.copy(c_abr[:], pp['abr'][:, 0:16]); P.copy(c_abi[:], pp['abi'][:, 0:16])
        P.ts(pp['t1'][:, 0:16], pp['ft'][:, 0:16], 64.0, None, ALU.mult)
        P.copy(ppi[:, 0:16], pp['t1'][:, 0:16])
        P.tt(c_g64[:], pp['t1'][:, 0:16], ppi[:, 0:16], ALU.subtract)
        P.memset(lb[:], 1.0, q='dve')
        P.ts(lb[:], lb[:], ldtc[:, 0:1], None, ALU.mult)
        for (rhs_, nm) in ((araw[:, 0, :], 'are'), (araw[:, 1, :], 'aim'), (lb[:], 'ldt')):
            ps_ = nextps()
            for j in range(4):
                P.mm(ps_[:, j * 64:(j + 1) * 64], Esel(j), rhs_, True, True)
            P.copy(pp[nm][:, :], ps_[:, 0:256])
        for (arr, nm) in ((b_re, 'bre'), (b_im, 'bim')):
            ps_ = nextps()
            for j in range(4):
                P.dma('sp', btile[:, j, :].rearrange("p (m c) -> p m c", c=16), arr[l, 8 * j:8 * j + 8].rearrange("m p c -> p m c"))
                P.mm(ps_[:, j * 64:(j + 1) * 64], btile[:, j, :], ident[0:64, 0:64], True, True)
            P.copy(pp[nm][:, :], ps_[:, 0:256])
        cexp(256, pp['ldt'][:, :], pp['are'][:, :], pp['aim'][:, :])
        g = lambda nm: pp[nm][:, :]
        P.ts(g('nr'), g('abr'), -1.0, None, ALU.add)
        P.tt(g('t1'), g('are'), g('are'), ALU.mult)
        P.tt(g('t2'), g('aim'), g('aim'), ALU.mult)
        P.tt(g('den'), g('t1'), g('t2'), ALU.add)
        P.recip(g('den'), g('den'))
        P.tt(g('t1'), g('nr'), g('are'), ALU.mult)
        P.tt(g('t2'), g('abi'), g('aim'), ALU.mult)
        P.tt(g('qre'), g('t1'), g('t2'), ALU.add)
        P.tt(g('qre'), g('qre'), g('den'), ALU.mult)
        P.tt(g('t1'), g('abi'), g('are'), ALU.mult)
        P.tt(g('t2'), g('nr'), g('aim'), ALU.mult)
        P.tt(g('qim'), g('t1'), g('t2'), ALU.subtract)
        P.tt(g('qim'), g('qim'), g('den'), ALU.mult)
        P.tt(g('t1'), g('qre'), g('bre'), ALU.mult)
        P.tt(g('t2'), g('qim'), g('bim'), ALU.mult)
        P.tt(g('lre'), g('t1'), g('t2'), ALU.subtract)
        P.tt(g('t1'), g('qre'), g('bim'), ALU.mult)
        P.tt(g('t2'), g('qim'), g('bre'), ALU.mult)
        P.tt(g('lim'), g('t1'), g('t2'), ALU.add)
        for mm_ in range(2):
            for gq in range(2):
                sc_ = sel[:, mm_ * 2 + gq:mm_ * 2 + gq + 1]
                P.ts(BT_re[:, :, mm_, gq * 64:(gq + 1) * 64], pp['lre'][:, :].rearrange("p (j q) -> p j q", j=4), sc_, None, ALU.mult)
                P.ts(BT_im[:, :, mm_, gq * 64:(gq + 1) * 64], pp['lim'][:, :].rearrange("p (j q) -> p j q", j=4), sc_, None, ALU.mult)
        for (arr, craw) in ((c_re, craw_re), (c_im, craw_im)):
            P.dma('sp', ctile[:, :].rearrange("c (g p) -> c g p", p=64), arr[l].rearrange("g c p -> c g p"))
            ps_ = nextps()
            for sc in range(16):
                P.mm(ps_[:, sc * 16:(sc + 1) * 16], ctile[0:16, sc * 128:(sc + 1) * 128], ident[0:16, 0:16], True, True)
            P.copy(craw[:].rearrange("p a b -> p (a b)"), ps_[:, 0:256])
        P.memset(CT_re[:], 0.0)
        P.memset(CT_imn[:], 0.0)
        for m in range(4):
            for gq in range(2):
                sc_ = sel[:, 4 + gq:5 + gq]
                c0_ = m * 32 + gq * 16
                P.ts(CT_re[:, m::4, c0_:c0_ + 16], craw_re[:, m::4, :], sc_, None, ALU.mult)
                P.ts(CT_imn[:, m::4, c0_:c0_ + 16], craw_im[:, m::4, :], sel[:, 6 + gq:7 + gq], None, ALU.mult)
        P.memset(kcar[:], 0.0); P.memset(vcar[:], 0.0); P.memset(pcar[:], 0.0); P.memset(gcar[:], 0.0)

    def layer(l):
        layer_prep(l)
        cfg['_stopfn']('prep')
        for t in range(NTILE):
            tile_layer(l, t)
            if l == nlayers - 1:
                final_tile(t)

    def tile_layer(l, t):
        c0 = t * TP
        last = (t == NTILE - 1)
        W = WMAX if last else TP
        segs = [(0, TP)] + ([(TP, NS)] if last else [])
        cfg['_stopfn']('tile%d' % t)
        LS = (lambda n: cfg['_stopfn']('L_' + n)) if last else (lambda n: None)

        def lin(ps, wfn, kcs, rfn):
            kcs = list(kcs)
            for i, kc in enumerate(kcs):
                for (s0, sn) in segs:
                    P.mm(ps[:, s0:s0 + sn], wfn(kc), rfn(kc, s0, sn), i == 0, i == len(kcs) - 1)

        def s3(ap):
            return ap.rearrange("p (b t) -> p b t", t=4)

        def norm(Am, Bfn):
            for kc in range(KC):
                P.act(sq[:, kc % 2, 0:W], x[:, kc, c0:c0 + W], AF.Square)
                for (s0, sn) in segs:
                    P.mm(psX[:, s0:s0 + sn], onesb[:], sq[:, kc % 2, s0:s0 + sn], kc == 0, kc == KC - 1)
            P.ts(rb[:, 0:W], psX[:, 0:W], 1.0 / D, EPS, ALU.mult, ALU.add)
            P.recip(rb[:, 0:W], rb[:, 0:W])
            P.act(rb[:, 0:W], rb[:, 0:W], AF.Sqrt)
            for kc in range(KC):
                tm = ntmp[:, kc % 2, :]
                P.tt(tm[:, 0:W], x[:, kc, c0:c0 + W], rb[:, 0:W], ALU.mult)
                P.act(h[:, kc, 0:TP], tm[:, 0:TP], AF.Identity, scale=Am[:, kc, 0:1], bias=Bfn(kc)[:, 0:1])
                if last:
                    v = s3(tm[:, TP:W])
                    P.tt(v, v, bc(Am[:, kc, 1:17].unsqueeze(2), [128, 16, 4]), ALU.mult)
                    P.tt(s3(h[:, kc, TP:W]), v, bc(Bfn(kc)[:, 1:17].unsqueeze(2), [128, 16, 4]), ALU.add)

        def resid(ps, oc, gch):
            P.stt(x[:, oc, c0:c0 + TP], ps[:, 0:TP], modT[:, gch + oc, 0:1], x[:, oc, c0:c0 + TP], ALU.mult, ALU.add)
            if last:
                tmv = s3(rb[:, 0:NS])
                P.tt(tmv, s3(ps[:, TP:W]), bc(modT[:, gch + oc, 1:17].unsqueeze(2), [128, 16, 4]), ALU.mult)
                P.tt(s3(x[:, oc, T:TOT]), s3(x[:, oc, T:TOT]), tmv, ALU.add)

        def gate_block(br, qtr, perm_b=False):
            gc0 = 1792 + br * 1024 + qtr * 256

            def issue(buf):
                P.dma('pool', buf[:, 0:2048].rearrange("p (k m) -> p k m", k=8),
                      w_in[l][:, gc0:gc0 + 256].rearrange("(k p) m -> p k m", p=128))
                dst = buf[:, 2048:3072].rearrange("p (k m) -> p k m", k=4)
                if not perm_b:
                    P.dma('pool', dst, wbr[br][l][:, qtr * 256:(qtr + 1) * 256].rearrange("(k p) m -> p k m", p=128))
                else:
                    for g_ in range(2):
                        P.dma('pool', dst[g_ * 64:(g_ + 1) * 64],
                              wbr[br][l][g_ * 256:(g_ + 1) * 256, qtr * 256:(qtr + 1) * 256].rearrange("(c d) m -> d c m", d=64))
            return issue

        def merge(br, src):
            for qtr in range(4):
                buf = W_.get(gate_block(br, qtr, perm_b=(br == 1)))
                gt = buf[:, 0:2048].rearrange("p (k m) -> p k m", k=8)
                bt = buf[:, 2048:3072].rearrange("p (k m) -> p k m", k=4)
                for o in range(2):
                    oc = qtr * 2 + o
                    psB = nextps(); psG = nextps()
                    lin(psB, lambda kc: bt[:, kc, o * 128:(o + 1) * 128], range(4), lambda kc, s0, sn: src[:, kc, s0:s0 + sn])
                    lin(psG, lambda kc: gt[:, kc, o * 128:(o + 1) * 128], range(8), lambda kc, s0, sn: h[:, kc, s0:s0 + sn])
                    P.act(sig[:, oc % 2, 0:W], psG[:, 0:W], AF.Sigmoid)
                    if br == 0:
                        P.tt(merged[:, oc, 0:W], sig[:, oc % 2, 0:W], psB[:, 0:W], ALU.mult)
                    else:
                        P.tt(mtmp[:, oc % 2, 0:W], sig[:, oc % 2, 0:W], psB[:, 0:W], ALU.mult)
                        dst = mb if br == 2 else merged
                        P.tt(dst[:, oc, 0:W], merged[:, oc, 0:W], mtmp[:, oc % 2, 0:W], ALU.add)

        norm(A1, lambda kc: modT[:, kc, :])
        if dbg and l == 0 and t == 3:
            dump('h', h[:], [128, KC, WMAX], BF16)

        cfg['_stopfn']('n1')
        LS('n1')
        wt = wv(W_.get(wload_std(w_in[l][:, 0:512], 8, 512)), 8, 512)
        if t == 0:
            P.memset(xa[:, :, 0:15], 0.0)
        else:
            P.copy(xa[:, :, 0:15], pcar[:])
        for oc in range(4):
            ps = nextps()
            lin(ps, lambda kc: wt[:, kc, oc * 128:(oc + 1) * 128], range(8), lambda kc, s0, sn: h[:, kc, s0:s0 + sn])
            P.copy(xa[:, oc, 15:15 + TP], ps[:, 0:TP], q='act')
            if last:
                P.copy(xes[:, oc, :, 15:19], s3(ps[:, TP:W]), q='act')
        P.copy(pcar[:], xa[:, :, TP:TP + 15])
        if last:
            for hh in range(2):
                P.dma('sp', sptok[0:120, hh, :], spool[l, hh * 8:(hh + 1) * 8].rearrange("b j f -> (b j) f"))
                for g_ in range(4):
                    P.mm(psX[:, g_ * 128:g_ * 128 + 120], sptok[0:120, hh, g_ * 128:(g_ + 1) * 128], ident[0:120, 0:120], True, True)
                for g_ in range(4):
                    P.copy(xes[:, g_, hh * 8:(hh + 1) * 8, 0:15],
                           psX[:, g_ * 128:g_ * 128 + 120].rearrange("p (b j) -> p b j", j=15), q='act')
        L_ = 15 + TP
        for g_ in range(4):
            w_ = 2 << g_
            cur = xa[:, g_, :]
            lo = 0
            for si, step in enumerate([1, 2, 4, 8][:g_ + 1]):
                nxt = scr[:, si % 2, :]
                P.tt(nxt[:, lo + step:L_], cur[:, lo + step:L_], cur[:, lo:L_ - step], ALU.add)
                cur = nxt
                lo += step
            P.stt(dbuf[:, g_, 0:TP], cur[:, 15:L_], 1.0 / w_, xa[:, g_, 15:L_], ALU.mult, ALU.subtract)
            if t == 0:
                P.tt(rb[:, 0:16], cur[:, 15:31], invc[:, g_, :], ALU.mult)
                P.tt(dbuf[:, g_, 0:16], rb[:, 0:16], xa[:, g_, 15:31], ALU.subtract)
            if last:
                cur = xes[:, g_, :, :]
                lo = 0
                for si, step in enumerate([1, 2, 4, 8][:g_ + 1]):
                    nxt = scrs[:, si % 2, :, :]
                    P.tt(nxt[:, :, lo + step:19], cur[:, :, lo + step:19], cur[:, :, lo:19 - step], ALU.add)
                    cur = nxt
                    lo += step
                P.stt(s3(dbuf[:, g_, TP:W]), cur[:, :, 15:19], 1.0 / w_, xes[:, g_, :, 15:19], ALU.mult, ALU.subtract)
        for g_ in range(4):
            ps = nextps()
            lin(ps, lambda kc: pw[:, g_, :], [0], lambda kc, s0, sn: dbuf[:, g_, s0:s0 + sn])
            P.act(ya[:, g_, 0:W], ps[:, 0:W], AF.Identity, scale=v_psc[:, g_:g_ + 1])
        if last:
            for g_ in range(4):
                P.mm(psX[0:15, g_ * 128:(g_ + 1) * 128], xa[:, g_, TP:TP + 15], ident[:], True, True)
            P.copy(otok[0:15, :], psX[0:15, 0:512], q='act')
            P.dma('sp', npool_p[l], otok[0:15, :], is_out=True)
            P.dma('sp', npool_s[l, :, 0:11, :], spool[l, :, 4:15, :], is_out=True)
            for g_ in range(4):
                P.copy(xsn[:, g_, :].rearrange("p (b t) -> p b t", t=4), xes[:, g_, :, 15:19])
                P.mm(psX[0:64, 512 + g_ * 128:512 + (g_ + 1) * 128], xsn[:, g_, :], ident[:], True, True)
            P.copy(otok[0:64, :], psX[0:64, 512:1024], q='act')
            for b in range(NSEQ):
                P.dma('sp', npool_s[l, b, 11:15, :], otok[b * 4:(b + 1) * 4, :], is_out=True)
        merge(0, ya)
        if dbg and l == 0 and t == 3:
            dump('ya', ya[:], [128, 4, WMAX], BF16)
            dump('mergedA', merged[:], [128, KC, WMAX])

        cfg['_stopfn']('pool')
        LS('pool')
        P.dma('sp', cosT[:, 0:W], k_rope[0][:, c0:c0 + W])
        P.dma('sp', sinT[:, 0:W], k_rope[1][:, c0:c0 + W])

        def issue_q(buf):
            dst = buf[:, 0:4096].rearrange("p (k c g d) -> p k c g d", k=8, c=4, g=2)
            for g_ in range(2):
                for c_ in range(4):
                    cq = 512 + g_ * 256 + c_ * 64
                    P.dma('pool', dst[:, :, c_, g_, :], w_in[l][:, cq:cq + 64].rearrange("(k p) d -> p k d", p=128))
        wt = wv(W_.get(issue_q), 8, 512)

        def rope(ps, dst_ap, dst32=None):
            P.copy(q32[:, 0:W], ps[:, 0:W], q='act')
            psr = nextps()
            for (s0, sn) in segs:
                P.mm(psr[:, s0:s0 + sn], rotm[:], q32[:, s0:s0 + sn], True, True)
            P.tt(rt1[:, 0:W], q32[:, 0:W], cosT[:, 0:W], ALU.mult)
            P.tt(rt2[:, 0:W], psr[:, 0:W], sinT[:, 0:W], ALU.mult)
            if dst32 is None:
                P.tt(dst_ap, rt1[:, 0:W], rt2[:, 0:W], ALU.add)
            else:
                P.tt(dst32, rt1[:, 0:W], rt2[:, 0:W], ALU.add)
                P.copy(dst_ap, dst32, q='act')
        for c_ in range(4):
            ps = nextps()
            lin(ps, lambda kc: wt[:, kc, c_ * 128:(c_ + 1) * 128], range(8), lambda kc, s0, sn: h[:, kc, s0:s0 + sn])
            rope(ps, qT[:, c_, 0:W])
        wt = wv(W_.get(wload_std(w_in[l][:, 1024:1280], 8, 256)), 8, 256)
        ps = nextps()
        lin(ps, lambda kc: wt[:, kc, 0:128], range(8), lambda kc, s0, sn: h[:, kc, s0:s0 + sn])
        rope(ps, kT[:, 128:128 + W], dst32=kr32[:, 0:W])
        P.copy(kT[:, 0:128], kcar[:])
        P.copy(vtok[:, 0, :], vcar[:])
        psv = nextps()
        for bi in range(4):
            for kc in range(KC):
                P.mm(psv[:, bi * 128:(bi + 1) * 128], h[:, kc, bi * 128:(bi + 1) * 128], wt[:, kc, 128:256], kc == 0, kc == KC - 1)
        P.copy(vtok[:, 1:5, :], psv[:, 0:512].rearrange("p (b f) -> p b f", b=4), q='act')
        if last:
            P.copy(otok[:, 0:128], psv[:, 384:512])
            P.dma('sp', nv_p[l], otok[:, 0:128], is_out=True)
            P.mm(psX[:, 0:128], kr32[:, 384:512], ident[:], True, True)
            P.copy(otok[:, 128:256], psX[:, 0:128])
            P.dma('sp', nk_p[l], otok[:, 128:256], is_out=True)
            for kc in range(KC):
                P.mm(psX[0:NS, 256:384], h[:, kc, TP:W], wt[:, kc, 128:256], kc == 0, kc == KC - 1)
            P.copy(vnew32[:], psX[0:NS, 256:384], q='act')
            P.copy(vnew[:], vnew32[:])
            for b in range(NSEQ):
                P.dma('sp', nv_s[l, b, 124:128, :], vnew32[b * 4:(b + 1) * 4, :], is_out=True)
            P.dma('sp', nv_s[l, :, 0:124, :], cv[l, :, 4:128, :], is_out=True)
            P.dma('sp', nk_s[l, :, 0:124, :], ck[l, :, 4:128, :], is_out=True)
            P.mm(psX[0:NS, 384:512], kr32[:, TP:W], ident[:], True, True)
            P.copy(otok[0:NS, 256:384], psX[0:NS, 384:512])
            for b in range(NSEQ):
                P.dma('sp', nk_s[l, b, 124:128, :], otok[b * 4:(b + 1) * 4, 256:384], is_out=True)
            P.dma('pool', cv_tok[:], cv[l].rearrange("b k f -> k b f"))
            for hh in range(2):
                P.dma('pool', ck_tok[:], ck[l, hh * 8:(hh + 1) * 8].rearrange("b k f -> k b f"))
                pst_ = PS[3][:, 512:1024].bitcast(BF16)
                for b8 in range(8):
                    P.tr(pst_[:, b8 * 128:(b8 + 1) * 128], ck_tok[:, b8, :], identb[:])
                P.copy(kcT[:, hh * 8:(hh + 1) * 8, :], pst_[:, :].rearrange("p (b f) -> p b f", b=8))
        P.copy(kcar[:], kT[:, TP:TP + 128])
        P.copy(vcar[:], vtok[:, 4, :])
        LS('attnin')

        def attn_unit(u, nq, g_, q_fn, k_parts, mask, nk, norm_fn, v_parts, out_fn):
            psS = PS[u]
            psTO = PS[2 + u]
            psT_ = psTO[:, 0:512].bitcast(BF16)
            psO = psTO[:, 512:1024]
            sm = smal[0:nq, u, :, :]
            sk = sinkb[0:nq, g_ * 4:(g_ + 1) * 4]
            S3 = psS[0:nq, :].rearrange("p (c k) -> p c k", c=4)[:, :, 0:nk]
            rows0 = v_parts[0][0]
            steps = []

            def s_qk():
                for c_ in range(4):
                    for ki, (k0, kn, kap) in enumerate(k_parts):
                        P.mm(psS[0:nq, c_ * 256 + k0:c_ * 256 + k0 + kn], q_fn(c_), kap, ki == 0, False)
                    P.mm(psS[0:nq, c_ * 256:c_ * 256 + nk], identb[0:nq, 0:nq], mask[0:nq, 0:nk], False, True)
            steps.append(s_qk)
            steps.append(lambda: P.rmax(sm[:, 0, :], S3))
            steps.append(lambda: P.ts(sm[:, 0, :], sm[:, 0, :], 0.125, None, ALU.mult))
            steps.append(lambda: P.tt(sm[:, 1, :], sm[:, 0, :], sk, ALU.max))
            steps.append(lambda: P.ts(sm[:, 2, :], sm[:, 1, :], -1.0, None, ALU.mult))
            steps.append(lambda: P.tt(sm[:, 4, :], sk, sm[:, 1, :], ALU.subtract))

            def s_exp():
                for c_ in range(4):
                    P.act(pbuf[0:nq, u, c_, 0:nk], psS[0:nq, c_ * 256:c_ * 256 + nk], AF.Exp, scale=0.125,
                          bias=sm[:, 2, c_:c_ + 1], accum_out=sm[:, 3, c_:c_ + 1])
                P.act(sm[:, 4, :], sm[:, 4, :], AF.Exp)
            steps.append(s_exp)
            steps.append(lambda: P.tt(sm[:, 5, :], sm[:, 3, :], sm[:, 4, :], ALU.add))
            steps.append(lambda: P.recip(sm[:, 6, :], sm[:, 5, :]))
            steps.append(lambda: norm_fn(u, sm[:, 6, :]))

            def s_tr():
                for c_ in range(4):
                    for vi, (rows, vap, src_fn) in enumerate(v_parts):
                        P.tr(psT_[0:rows, (c_ * 2 + vi) * 128:(c_ * 2 + vi) * 128 + nq], src_fn(u, c_), identb[0:nq, 0:nq])
            steps.append(s_tr)
            steps.append(lambda: P.copy(pT[0:rows0, u, :, 0:nq], psT_[0:rows0, :].rearrange("p (s q) -> p s q", s=8)[:, :, 0:nq], q='act'))

            def s_pv():
                for c_ in range(4):
                    for vi, (rows, vap, src_fn) in enumerate(v_parts):
                        P.mm(psO[:, c_ * 128:c_ * 128 + nq], vap, pT[0:rows, u, c_ * 2 + vi, 0:nq], vi == 0, vi == len(v_parts) - 1)
            steps.append(s_pv)
            steps.append(lambda: out_fn(psO))
            return steps

        def run_pair(ua, ub_):
            for fa, fb in zip(ua, ub_):
                fa()
                fb()

        for bi in range(4):
            gb = t * 4 + bi
            units = []
            for g_ in range(2):
                gs = slice(g_ * 64, (g_ + 1) * 64)

                def outp(psO, bi=bi, gs=gs):
                    P.copy(yb[gs, :, bi * 128:(bi + 1) * 128], psO[gs, :].rearrange("p (c q) -> p c q", c=4), q='act')

                def normp(u, rinv):
                    P.tt(pn[:, u, :, :], pbuf[:, u, :, :], bc(rinv.unsqueeze(2), [128, 4, 256]), ALU.mult)
                units.append(attn_unit(g_, 128, g_, lambda c_, bi=bi, gs=gs: qT[gs, c_, bi * 128:(bi + 1) * 128],
                                       [(0, 256, kT[gs, bi * 128:bi * 128 + 256])],
                                       maskB if gb == 0 else maskA, 256, normp,
                                       [(128, vtok[:, bi, :], lambda u, c_: pn[:, u, c_, 0:128]),
                                        (128, vtok[:, bi + 1, :], lambda u, c_: pn[:, u, c_, 128:256])], outp))
            run_pair(units[0], units[1])
        LS('attnp')
        if last:
            for b in range(NSEQ):
                units = []
                for g_ in range(2):
                    gs = slice(g_ * 64, (g_ + 1) * 64)
                    cs_ = slice(TP + 4 * b, TP + 4 * b + 4)

                    def outp(psO, gs=gs, cs_=cs_):
                        P.copy(yb[gs, :, cs_], psO[gs, :].rearrange("p (c q) -> p c q", c=4)[:, :, 0:4], q='act')

                    def norms(u, rinv, b=b):
                        P.tt(pn[0:4, u, :, 0:128], pbuf[0:4, u, :, 0:128], bc(rinv.unsqueeze(2), [4, 4, 128]), ALU.mult)
                        P.memset(pz[0:4, u, :, :], 0.0, q='dve')
                        P.tt(pz[0:4, u, :, 4 * b:4 * b + 4], pbuf[0:4, u, :, 128:132], bc(rinv.unsqueeze(2), [4, 4, 4]), ALU.mult)
                    units.append(attn_unit(g_, 4, g_, lambda c_, gs=gs, cs_=cs_: qT[gs, c_, cs_],
                                           [(0, 128, kcT[gs, b, :]), (128, 4, kT[gs, 128 + TP + 4 * b:128 + TP + 4 * b + 4])],
                                           maskA, 132, norms,
                                           [(128, cv_tok[:, b, :], lambda u, c_: pn[0:4, u, c_, 0:128]),
                                            (64, vnew[:, :], lambda u, c_: pz[0:4, u, c_, :])], outp))
                run_pair(units[0], units[1])
        if dbg and l == 0 and t == 3:
            dump('yb', yb[:], [128, 4, WMAX], BF16)
            dump('qT', qT[:], [128, 4, WMAX], BF16)
        merge(1, yb)

        cfg['_stopfn']('attn')
        LS('attn')
        wt = wv(W_.get(wload_std(w_in[l][:, 1280:1792], 8, 512)), 8, 512)
        cfg['_stopfn']('ssm_a')
        for j in range(4):
            ps = nextps()
            lin(ps, lambda kc: wt[:, kc, j * 128:(j + 1) * 128], range(8), lambda kc, s0, sn: h[:, kc, s0:s0 + sn])
            P.copy(u32[:, j, 0:W], ps[:, 0:W], q='act')
            P.copy(ub[:, j, 0:W], ps[:, 0:W], q='act')
        cfg['_stopfn']('ssm_b')
        LS('s0')
        P.ts(ctmp[:], c_g64[:], float(8 * t), None, ALU.mult)
        P.copy(ctmpi[:], ctmp[:])
        P.tt(c_ct[:], ctmp[:], ctmpi[:], ALU.subtract)
        cfg['_stopfn']('ssm_u')
        psYs = None
        if last:
            psdr = nextps(); psdi = nextps()
            for hf in range(2):
                P.ts(ubs[:, hf, :, :], ub[:, :, TP:W], sel[:, 4 + hf:5 + hf], None, ALU.mult)
            for sc in range(16):
                j, m = sc // 4, sc % 4
                P.mm(psdr[:, sc * NS:(sc + 1) * NS], BT_re[:, j, m % 2, :], ubs[:, m // 2, j, :], True, True)
                P.mm(psdi[:, sc * NS:(sc + 1) * NS], BT_im[:, j, m % 2, :], ubs[:, m // 2, j, :], True, True)
            LS('s0b')
            P.copy(dsr[:].rearrange("p a b c -> p (a b c)"), psdr[:, :], q='act')
            LS('s0c')
            P.copy(dsi[:].rearrange("p a b c -> p (a b c)"), psdi[:, :])
            LS('s1')
            for (src_, dstq) in ((sre, hq_re), (sim, hq_im)):
                for r4 in range(4):
                    P.dma('sp', htok[:], src_[l][:, r4 * 512:(r4 + 1) * 512])
                    for s_ in range(4):
                        sc = r4 * 4 + s_
                        P.mm(psX[:, sc * 16:(sc + 1) * 16], htok[0:16, s_ * 128:(s_ + 1) * 128], ident[0:16, 0:16], True, True)
                P.copy(dstq[:, :, :, 0], psX[:, 0:256].rearrange("p (a b) -> p a b", a=16), q='act')
            LS('s2')
            ar_b = bc(c_abr[:].unsqueeze(2), [128, 16, NSEQ])
            ai_b = bc(c_abi[:].unsqueeze(2), [128, 16, NSEQ])
            for tt_ in range(4):
                pr = hq_re[:, :, :, tt_]; pi_ = hq_im[:, :, :, tt_]
                P.tt(st1[:], ar_b, pr, ALU.mult)
                P.tt(st2[:], ai_b, pi_, ALU.mult)
                P.tt(st1[:], st1[:], st2[:], ALU.subtract)
                P.tt(hq_re[:, :, :, tt_ + 1], st1[:], dsr[:, :, :, tt_], ALU.add)
                P.tt(st1[:], ar_b, pi_, ALU.mult)
                P.tt(st2[:], ai_b, pr, ALU.mult)
                P.tt(st1[:], st1[:], st2[:], ALU.add)
                P.tt(hq_im[:, :, :, tt_ + 1], st1[:], dsi[:, :, :, tt_], ALU.add)
            LS('s3')
            P.copy(hsb_re[:].rearrange("p a (b t) -> p a b t", t=4), hq_re[:, :, :, 1:5])
            P.copy(hsb_im[:].rearrange("p a (b t) -> p a b t", t=4), hq_im[:, :, :, 1:5])
            LS('s4')
            for (srcq, dsto) in ((hq_re, nre_s), (hq_im, nim_s)):
                for r4 in range(4):
                    for s_ in range(4):
                        sc = r4 * 4 + s_
                        P.mm(psX[0:16, s_ * 128:(s_ + 1) * 128], srcq[:, sc, :, 4], ident[:], True, True)
                    P.copy(htok[0:16, :], psX[0:16, 0:512], q='act')
                    P.dma('sp', dsto[l][:, r4 * 512:(r4 + 1) * 512], htok[:], is_out=True)
        LS('ssms')
        for j in range(4):
            psY = PS[3]

            def chain(m, ci):
                sc = 4 * j + m
                u = ci
                hs_ = slice((m // 2) * 64, (m // 2) * 64 + 64)
                A_, B_, C_, D_ = pl[4 * ci][:, :], pl[4 * ci + 1][:, :], pl[4 * ci + 2][:, :], pl[4 * ci + 3][:, :]
                pi_ = pli[:, ci, :]
                sn_ = tb_s[:, u, :]; cs_ = tb_c[:, u, :]
                st = {}
                rho_b = bc(c_rho[:, sc:sc + 1], [128, TP])
                steps = []

                def s_drive():
                    st['psD'] = nextps()
                    P.mm(st['psD'][:, 0:TP], BT_re[hs_, j, m % 2, :], ub[hs_, j, 0:TP], True, True)
                    P.mm(st['psD'][:, TP:2 * TP], BT_im[hs_, j, m % 2, :], ub[hs_, j, 0:TP], True, True)
                steps.append(s_drive)
                steps.append(lambda: P.act(A_, thl[:, 1, :], AF.Identity, scale=c_ft[:, sc:sc + 1], bias=c_ct[:, sc:sc + 1]))
                steps.append(lambda: P.stt(A_, thl[:, 0, :], c_g64[:, sc:sc + 1], A_, ALU.mult, ALU.add))
                steps.append(lambda: P.copy(pi_, A_))
                steps.append(lambda: P.tt(A_, A_, pi_, ALU.subtract))
                steps.append(lambda: P.stt(B_, A_, -1.0, A_, ALU.mult, ALU.max))

                def s_sin():
                    P.act(sn_, A_, AF.Sin, scale=TWO_PI_S)
                    P.act(cs_, B_, AF.Sin, scale=-TWO_PI_S, bias=halfpi[:, 0:1])
                steps.append(s_sin)
                dre = lambda: st['psD'][:, 0:TP]
                dim_ = lambda: st['psD'][:, TP:2 * TP]
                steps.append(lambda: P.tt(A_, dre(), cs_, ALU.mult))
                steps.append(lambda: P.tt(B_, dim_(), sn_, ALU.mult))
                steps.append(lambda: P.tt(A_, A_, B_, ALU.add))
                steps.append(lambda: P.tt(B_, dim_(), cs_, ALU.mult))
                steps.append(lambda: P.tt(C_, dre(), sn_, ALU.mult))
                steps.append(lambda: P.tt(B_, B_, C_, ALU.subtract))
                steps.append(lambda: P.scan(C_, rho_b, A_, gcar[:, sc, 0:1]))
                steps.append(lambda: P.scan(D_, rho_b, B_, gcar[:, sc, 1:2]))

                def s_carry():
                    P.copy(gcar[:, sc, 0:1], C_[:, TP - 1:TP], q='act')
                    P.copy(gcar[:, sc, 1:2], D_[:, TP - 1:TP], q='act')
                steps.append(s_carry)
                steps.append(lambda: P.tt(A_, cs_, C_, ALU.mult))
                steps.append(lambda: P.tt(B_, sn_, D_, ALU.mult))

                def s_hre():
                    P.tt(hre[:, u, :], A_, B_, ALU.subtract)
                    if last:
                        P.tt(hl[:, sc, 0:1], A_[:, TP - 1:TP], B_[:, TP - 1:TP], ALU.subtract)
                steps.append(s_hre)
                steps.append(lambda: P.tt(A_, cs_, D_, ALU.mult))
                steps.append(lambda: P.tt(B_, sn_, C_, ALU.mult))

                def s_him():
                    P.tt(him[:, u, :], A_, B_, ALU.add)
                    if last:
                        P.tt(hl[:, sc, 1:2], A_[:, TP - 1:TP], B_[:, TP - 1:TP], ALU.add)
                steps.append(s_him)

                def s_cmm():
                    P.mm(psY[:, 0:TP], CT_re[:, sc, :], hre[:, u, :], m == 0, False)
                    P.mm(psY[:, 0:TP], CT_imn[:, sc, :], him[:, u, :], False, m == 3)
                    if last:
                        P.mm(psY[:, TP:W], CT_re[:, sc, :], hsb_re[:, sc, :], m == 0, False)
                        P.mm(psY[:, TP:W], CT_imn[:, sc, :], hsb_im[:, sc, :], False, m == 3)
                steps.append(s_cmm)
                return steps
            for pr in range(2):
                ca = chain(2 * pr, 0)
                cb = chain(2 * pr + 1, 1)
                for fa, fb in zip(ca, cb):
                    fa()
                    fb()
            cfg['_stopfn']('ssm_post')
            Y1 = ysm[:, 0, 0:W]; Y2 = ysm[:, 1, 0:W]; SG = ysm[:, 2, 0:W]
            P.stt(Y1, u32[:, j, 0:W], v_ssmd[:, j:j + 1], psY[:, 0:W], ALU.mult, ALU.add)
            P.act(Y2, Y1, AF.Square, scale=0.2114592159259085)
            P.stt(Y2, Y2, 1.0, Y1, ALU.add, ALU.mult)
            P.act(SG, Y2, AF.Sigmoid, scale=1.5957691216)
            P.tt(yg[:, j, 0:W], Y1, SG, ALU.mult)
        if last:
            for (cidx, dsto) in ((0, nre_p), (1, nim_p)):
                P.mm(psX[0:16, 0:128], hl[:, :, cidx], ident[:], True, True)
                P.copy(otok[0:16, cidx * 128:(cidx + 1) * 128], psX[0:16, 0:128])
                P.dma('sp', dsto[l], otok[0:16, cidx * 128:(cidx + 1) * 128], is_out=True)
        wt = wv(W_.get(wload_std(w_glu[l], 4, 512)), 4, 512)
        for oc in range(4):
            ps = nextps()
            lin(ps, lambda kc: wt[:, kc, oc * 128:(oc + 1) * 128], range(4), lambda kc, s0, sn: yg[:, kc, s0:s0 + sn])
            P.act(ysm[:, oc % 2, 0:W], ps[:, 0:W], AF.Sigmoid)
            P.tt(yc[:, oc, 0:W], yg[:, oc, 0:W], ysm[:, oc % 2, 0:W], ALU.mult)
        if dbg and l == 0 and t == 3:
            dump('yc', yc[:], [128, 4, WMAX], BF16)
        merge(2, yc)

        cfg['_stopfn']('ssm')
        LS('ssm')
        for hb in range(2):
            wt = wv(W_.get(wload_std(w_out[l][:, hb * 512:(hb + 1) * 512], 8, 512)), 8, 512)
            for o in range(4):
                oc = hb * 4 + o
                ps = nextps()
                lin(ps, lambda kc: wt[:, kc, o * 128:(o + 1) * 128], range(8), lambda kc, s0, sn: mb[:, kc, s0:s0 + sn])
                resid(ps, oc, 16)
        cfg['_stopfn']('outp')
        norm(A2, lambda kc: modT[:, 24 + kc, :])
        for hp in range(HC // 2):
            def issue(buf, hp=hp):
                dst = buf[:, 0:4096].rearrange("p (k a m) -> p k a m", k=8, a=2)
                for a_ in range(2):
                    cw = a_ * DFF + hp * 256
                    P.dma('pool', dst[:, :, a_, :], w_ffn_in[l][:, cw:cw + 256].rearrange("(k p) n -> p k n", p=128))
            wt = W_.get(issue)[:, 0:4096].rearrange("p (k a m) -> p k a m", k=8, a=2)
            for o in range(2):
                hc = hp * 2 + o
                psA = nextps(); psB = nextps()
                lin(psA, lambda kc: wt[:, kc, 0, o * 128:(o + 1) * 128], range(8), lambda kc, s0, sn: h[:, kc, s0:s0 + sn])
                lin(psB, lambda kc: wt[:, kc, 1, o * 128:(o + 1) * 128], range(8), lambda kc, s0, sn: h[:, kc, s0:s0 + sn])
                P.act(sa[:, hc % 2, 0:W], psA[:, 0:W], AF.Silu)
                P.tt(hid[:, hc, 0:W], sa[:, hc % 2, 0:W], psB[:, 0:W], ALU.mult)
        for oc in range(KC):
            wt = wv(W_.get(wload_std(w_ffn_out[l][:, oc * 128:(oc + 1) * 128], HC, 128)), HC, 128)
            ps = nextps()
            lin(ps, lambda kc: wt[:, kc, :], range(HC), lambda kc, s0, sn: hid[:, kc, s0:s0 + sn])
            resid(ps, oc, 40)
        cfg['_stopfn']('ffn')
        LS('end')

    def final_tile(t):
        c0 = t * TP
        last = (t == NTILE - 1)
        W = WMAX if last else TP
        segs = [(0, TP)] + ([(TP, NS)] if last else [])
        for kc in range(KC):
            P.act(sq[:, kc % 2, 0:W], x[:, kc, c0:c0 + W], AF.Square)
            for (s0, sn) in segs:
                P.mm(psX[:, s0:s0 + sn], onesb[:], sq[:, kc % 2, s0:s0 + sn], kc == 0, kc == KC - 1)
        P.ts(rb[:, 0:W], psX[:, 0:W], 1.0 / D, EPS, ALU.mult, ALU.add)
        P.recip(rb[:, 0:W], rb[:, 0:W])
        P.act(rb[:, 0:W], rb[:, 0:W], AF.Sqrt)
        for kc in range(KC):
            P.stt(yf[:, kc, 0:W], x[:, kc, c0:c0 + W], v_fng[:, kc:kc + 1], rb[:, 0:W], ALU.mult, ALU.mult)
        nblk = 5 if last else 4
        for blk in range(nblk):
            nr = 128 if blk < 4 else NS
            ps = nextps()
            for kc in range(KC):
                P.mm(ps[0:nr, kc * 128:(kc + 1) * 128], yf[:, kc, blk * 128:blk * 128 + nr], ident[:], True, True)
            P.copy(iotok[0:nr, :], ps[0:nr, :], q='act')
            if blk < 4:
                P.dma('sp', y_p[c0 + blk * 128:c0 + (blk + 1) * 128, :], iotok[0:nr, :], is_out=True)
            else:
                P.dma('sp', y_s, iotok[0:nr, :], is_out=True)

    def stop(name):
        if cfg.get('stop') == name:
            raise StopBuild()
    cfg['_stopfn'] = stop
    for dry in (True, False):
        P.dry = dry
        psrr[0] = 0
        try:
            body()
        except StopBuild:
            pass
    return P


def _consts():
    ident = np.eye(128, dtype=np.float32)
    rotm = np.zeros((128, 128), np.float32)
    for d in range(128):
        dd = d % 64
        if dd < 8:
            rotm[d + 8, d] = -1.0
        elif dd < 16:
            rotm[d - 8, d] = 1.0
    qi = np.arange(128)[:, None]
    kj = np.arange(256)[None, :]
    diff = 128 + qi - kj
    band = (diff >= 0) & (diff < 128)
    mA = np.where(band, 0.0, -30000.0).astype(np.float32)
    mB = np.where(band & (kj >= 128), 0.0, -30000.0).astype(np.float32)
    mask = np.stack([mA, mB])
    pos = np.concatenate([np.arange(T), np.tile(PAST + np.arange(4), NSEQ)]).astype(np.float32)
    inv = (500000.0 ** (-np.arange(0, 16, 2, dtype=np.float32) / 16)).astype(np.float32)
    ang = pos[:, None] * inv[None, :]
    cos = np.cos(ang).astype(np.float32)
    sin = np.sin(ang).astype(np.float32)
    rope = np.zeros((2, 128, TOT), np.float32)
    rope[0] = 1.0
    for p in range(128):
        dd = p % 64
        if dd < 16:
            rope[0, p] = cos[:, dd % 8]
            rope[1, p] = sin[:, dd % 8]
    sel = np.zeros((128, 8), np.float32)
    for p in range(128):
        for mm_ in range(2):
            for gq in range(2):
                sel[p, mm_ * 2 + gq] = 1.0 if ((p % 64) // 32 == mm_ and (p % 32) // 16 == gq) else 0.0
        for gq in range(2):
            sel[p, 4 + gq] = 1.0 if p // 64 == gq else 0.0
            sel[p, 6 + gq] = -sel[p, 4 + gq]
    invc = np.zeros((128, 4, 16), np.float32)
    for g in range(4):
        w = 2 << g
        invc[:, g, :] = 1.0 / np.minimum(np.arange(16) + 1, w)
    tt_ = np.arange(512)
    thl = np.concatenate([(tt_ // 64), (tt_ % 64)]).astype(np.float32)[None, :]
    kprep = np.zeros((32, 800), np.float32)
    for g in range(32):
        if g % 2 == 0:
            kprep[g, g // 2] = 1.0
        else:
            kprep[g, 16 + g // 2] = 1.0
        kprep[g, 32:96] = 1.0
        kprep[g, 160 + 64:160 + 128] = 1.0
        j, mg = g // 8, g % 8
        kprep[g, 288 + j * 128 + mg * 16:288 + j * 128 + mg * 16 + 16] = 1.0
    return dict(k_prep=kprep, k_ident=ident, k_rotm=rotm, k_mask=mask, k_rope=rope, k_sel=sel,
                k_invc=invc.reshape(128, 64), k_thl=thl)


_WNAMES = ['norm1_g', 'norm2_g', 'w_ada', 'b_ada', 'w_in', 'pool_w', 'pool_scale', 'attn_sinks', 'ssm_a_re',
           'ssm_a_im', 'ssm_log_dt', 'ssm_b_re', 'ssm_b_im', 'ssm_c_re', 'ssm_c_im', 'ssm_d', 'w_glu',
           'w_branch_a', 'w_branch_b', 'w_branch_c', 'w_out', 'w_ffn_in', 'w_ffn_out', 'final_norm_g']


def make_in_maps(inputs, ncores=8):
    f = lambda a: np.ascontiguousarray(np.asarray(a, dtype=np.float32))
    shared = {n: f(inputs[n]) for n in _WNAMES}
    shared.update(_consts())
    maps = []
    for c in range(ncores):
        sl = slice(c * NSEQ, (c + 1) * NSEQ)
        m = dict(shared)
        m['xp'] = f(inputs['x_prompt'][c])
        m['xs'] = f(np.asarray(inputs['x_sample'])[sl].reshape(NS, D))
        m['ck'] = f(np.asarray(inputs['cache_win_k'])[:, sl].reshape(NL, NSEQ, 128, 128))
        m['cv'] = f(np.asarray(inputs['cache_win_v'])[:, sl].reshape(NL, NSEQ, 128, 128))
        m['spool'] = f(np.asarray(inputs['state_pool'])[:, sl])
        m['sre'] = f(np.asarray(inputs['state_ssm_re'])[:, sl].reshape(NL, NSEQ, 2048))
        m['sim'] = f(np.asarray(inputs['state_ssm_im'])[:, sl].reshape(NL, NSEQ, 2048))
        m['c17'] = f(np.concatenate([np.asarray(inputs['c_prompt'])[c:c + 1], np.asarray(inputs['c_sample'])[sl]], 0))
        maps.append(m)
    return maps


def assemble(results):
    R = results
    n = len(R)
    cat = lambda k, ax: np.concatenate([r[k] for r in R], axis=ax)
    y_prompt = np.stack([r['y_p'] for r in R])
    y_sample = cat('y_s', 0).reshape(n * NSEQ, 4, D)
    nk_p = np.stack([r['nk_p'] for r in R], 1).reshape(NL, n, 128, 2, 64)
    nv_p = np.stack([r['nv_p'] for r in R], 1).reshape(NL, n, 128, 2, 64)
    npool_p = np.stack([r['npool_p'] for r in R], 1)
    nre_p = np.stack([r['nre_p'] for r in R], 1).reshape(NL, n, 32, 64)
    nim_p = np.stack([r['nim_p'] for r in R], 1).reshape(NL, n, 32, 64)
    nk_s = cat('nk_s', 1).reshape(NL, n * NSEQ, 128, 2, 64)
    nv_s = cat('nv_s', 1).reshape(NL, n * NSEQ, 128, 2, 64)
    npool_s = cat('npool_s', 1)
    nre_s = cat('nre_s', 1).reshape(NL, n * NSEQ, 32, 64)
    nim_s = cat('nim_s', 1).reshape(NL, n * NSEQ, 32, 64)
    outs = (y_prompt, y_sample, nk_p, nv_p, npool_p, nre_p, nim_p, nk_s, nv_s, npool_s, nre_s, nim_s)
    return tuple(np.ascontiguousarray(o, dtype=np.float32) for o in outs)


def kernel(**inputs):
    from contextlib import ExitStack
    nc = bass.Bass("TRN2", target_bir_lowering=False)
    cfg = {}
    with ExitStack() as es:
        P = build(nc, cfg)
        P.finish(es)
    in_maps = make_in_maps(inputs, 8)
    res = run_bass_kernel_spmd(nc, in_maps, core_ids=list(range(8)))
    return assemble(res.results)
```

```python
import math
import numpy as np
import concourse.bass as bass
import concourse.mybir as mybir
from concourse.bass_utils import run_bass_kernel_spmd

F32 = mybir.dt.float32
BF16 = mybir.dt.bfloat16
I32 = mybir.dt.int32
AF = mybir.ActivationFunctionType
ALU = mybir.AluOpType
AX = mybir.AxisListType

NL = 4
D = 1024
KC = 8
T = 2048
TP = 512
NTILE = 4
NSEQ = 16
NS = 64
TOT = T + NS
WMAX = TP + NS
PAST = 8192
EPS = 1e-6
INC = 1792 + 3072
DFF = 2816
HC = 22
NSEM = 26
BLK = 64
TWO_PI_S = 6.28318


class Prog:
    QS = ('pe', 'act', 'dve', 'pool', 'sp')

    def __init__(self, nc):
        self.nc = nc
        self.dry = False
        self.ins = []
        self.qins = {q: [] for q in self.QS}
        self.last_w = {}
        self.rd_c = {}
        self.rd_d = {}
        self.dma_cnt = {}
        self.dma_last = {}
        self.sem_pool = {'pool': list(range(0, 14)), 'sp': list(range(14, NSEM))}
        self.out_dmas = []
        self.base = {}
        self.n_t = 0
        self.retired = {}
        self.dependents = {}
        self.dsem = None

    def sbt(self, name, shape, dt=F32):
        t = self.nc.alloc_sbuf_tensor(name, list(shape), dt)
        return t

    def sbt_at(self, name, shape, dt, off):
        self.n_t += 1
        return self.nc.alloc_sbuf_tensor_at("%s_%d" % (name, self.n_t), list(shape), dt, offset=off)

    def pst(self, name, shape, dt=F32):
        return self.nc.alloc_psum_tensor(name, list(shape), dt)

    def _ranges(self, ap):
        sp = str(ap.space)
        if 'DRAM' in sp:
            return None
        name = ap.tensor.name
        key = self.base.get(name)
        if key is None:
            m = self.nc.lookup_mloc(ap.tensor)
            if 'PSUM' in sp:
                key = ('P', m.bank * 2048 + m.addr)
            else:
                key = ('S', m.addr)
            self.base[name] = key
        spc, b0 = key
        es = mybir.dt.size(ap.dtype)
        dims = list(ap.ap)
        pstride = dims[0][0]
        off = ap.offset
        foff = off % pstride if pstride > 0 else off
        free = [(abs(s), n) for (s, n) in dims[1:] if n > 1 and s != 0]
        free.sort(reverse=True)
        out = []

        def rec(base, ds):
            if not ds:
                out.append((base, base + 1))
                return
            ext = sum(s * (n - 1) for s, n in ds) + 1
            if len(ds) == 1 or ext * es <= 2 * BLK or ds[0][1] > 64:
                out.append((base, base + ext))
                return
            s, n = ds[0]
            inner = sum(s2 * (n2 - 1) for s2, n2 in ds[1:]) + 1
            if inner >= s:
                out.append((base, base + ext))
                return
            for i in range(n):
                rec(base + i * s, ds[1:])
        rec(foff, free)
        blocks = set()
        blk = 2048 if spc == 'P' else BLK
        for lo, hi in out:
            a = (b0 + lo * es) // blk
            b = (b0 + hi * es - 1) // blk
            for k in range(a, b + 1):
                blocks.add((spc, k))
        return blocks

    def _add(self, q, fn, deps, dma=False):
        iid = len(self.ins)
        rec = dict(id=iid, q=q, fn=fn, dma=dma, signal=False, qidx=len(self.qins[q]))
        nd = []
        for d in set(deps):
            d = self.retired.get(d, d)
            dr = self.ins[d]
            if (not dma) and (not dr['dma']) and dr['q'] == q:
                if q == 'pe':
                    continue
            nd.append(d)
            if dr['dma']:
                self.dependents.setdefault(d, []).append(iid)
        rec['deps'] = nd
        self.ins.append(rec)
        self.qins[q].append(rec)
        return iid

    def op(self, q, fn, outs=(), ins=(), dma=False, out=False):
        if self.dry:
            return None
        rb = set()
        wb = set()
        for a in ins:
            if a is None or isinstance(a, (int, float)):
                continue
            r = self._ranges(a)
            if r:
                rb |= r
        for a in outs:
            r = self._ranges(a)
            if r:
                wb |= r
        pr = set(k for k in rb if k[0] == 'P')
        if pr:
            rb -= pr
            wb |= pr
        deps = set()
        for k in rb:
            w = self.last_w.get(k)
            if w is not None:
                deps.add(w)
        for k in wb:
            w = self.last_w.get(k)
            if w is not None:
                deps.add(w)
            rc = self.rd_c.get(k)
            if rc:
                deps.update(rc.values())
            rd = self.rd_d.get(k)
            if rd:
                deps.update(rd)
        sem = None
        if dma:
            pool_ = self.sem_pool[q]
            cnt = self.dma_cnt.get(q, 0)
            self.dma_cnt[q] = cnt + 1
            sem = pool_[cnt % len(pool_)]
            prev = self.dma_last.get(sem)
            if prev is not None:
                deps.add(prev)
        iid = self._add(q, fn, deps, dma=dma)
        rec = self.ins[iid]
        if dma:
            rec['sem'] = sem
            rec['semval'] = 16 * (cnt // len(pool_) + 1)
            self.dma_last[sem] = iid
            if out:
                self.out_dmas.append(iid)
        for k in rb:
            if dma:
                self.rd_d.setdefault(k, []).append(iid)
            else:
                self.rd_c.setdefault(k, {})[q] = iid
        for k in wb:
            self.last_w[k] = iid
            self.rd_c[k] = {}
            self.rd_d[k] = []
        return iid

    def finish(self, es):
        fdeps = [self.retired.get(d, d) for d in self.out_dmas]
        fin = dict(id=len(self.ins), q='sp', fn=None, dma=False, signal=False,
                   qidx=len(self.qins['sp']), deps=list(set(fdeps)))
        self.ins.append(fin)
        self.qins['sp'].append(fin)
        for r in self.ins:
            for d in r['deps']:
                self.ins[d]['signal'] = True
        for q in self.QS:
            c = 0
            for r in self.qins[q]:
                if (not r['dma']) and r['signal']:
                    c += 1
                r['cnt'] = c
        nc = self.nc
        csem = {q: es.enter_context(nc.semaphore('cs_' + q)) for q in self.QS}
        dsem = [es.enter_context(nc.semaphore('ds%d' % i)) for i in range(NSEM)]
        self.dsem = dsem
        block = es.enter_context(nc.Block())

        def emit(eng, q):
            waited = {}
            for r in self.qins[q]:
                need = {}
                for d in r['deps']:
                    dr = self.ins[d]
                    if dr['dma']:
                        key = ('d', dr['sem'])
                        val = dr['semval']
                        sem = dsem[dr['sem']]
                    else:
                        key = ('c', dr['q'])
                        val = dr['cnt']
                        sem = csem[dr['q']]
                    if need.get(key, (0, None))[0] < val:
                        need[key] = (val, sem)
                for key, (val, sem) in need.items():
                    if waited.get(key, 0) >= val:
                        continue
                    eng.wait_ge(sem, val)
                    waited[key] = val
                if r['fn'] is None:
                    continue
                bi = r['fn'](eng)
                if r['dma']:
                    bi.then_inc(dsem[r['sem']], 16)
                elif r['signal']:
                    bi.then_inc(csem[q], 1)
        block.tensor(lambda e: emit(e, 'pe'))
        block.scalar(lambda e: emit(e, 'act'))
        block.vector(lambda e: emit(e, 'dve'))
        block.gpsimd(lambda e: emit(e, 'pool'))
        block.sync(lambda e: emit(e, 'sp'))

    def dma(self, q, out, in_, is_out=False, **kw):
        return self.op(q, lambda e: e.dma_start(out=out, in_=in_, **kw), [out], [in_], dma=True, out=is_out)

    def mm(self, out, lhsT, rhs, start, stop):
        return self.op('pe', lambda e: e.matmul(out, lhsT, rhs, start=start, stop=stop), [out], [lhsT, rhs])

    def tr(self, out, in_, ident):
        return self.op('pe', lambda e: e.transpose(out, in_, ident), [out], [in_, ident])

    def act(self, out, in_, func, scale=1.0, bias=0.0, accum_out=None, q='act'):
        ins = [in_]
        outs = [out]
        if not isinstance(scale, (int, float)):
            ins.append(scale)
        if not isinstance(bias, (int, float)):
            ins.append(bias)
        kw = {}
        if accum_out is not None:
            outs.append(accum_out)
            kw['accum_out'] = accum_out
        return self.op(q, lambda e: e.activation(out, in_, func, bias=bias, scale=scale, **kw), outs, ins)

    def tt(self, out, a, b, op, q='dve'):
        return self.op(q, lambda e: e.tensor_tensor(out, a, b, op), [out], [a, b])

    def ts(self, out, a, s1, s2, op0, op1=None, q='dve'):
        ins = [a]
        if not isinstance(s1, (int, float)):
            ins.append(s1)
        if s2 is not None and not isinstance(s2, (int, float)):
            ins.append(s2)
        if op1 is None:
            return self.op(q, lambda e: e.tensor_scalar(out, a, s1, None, op0), [out], ins)
        return self.op(q, lambda e: e.tensor_scalar(out, a, s1, s2, op0, op1), [out], ins)

    def stt(self, out, in0, scalar, in1, op0, op1, q='dve'):
        ins = [in0, in1]
        if not isinstance(scalar, (int, float)):
            ins.append(scalar)
        return self.op(q, lambda e: e.scalar_tensor_tensor(out, in0, scalar, in1, op0, op1), [out], ins)

    def copy(self, out, in_, q='dve'):
        if q == 'act':
            return self.op(q, lambda e: e.copy(out, in_), [out], [in_])
        return self.op(q, lambda e: e.tensor_copy(out, in_), [out], [in_])

    def memset(self, ap, v, q='pool'):
        return self.op(q, lambda e: e.memset(ap, v), [ap], [])

    def recip(self, out, in_):
        return self.op('dve', lambda e: e.reciprocal(out, in_), [out], [in_])

    def scan(self, out, d0, d1, init):
        ins = [d0, d1]
        if not isinstance(init, (int, float)):
            ins.append(init)
        return self.op('dve', lambda e: e.tensor_tensor_scan(out, d0, d1, init, ALU.mult, ALU.add), [out], ins)

    def rmax(self, out, in_):
        return self.op('dve', lambda e: e.tensor_reduce(out, in_, AX.X, ALU.max), [out], [in_])


class StopBuild(Exception):
    pass


class WStream:
    NB = 3
    LOOK = 2

    def __init__(self, P, bufs):
        self.P = P
        self.bufs = bufs
        self.plan = []
        self.i = 0
        self.issued = 0

    def get(self, issue):
        P = self.P
        if P.dry:
            self.plan.append(issue)
            return self.bufs[(len(self.plan) - 1) % self.NB]
        i = self.i
        while self.issued <= min(i + self.LOOK, len(self.plan) - 1):
            j = self.issued
            self.plan[j](self.bufs[j % self.NB])
            self.issued += 1
        self.i += 1
        return self.bufs[i % self.NB]


def bc(ap, shape):
    return ap.to_broadcast(list(shape))


def build(nc, cfg):
    P = Prog(nc)
    dbg = cfg.get('dbg', False)
    nlayers = cfg.get('nlayers', NL)

    def DI(name, shape, dt=F32):
        return nc.dram_tensor(name, list(shape), dt, kind="ExternalInput").ap()

    def DO(name, shape, dt=F32):
        return nc.dram_tensor(name, list(shape), dt, kind="ExternalOutput").ap()

    xp = DI('xp', [T, D]); xs = DI('xs', [NS, D])
    ck = DI('ck', [NL, NSEQ, 128, 128]); cv = DI('cv', [NL, NSEQ, 128, 128])
    spool = DI('spool', [NL, NSEQ, 15, 512])
    sre = DI('sre', [NL, NSEQ, 2048]); sim = DI('sim', [NL, NSEQ, 2048])
    c17 = DI('c17', [17, D])
    n1g = DI('norm1_g', [NL, D]); n2g = DI('norm2_g', [NL, D])
    w_ada = DI('w_ada', [NL, D, 6 * D]); b_ada = DI('b_ada', [NL, 6 * D])
    w_in = DI('w_in', [NL, D, INC])
    pool_w = DI('pool_w', [NL, 4, 128, 128]); pool_scale = DI('pool_scale', [NL, 512])
    sinks = DI('attn_sinks', [NL, 8])
    a_re = DI('ssm_a_re', [NL, 32, 64]); a_im = DI('ssm_a_im', [NL, 32, 64]); log_dt = DI('ssm_log_dt', [NL, 32])
    b_re = DI('ssm_b_re', [NL, 32, 64, 16]); b_im = DI('ssm_b_im', [NL, 32, 64, 16])
    c_re = DI('ssm_c_re', [NL, 32, 16, 64]); c_im = DI('ssm_c_im', [NL, 32, 16, 64])
    ssm_d = DI('ssm_d', [NL, 512])
    w_glu = DI('w_glu', [NL, 512, 512])
    wbr = [DI('w_branch_a', [NL, 512, D]), DI('w_branch_b', [NL, 512, D]), DI('w_branch_c', [NL, 512, D])]
    w_out = DI('w_out', [NL, D, D])
    w_ffn_in = DI('w_ffn_in', [NL, D, 2 * DFF]); w_ffn_out = DI('w_ffn_out', [NL, DFF, D])
    fng = DI('final_norm_g', [D])
    k_ident = DI('k_ident', [128, 128]); k_rotm = DI('k_rotm', [128, 128])
    k_mask = DI('k_mask', [2, 128, 256]); k_rope = DI('k_rope', [2, 128, TOT])
    k_prep = DI('k_prep', [32, 800]); k_sel = DI('k_sel', [128, 8]); k_invc = DI('k_invc', [128, 64]); k_thl = DI('k_thl', [1, 1024])

    y_p = DO('y_p', [T, D]); y_s = DO('y_s', [NS, D])
    nk_p = DO('nk_p', [NL, 128, 128]); nv_p = DO('nv_p', [NL, 128, 128])
    npool_p = DO('npool_p', [NL, 15, 512])
    nre_p = DO('nre_p', [NL, 16, 128]); nim_p = DO('nim_p', [NL, 16, 128])
    nk_s = DO('nk_s', [NL, NSEQ, 128, 128]); nv_s = DO('nv_s', [NL, NSEQ, 128, 128])
    npool_s = DO('npool_s', [NL, NSEQ, 15, 512])
    nre_s = DO('nre_s', [NL, NSEQ, 2048]); nim_s = DO('nim_s', [NL, NSEQ, 2048])
    dbg_outs = {}

    def dump(name, ap, shape, dt=F32):
        if not dbg or P.dry:
            return
        o = DO('dbg_' + name, shape, dt)
        P.dma('sp', o, ap, is_out=True)

    x = P.sbt('x', [128, KC, TOT])
    wbufs = [P.sbt('wbuf%d' % i, [128, 4096], BF16) for i in range(3)]
    modT = P.sbt('modT', [128, 48, 17]); A1 = P.sbt('A1', [128, 8, 17]); A2 = P.sbt('A2', [128, 8, 17])
    scT = P.sbt('scT', [128, 8, 17], BF16)
    vecs = P.sbt('vecs', [128, 72]); v_fng = P.sbt('v_fng', [128, 8])
    v_n1g = vecs[:, 0:8]; v_n2g = vecs[:, 8:16]; v_bada = vecs[:, 16:64]; v_psc = vecs[:, 64:68]; v_ssmd = vecs[:, 68:72]
    BT_re = P.sbt('BT_re', [128, 4, 2, 128], BF16); BT_im = P.sbt('BT_im', [128, 4, 2, 128], BF16)
    CT_re = P.sbt('CT_re', [128, 16, 128], BF16); CT_imn = P.sbt('CT_imn', [128, 16, 128], BF16)
    c_rho = P.sbt('c_rho', [128, 16]); c_ft = P.sbt('c_ft', [128, 16]); c_g64 = P.sbt('c_g64', [128, 16])
    c_abr = P.sbt('c_abr', [128, 16]); c_abi = P.sbt('c_abi', [128, 16]); c_ct = P.sbt('c_ct', [128, 16])
    pw = P.sbt('pw', [128, 4, 128], BF16)
    kcar = P.sbt('kcar', [128, 128], BF16); vcar = P.sbt('vcar', [128, 128], BF16)
    pcar = P.sbt('pcar', [128, 4, 15]); gcar = P.sbt('gcar', [128, 16, 2])
    ident = P.sbt('ident', [128, 128]); identb = P.sbt('identb', [128, 128], BF16)
    rotm = P.sbt('rotm', [128, 128]); onesb = P.sbt('onesb', [128, 128], BF16)
    maskA = P.sbt('maskA', [128, 256], BF16); maskB = P.sbt('maskB', [128, 256], BF16)
    sel = P.sbt('sel', [128, 8]); invc = P.sbt('invc', [128, 4, 16]); sinkb = P.sbt('sinkb', [128, 8])
    thl = P.sbt('thl', [128, 2, 512], BF16)
    h = P.sbt('h', [128, KC, WMAX], BF16)
    merged = P.sbt('merged', [128, KC, WMAX])
    otok = P.sbt('otok', [128, 512])
    halfpi = P.sbt('halfpi', [128, 1]); ctmp = P.sbt('ctmp', [128, 16]); ctmpi = P.sbt('ctmpi', [128, 16], I32)
    arena_sz = (nc.sbuf_bytes_remaining - 64) // 64 * 64
    arena = P.sbt('arena', [128, arena_sz // 4])
    abase = nc.lookup_mloc(arena).addr
    cfg['arena'] = arena_sz

    class Ar:
        def __init__(self):
            self.off = 0

        def a(self, name, shape, dt=F32):
            n = 1
            for s_ in shape[1:]:
                n *= s_
            nb = (n * mybir.dt.size(dt) + 63) // 64 * 64
            assert self.off + nb <= arena_sz, (name, self.off, nb, arena_sz)
            t = P.sbt_at(name, shape, dt, abase + self.off)
            self.off += nb
            return t

    PS = [P.pst('ps%d' % i, [128, 1024]) for i in range(4)]
    psrr = [0]
    ps1rr = [0]

    def nextps():
        psrr[0] = (psrr[0] + 1) % 3
        return PS[psrr[0]]
    psX = PS[3]

    W_ = WStream(P, wbufs)

    def wv(buf, kc, n):
        return buf[:, 0:kc * n].rearrange("p (k m) -> p k m", k=kc)

    def wload_std(src, kc, n):
        def issue(buf):
            P.dma('pool', wv(buf, kc, n), src.rearrange("(k p) m -> p k m", p=128))
        return issue

    ar = Ar()
    sq = ar.a('sq', [128, 2, WMAX], BF16); rb = ar.a('rb', [128, WMAX]); ntmp = ar.a('ntmp', [128, 2, WMAX])
    norm_end = ar.off
    ar.off = 0
    sig = ar.a('sig', [128, 2, WMAX]); mtmp = ar.a('mtmp', [128, 2, WMAX])
    br_base = max(ar.off, norm_end)
    ar.off = br_base
    xa = ar.a('xa', [128, 4, 15 + TP]); xes = ar.a('xes', [128, 4, NSEQ, 19])
    scr = ar.a('scr', [128, 2, 15 + TP]); scrs = ar.a('scrs', [128, 2, NSEQ, 19])
    dbuf = ar.a('dbuf', [128, 4, WMAX], BF16); ya = ar.a('ya', [128, 4, WMAX], BF16)
    sptok = ar.a('sptok', [128, 2, 512]); xsn = ar.a('xsn', [128, 4, NS])
    pool_end = ar.off
    ar.off = br_base
    qT = ar.a('qT', [128, 4, WMAX], BF16); kT = ar.a('kT', [128, 128 + WMAX], BF16)
    vtok = ar.a('vtok', [128, 6, 128], BF16)
    yb = ar.a('yb', [128, 4, WMAX], BF16)
    cv_tok = ar.a('cv_tok', [128, NSEQ, 128], BF16); kcT = ar.a('kcT', [128, NSEQ, 128], BF16)
    vnew = ar.a('vnew', [64, 128], BF16); vnew32 = ar.a('vnew32', [64, 128])
    smal = ar.a('smal', [128, 2, 8, 4])
    pz = ar.a('pz', [4, 2, 4, 64], BF16)
    al0 = ar.off
    pbuf = ar.a('pbuf', [128, 2, 4, 256]); pn = ar.a('pn', [128, 2, 4, 256], BF16)
    pT = ar.a('pT', [128, 2, 8, 128], BF16)
    attn_end = ar.off
    ar.off = al0
    q32 = ar.a('q32', [128, WMAX]); rt1 = ar.a('rt1', [128, WMAX]); rt2 = ar.a('rt2', [128, WMAX])
    cosT = ar.a('cosT', [128, WMAX]); sinT = ar.a('sinT', [128, WMAX]); kr32 = ar.a('kr32', [128, WMAX])
    ck_tok = ar.a('ck_tok', [128, 8, 128], BF16)
    attn_end = max(attn_end, ar.off)
    ar.off = br_base
    u32 = ar.a('u32', [128, 4, WMAX]); ub = ar.a('ub', [128, 4, WMAX], BF16)
    yg = ar.a('yg', [128, 4, WMAX], BF16)
    mb = P.sbt_at('mb', [128, KC, WMAX], BF16, abase + br_base)
    sl0 = ar.off
    pl = [ar.a('pl%d' % i, [128, TP]) for i in range(8)]
    pli = ar.a('pli', [128, 2, TP], I32)
    tb_s = ar.a('tb_s', [128, 2, TP]); tb_c = ar.a('tb_c', [128, 2, TP])
    hre = ar.a('hre', [128, 2, TP], BF16); him = ar.a('him', [128, 2, TP], BF16)
    sl1 = ar.off
    hl = ar.a('hl', [128, 16, 2])
    hsb_re = ar.a('hsb_re', [128, 16, NS], BF16); hsb_im = ar.a('hsb_im', [128, 16, NS], BF16)
    ubs = ar.a('ubs', [128, 2, 4, NS], BF16)
    ssm_end = ar.off
    ar.off = sl0
    ysm = ar.a('ysm', [128, 3, WMAX]); yc = ar.a('yc', [128, 4, WMAX], BF16)
    assert ar.off <= sl1
    ar.off = sl0
    htok = ar.a('htok', [16, 512])
    dsr = ar.a('dsr', [128, 16, NSEQ, 4]); dsi = ar.a('dsi', [128, 16, NSEQ, 4])
    hq_re = ar.a('hq_re', [128, 16, NSEQ, 5]); hq_im = ar.a('hq_im', [128, 16, NSEQ, 5])
    st1 = ar.a('st1', [128, 16, NSEQ]); st2 = ar.a('st2', [128, 16, NSEQ])
    assert ar.off <= sl1, (ar.off, sl1)
    ar.off = norm_end
    hid = ar.a('hid', [128, HC, WMAX], BF16); sa = ar.a('sa', [128, 2, WMAX])
    ffn_end = ar.off
    ar.off = norm_end
    yf = ar.a('yf', [128, KC, WMAX]); iotok = ar.a('iotok', [128, D])
    fin_end = ar.off
    ar.off = 0
    pp = {}
    for nm in ['are', 'aim', 'ldt', 'dt', 'lre', 'lim', 'rho', 'ft', 'fr', 'afr', 'sn', 'cs', 'abr', 'abi',
               'nr', 'den', 'qre', 'qim', 't1', 't2', 'bre', 'bim']:
        pp[nm] = ar.a('pp_' + nm, [128, 256])
    ppi = ar.a('pp_i', [128, 256], I32)
    craw_re = ar.a('craw_re', [128, 16, 16]); craw_im = ar.a('craw_im', [128, 16, 16])
    vst = ar.a('vst', [72, 128]); kp = ar.a('kp', [32, 800]); araw = ar.a('araw', [32, 2, 64])
    a2 = ar.a('a2', [32, 2, 2, 128]); ldtc = ar.a('ldtc', [32, 1]); rs01 = ar.a('rs01', [32, 2, 16])
    lb = ar.a('lb', [32, 64]); btile = ar.a('btile', [64, 4, 128]); ctile = ar.a('ctile', [16, 2048])
    c17sb = ar.a('c17sb', [17, D])
    cfg['arena_used'] = dict(pool=pool_end, attn=attn_end, ssm=ssm_end, ffn=ffn_end, fin=fin_end, prep=ar.off)

    def body():
        P.dma('sp', ident[:], k_ident)
        P.dma('sp', rotm[:], k_rotm)
        P.dma('pool', identb[:], k_ident)
        P.dma('pool', maskA[:], k_mask[0])
        P.dma('pool', maskB[:], k_mask[1])
        P.memset(onesb[:], 1.0)
        P.dma('sp', sel[:], k_sel)
        P.dma('sp', invc[:].rearrange("p a b -> p (a b)"), k_invc)
        P.dma('pool', thl[:].rearrange("p a b -> p (a b)"), k_thl.partition_broadcast(128))
        P.dma('sp', vst[0:8, :], fng.rearrange("(k p) -> k p", p=128))
        P.mm(psX[:, 512:520], vst[0:8, :], ident[0:8, 0:8], True, True)
        P.copy(v_fng[:], psX[:, 512:520])
        P.dma('sp', c17sb[:], c17)
        for kc in range(KC):
            P.mm(psX[:, kc * 17:(kc + 1) * 17], c17sb[0:17, kc * 128:(kc + 1) * 128], ident[0:17, 0:17], True, True)
        P.act(scT[:].rearrange("p a b -> p (a b)"), psX[:, 0:136], AF.Silu)
        for blk in range(TOT // 128 + 1):
            r0 = blk * 128
            nr = 128 if blk < 16 else NS
            if blk < 16:
                P.dma('sp', iotok[0:nr, :], xp[r0:r0 + nr, :])
            else:
                P.dma('sp', iotok[0:nr, :], xs)
            ps = nextps()
            for kc in range(KC):
                P.mm(ps[:, kc * 128:kc * 128 + nr], iotok[0:nr, kc * 128:(kc + 1) * 128], ident[0:nr, 0:nr], True, True)
            P.copy(x[:, :, r0:r0 + nr], ps[:, :].rearrange("p (k m) -> p k m", k=KC)[:, :, 0:nr], q='act')

        cfg['_stopfn']('setup')
        for l in range(nlayers):
            layer(l)

    def cexp(shape_n, ldt_ap, are_ap, aim_ap):
        n = shape_n
        g = lambda nm: pp[nm][:, 0:n]
        P.act(g('dt'), ldt_ap, AF.Exp)
        P.tt(g('lre'), are_ap, g('dt'), ALU.mult)
        P.tt(g('lim'), aim_ap, g('dt'), ALU.mult)
        P.act(g('rho'), g('lre'), AF.Exp)
        P.ts(g('ft'), g('lim'), 1.0 / (2 * math.pi), None, ALU.mult)
        P.copy(ppi[:, 0:n], g('ft'))
        P.tt(g('fr'), g('ft'), ppi[:, 0:n], ALU.subtract)
        P.stt(g('afr'), g('fr'), -1.0, g('fr'), ALU.mult, ALU.max)
        P.act(g('sn'), g('fr'), AF.Sin, scale=TWO_PI_S)
        P.act(g('cs'), g('afr'), AF.Sin, scale=-TWO_PI_S, bias=halfpi[:, 0:1])
        P.tt(g('abr'), g('rho'), g('cs'), ALU.mult)
        P.tt(g('abi'), g('rho'), g('sn'), ALU.mult)

    def layer_prep(l):
        r0 = 0
        for (src_, n_) in ((n1g[l], 8), (n2g[l], 8), (b_ada[l], 48), (pool_scale[l], 4), (ssm_d[l], 4)):
            P.dma('sp', vst[r0:r0 + n_, :], src_.rearrange("(k p) -> k p", p=128))
            r0 += n_
        P.mm(psX[:, 0:72], vst[0:72, :], ident[0:72, 0:72], True, True)
        P.copy(vecs[:], psX[:, 0:72])
        P.dma('sp', sinkb[:], sinks[l:l + 1, :].partition_broadcast(128))
        P.dma('pool', pw[:], pool_w[l].rearrange("g i o -> i g o"))
        P.memset(halfpi[:], math.pi / 2)
        for bk in range(12):
            wt = wv(W_.get(wload_std(w_ada[l][:, bk * 512:(bk + 1) * 512], 8, 512)), 8, 512)
            for o4 in range(4):
                for kc in range(KC):
                    P.mm(psX[:, o4 * 17:(o4 + 1) * 17], wt[:, kc, o4 * 128:(o4 + 1) * 128], scT[:, kc, :], kc == 0, kc == KC - 1)
            P.tt(modT[:, bk * 4:(bk + 1) * 4, :], psX[:, 0:68].rearrange("p (a b) -> p a b", a=4),
                 bc(v_bada[:, bk * 4:(bk + 1) * 4].unsqueeze(2), [128, 4, 17]), ALU.add)
        P.ts(A1[:], modT[:, 8:16, :], 1.0, None, ALU.add)
        P.tt(A1[:], A1[:], bc(v_n1g.unsqueeze(2), [128, 8, 17]), ALU.mult)
        P.ts(A2[:], modT[:, 32:40, :], 1.0, None, ALU.add)
        P.tt(A2[:], A2[:], bc(v_n2g.unsqueeze(2), [128, 8, 17]), ALU.mult)
        P.dma('sp', kp[:], k_prep)
        P.dma('sp', araw[:, 0, :], a_re[l]); P.dma('sp', araw[:, 1, :], a_im[l])
        P.dma('sp', ldtc[:], log_dt[l].rearrange("(g o) -> g o", o=1))
        S0 = kp[:, 0:16]; S1 = kp[:, 16:32]; E_lo = kp[:, 32:160]; E_hi = kp[:, 160:288]
        Esel = lambda j: kp[:, 288 + j * 128:288 + (j + 1) * 128]
        P.memset(a2[:], 0.0, q='dve')
        for r_ in range(2):
            P.copy(a2[:, r_, 0, 0:64], araw[:, r_, :])
            P.copy(a2[:, r_, 1, 64:128], araw[:, r_, :])
        P.ts(rs01[:, 0, :], S0, ldtc[:, 0:1], None, ALU.mult)
        P.ts(rs01[:, 1, :], S1, ldtc[:, 0:1], None, ALU.mult)
        psc_ = nextps()
        for r_ in range(2):
            P.mm(psc_[:, r_ * 16:(r_ + 1) * 16], a2[:, r_, 0, :], S0, True, False)
            P.mm(psc_[:, r_ * 16:(r_ + 1) * 16], a2[:, r_, 1, :], S1, False, True)
        P.mm(psc_[:, 32:48], E_lo, rs01[:, 0, :], True, False)
        P.mm(psc_[:, 32:48], E_hi, rs01[:, 1, :], False, True)
        P.copy(pp['are'][:, 0:16], psc_[:, 0:16]); P.copy(pp['aim'][:, 0:16], psc_[:, 16:32])
        P.copy(pp['ldt'][:, 0:16], psc_[:, 32:48])
        cexp(16, pp['ldt'][:, 0:16], pp['are'][:, 0:16], pp['aim'][:, 0:16])
        P.copy(c_rho[:], pp['rho'][:, 0:16]); P.copy(c_ft[:], pp['ft'][:, 0:16])
        P.copy(c_abr[:], pp['abr'][:, 0:16]); P.copy(c_abi[:], pp['abi'][:, 0:16])
        P.ts(pp['t1'][:, 0:16], pp['ft'][:, 0:16], 64.0, None, ALU.mult)
        P.copy(ppi[:, 0:16], pp['t1'][:, 0:16])
        P.tt(c_g64[:], pp['t1'][:, 0:16], ppi[:, 0:16], ALU.subtract)
        P.memset(lb[:], 1.0, q='dve')
        P.ts(lb[:], lb[:], ldtc[:, 0:1], None, ALU.mult)
        for (rhs_, nm) in ((araw[:, 0, :], 'are'), (araw[:, 1, :], 'aim'), (lb[:], 'ldt')):
            ps_ = nextps()
            for j in range(4):
                P.mm(ps_[:, j * 64:(j + 1) * 64], Esel(j), rhs_, True, True)
            P.copy(pp[nm][:, :], ps_[:, 0:256])
        for (arr, nm) in ((b_re, 'bre'), (b_im, 'bim')):
            ps_ = nextps()
            for j in range(4):
                P.dma('sp', btile[:, j, :].rearrange("p (m c) -> p m c", c=16), arr[l, 8 * j:8 * j + 8].rearrange("m p c -> p m c"))
                P.mm(ps_[:, j * 64:(j + 1) * 64], btile[:, j, :], ident[0:64, 0:64], True, True)
            P.copy(pp[nm][:, :], ps_[:, 0:256])
        cexp(256, pp['ldt'][:, :], pp['are'][:, :], pp['aim'][:, :])
        g = lambda nm: pp[nm][:, :]
        P.ts(g('nr'), g('abr'), -1.0, None, ALU.add)
        P.tt(g('t1'), g('are'), g('are'), ALU.mult)
        P.tt(g('t2'), g('aim'), g('aim'), ALU.mult)
        P.tt(g('den'), g('t1'), g('t2'), ALU.add)
        P.recip(g('den'), g('den'))
        P.tt(g('t1'), g('nr'), g('are'), ALU.mult)
        P.tt(g('t2'), g('abi'), g('aim'), ALU.mult)
        P.tt(g('qre'), g('t1'), g('t2'), ALU.add)
        P.tt(g('qre'), g('qre'), g('den'), ALU.mult)
        P.tt(g('t1'), g('abi'), g('are'), ALU.mult)
        P.tt(g('t2'), g('nr'), g('aim'), ALU.mult)
        P.tt(g('qim'), g('t1'), g('t2'), ALU.subtract)
        P.tt(g('qim'), g('qim'), g('den'), ALU.mult)
        P.tt(g('t1'), g('qre'), g('bre'), ALU.mult)
        P.tt(g('t2'), g('qim'), g('bim'), ALU.mult)
        P.tt(g('lre'), g('t1'), g('t2'), ALU.subtract)
        P.tt(g('t1'), g('qre'), g('bim'), ALU.mult)
        P.tt(g('t2'), g('qim'), g('bre'), ALU.mult)
        P.tt(g('lim'), g('t1'), g('t2'), ALU.add)
        for mm_ in range(2):
            for gq in range(2):
                sc_ = sel[:, mm_ * 2 + gq:mm_ * 2 + gq + 1]
                P.ts(BT_re[:, :, mm_, gq * 64:(gq + 1) * 64], pp['lre'][:, :].rearrange("p (j q) -> p j q", j=4), sc_, None, ALU.mult)
                P.ts(BT_im[:, :, mm_, gq * 64:(gq + 1) * 64], pp['lim'][:, :].rearrange("p (j q) -> p j q", j=4), sc_, None, ALU.mult)
        for (arr, craw) in ((c_re, craw_re), (c_im, craw_im)):
            P.dma('sp', ctile[:, :].rearrange("c (g p) -> c g p", p=64), arr[l].rearrange("g c p -> c g p"))
            ps_ = nextps()
            for sc in range(16):
                P.mm(ps_[:, sc * 16:(sc + 1) * 16], ctile[0:16, sc * 128:(sc + 1) * 128], ident[0:16, 0:16], True, True)
            P.copy(craw[:].rearrange("p a b -> p (a b)"), ps_[:, 0:256])
        P.memset(CT_re[:], 0.0)
        P.memset(CT_imn[:], 0.0)
        for m in range(4):
            for gq in range(2):
                sc_ = sel[:, 4 + gq:5 + gq]
                c0_ = m * 32 + gq * 16
                P.ts(CT_re[:, m::4, c0_:c0_ + 16], craw_re[:, m::4, :], sc_, None, ALU.mult)
                P.ts(CT_imn[:, m::4, c0_:c0_ + 16], craw_im[:, m::4, :], sel[:, 6 + gq:7 + gq], None, ALU.mult)
        P.memset(kcar[:], 0.0); P.memset(vcar[:], 0.0); P.memset(pcar[:], 0.0); P.memset(gcar[:], 0.0)

    def layer(l):
        layer_prep(l)
        cfg['_stopfn']('prep')
        for t in range(NTILE):
            tile_layer(l, t)
            if l == nlayers - 1:
                final_tile(t)

    def tile_layer(l, t):
        c0 = t * TP
        last = (t == NTILE - 1)
        W = WMAX if last else TP
        segs = [(0, TP)] + ([(TP, NS)] if last else [])
        cfg['_stopfn']('tile%d' % t)
        def nps():
            if last:
                return nextps()
            ps1rr[0] = (ps1rr[0] + 1) % 6
            k = ps1rr[0]
            return PS[k % 3][:, (k // 3) * 512:(k // 3 + 1) * 512]
        LS = (lambda n: cfg['_stopfn']('L_' + n)) if last else (lambda n: None)

        def lin(ps, wfn, kcs, rfn):
            kcs = list(kcs)
            for i, kc in enumerate(kcs):
                for (s0, sn) in segs:
                    P.mm(ps[:, s0:s0 + sn], wfn(kc), rfn(kc, s0, sn), i == 0, i == len(kcs) - 1)

        def s3(ap):
            return ap.rearrange("p (b t) -> p b t", t=4)

        def norm(Am, Bfn):
            for kc in range(KC):
                P.act(sq[:, kc % 2, 0:W], x[:, kc, c0:c0 + W], AF.Square)
                for (s0, sn) in segs:
                    P.mm(psX[:, s0:s0 + sn], onesb[:], sq[:, kc % 2, s0:s0 + sn], kc == 0, kc == KC - 1)
            P.ts(rb[:, 0:W], psX[:, 0:W], 1.0 / D, EPS, ALU.mult, ALU.add)
            P.recip(rb[:, 0:W], rb[:, 0:W])
            P.act(rb[:, 0:W], rb[:, 0:W], AF.Sqrt)
            for kc in range(KC):
                tm = ntmp[:, kc % 2, :]
                P.tt(tm[:, 0:W], x[:, kc, c0:c0 + W], rb[:, 0:W], ALU.mult)
                P.act(h[:, kc, 0:TP], tm[:, 0:TP], AF.Identity, scale=Am[:, kc, 0:1], bias=Bfn(kc)[:, 0:1])
                if last:
                    v = s3(tm[:, TP:W])
                    P.tt(v, v, bc(Am[:, kc, 1:17].unsqueeze(2), [128, 16, 4]), ALU.mult)
                    P.tt(s3(h[:, kc, TP:W]), v, bc(Bfn(kc)[:, 1:17].unsqueeze(2), [128, 16, 4]), ALU.add)

        def resid(ps, oc, gch):
            P.stt(x[:, oc, c0:c0 + TP], ps[:, 0:TP], modT[:, gch + oc, 0:1], x[:, oc, c0:c0 + TP], ALU.mult, ALU.add)
            if last:
                tmv = s3(rb[:, 0:NS])
                P.tt(tmv, s3(ps[:, TP:W]), bc(modT[:, gch + oc, 1:17].unsqueeze(2), [128, 16, 4]), ALU.mult)
                P.tt(s3(x[:, oc, T:TOT]), s3(x[:, oc, T:TOT]), tmv, ALU.add)

        def gate_block(br, qtr, perm_b=False):
            gc0 = 1792 + br * 1024 + qtr * 256

            def issue(buf):
                P.dma('pool', buf[:, 0:2048].rearrange("p (k m) -> p k m", k=8),
                      w_in[l][:, gc0:gc0 + 256].rearrange("(k p) m -> p k m", p=128))
                dst = buf[:, 2048:3072].rearrange("p (k m) -> p k m", k=4)
                if not perm_b:
                    P.dma('pool', dst, wbr[br][l][:, qtr * 256:(qtr + 1) * 256].rearrange("(k p) m -> p k m", p=128))
                else:
                    for g_ in range(2):
                        P.dma('pool', dst[g_ * 64:(g_ + 1) * 64],
                              wbr[br][l][g_ * 256:(g_ + 1) * 256, qtr * 256:(qtr + 1) * 256].rearrange("(c d) m -> d c m", d=64))
            return issue

        def merge(br, src):
            for qtr in range(4):
                buf = W_.get(gate_block(br, qtr, perm_b=(br == 1)))
                gt = buf[:, 0:2048].rearrange("p (k m) -> p k m", k=8)
                bt = buf[:, 2048:3072].rearrange("p (k m) -> p k m", k=4)
                for o in range(2):
                    oc = qtr * 2 + o
                    psB = nps(); psG = nps()
                    lin(psB, lambda kc: bt[:, kc, o * 128:(o + 1) * 128], range(4), lambda kc, s0, sn: src[:, kc, s0:s0 + sn])
                    lin(psG, lambda kc: gt[:, kc, o * 128:(o + 1) * 128], range(8), lambda kc, s0, sn: h[:, kc, s0:s0 + sn])
                    P.act(sig[:, oc % 2, 0:W], psG[:, 0:W], AF.Sigmoid)
                    if br == 0:
                        P.tt(merged[:, oc, 0:W], sig[:, oc % 2, 0:W], psB[:, 0:W], ALU.mult)
                    else:
                        P.tt(mtmp[:, oc % 2, 0:W], sig[:, oc % 2, 0:W], psB[:, 0:W], ALU.mult)
                        dst = mb if br == 2 else merged
                        P.tt(dst[:, oc, 0:W], merged[:, oc, 0:W], mtmp[:, oc % 2, 0:W], ALU.add)

        norm(A1, lambda kc: modT[:, kc, :])
        if dbg and l == 0 and t == 3:
            dump('h', h[:], [128, KC, WMAX], BF16)

        cfg['_stopfn']('n1')
        LS('n1')
        wt = wv(W_.get(wload_std(w_in[l][:, 0:512], 8, 512)), 8, 512)
        if t == 0:
            P.memset(xa[:, :, 0:15], 0.0)
        else:
            P.copy(xa[:, :, 0:15], pcar[:])
        for oc in range(4):
            ps = nextps()
            lin(ps, lambda kc: wt[:, kc, oc * 128:(oc + 1) * 128], range(8), lambda kc, s0, sn: h[:, kc, s0:s0 + sn])
            P.copy(xa[:, oc, 15:15 + TP], ps[:, 0:TP], q='act')
            if last:
                P.copy(xes[:, oc, :, 15:19], s3(ps[:, TP:W]), q='act')
        P.copy(pcar[:], xa[:, :, TP:TP + 15])
        if last:
            for hh in range(2):
                P.dma('sp', sptok[0:120, hh, :], spool[l, hh * 8:(hh + 1) * 8].rearrange("b j f -> (b j) f"))
                for g_ in range(4):
                    P.mm(psX[:, g_ * 128:g_ * 128 + 120], sptok[0:120, hh, g_ * 128:(g_ + 1) * 128], ident[0:120, 0:120], True, True)
                for g_ in range(4):
                    P.copy(xes[:, g_, hh * 8:(hh + 1) * 8, 0:15],
                           psX[:, g_ * 128:g_ * 128 + 120].rearrange("p (b j) -> p b j", j=15), q='act')
        L_ = 15 + TP
        for g_ in range(4):
            w_ = 2 << g_
            cur = xa[:, g_, :]
            lo = 0
            for si, step in enumerate([1, 2, 4, 8][:g_ + 1]):
                nxt = scr[:, si % 2, :]
                P.tt(nxt[:, lo + step:L_], cur[:, lo + step:L_], cur[:, lo:L_ - step], ALU.add)
                cur = nxt
                lo += step
            P.stt(dbuf[:, g_, 0:TP], cur[:, 15:L_], 1.0 / w_, xa[:, g_, 15:L_], ALU.mult, ALU.subtract)
            if t == 0:
                P.tt(rb[:, 0:16], cur[:, 15:31], invc[:, g_, :], ALU.mult)
                P.tt(dbuf[:, g_, 0:16], rb[:, 0:16], xa[:, g_, 15:31], ALU.subtract)
            if last:
                cur = xes[:, g_, :, :]
                lo = 0
                for si, step in enumerate([1, 2, 4, 8][:g_ + 1]):
                    nxt = scrs[:, si % 2, :, :]
                    P.tt(nxt[:, :, lo + step:19], cur[:, :, lo + step:19], cur[:, :, lo:19 - step], ALU.add)
                    cur = nxt
                    lo += step
                P.stt(s3(dbuf[:, g_, TP:W]), cur[:, :, 15:19], 1.0 / w_, xes[:, g_, :, 15:19], ALU.mult, ALU.subtract)
        for g_ in range(4):
            ps = nextps()
            lin(ps, lambda kc: pw[:, g_, :], [0], lambda kc, s0, sn: dbuf[:, g_, s0:s0 + sn])
            P.act(ya[:, g_, 0:W], ps[:, 0:W], AF.Identity, scale=v_psc[:, g_:g_ + 1])
        if last:
            for g_ in range(4):
                P.mm(psX[0:15, g_ * 128:(g_ + 1) * 128], xa[:, g_, TP:TP + 15], ident[:], True, True)
            P.copy(otok[0:15, :], psX[0:15, 0:512], q='act')
            P.dma('sp', npool_p[l], otok[0:15, :], is_out=True)
            P.dma('sp', npool_s[l, :, 0:11, :], spool[l, :, 4:15, :], is_out=True)
            for g_ in range(4):
                P.copy(xsn[:, g_, :].rearrange("p (b t) -> p b t", t=4), xes[:, g_, :, 15:19])
                P.mm(psX[0:64, 512 + g_ * 128:512 + (g_ + 1) * 128], xsn[:, g_, :], ident[:], True, True)
            P.copy(otok[0:64, :], psX[0:64, 512:1024], q='act')
            for b in range(NSEQ):
                P.dma('sp', npool_s[l, b, 11:15, :], otok[b * 4:(b + 1) * 4, :], is_out=True)
        merge(0, ya)
        if dbg and l == 0 and t == 3:
            dump('ya', ya[:], [128, 4, WMAX], BF16)
            dump('mergedA', merged[:], [128, KC, WMAX])

        cfg['_stopfn']('pool')
        LS('pool')
        P.dma('sp', cosT[:, 0:W], k_rope[0][:, c0:c0 + W])
        P.dma('sp', sinT[:, 0:W], k_rope[1][:, c0:c0 + W])

        def issue_q(buf):
            dst = buf[:, 0:4096].rearrange("p (k c g d) -> p k c g d", k=8, c=4, g=2)
            for g_ in range(2):
                for c_ in range(4):
                    cq = 512 + g_ * 256 + c_ * 64
                    P.dma('pool', dst[:, :, c_, g_, :], w_in[l][:, cq:cq + 64].rearrange("(k p) d -> p k d", p=128))
        wt = wv(W_.get(issue_q), 8, 512)

        def rope(ps, dst_ap, dst32=None):
            P.copy(q32[:, 0:W], ps[:, 0:W], q='act')
            psr = nextps()
            for (s0, sn) in segs:
                P.mm(psr[:, s0:s0 + sn], rotm[:], q32[:, s0:s0 + sn], True, True)
            P.tt(rt1[:, 0:W], q32[:, 0:W], cosT[:, 0:W], ALU.mult)
            P.tt(rt2[:, 0:W], psr[:, 0:W], sinT[:, 0:W], ALU.mult)
            if dst32 is None:
                P.tt(dst_ap, rt1[:, 0:W], rt2[:, 0:W], ALU.add)
            else:
                P.tt(dst32, rt1[:, 0:W], rt2[:, 0:W], ALU.add)
                P.copy(dst_ap, dst32, q='act')
        for c_ in range(4):
            ps = nextps()
            lin(ps, lambda kc: wt[:, kc, c_ * 128:(c_ + 1) * 128], range(8), lambda kc, s0, sn: h[:, kc, s0:s0 + sn])
            rope(ps, qT[:, c_, 0:W])
        wt = wv(W_.get(wload_std(w_in[l][:, 1024:1280], 8, 256)), 8, 256)
        ps = nextps()
        lin(ps, lambda kc: wt[:, kc, 0:128], range(8), lambda kc, s0, sn: h[:, kc, s0:s0 + sn])
        rope(ps, kT[:, 128:128 + W], dst32=kr32[:, 0:W])
        P.copy(kT[:, 0:128], kcar[:])
        P.copy(vtok[:, 0, :], vcar[:])
        psv = nextps()
        for bi in range(4):
            for kc in range(KC):
                P.mm(psv[:, bi * 128:(bi + 1) * 128], h[:, kc, bi * 128:(bi + 1) * 128], wt[:, kc, 128:256], kc == 0, kc == KC - 1)
        P.copy(vtok[:, 1:5, :], psv[:, 0:512].rearrange("p (b f) -> p b f", b=4), q='act')
        if last:
            P.copy(otok[:, 0:128], psv[:, 384:512])
            P.dma('sp', nv_p[l], otok[:, 0:128], is_out=True)
            P.mm(psX[:, 0:128], kr32[:, 384:512], ident[:], True, True)
            P.copy(otok[:, 128:256], psX[:, 0:128])
            P.dma('sp', nk_p[l], otok[:, 128:256], is_out=True)
            for kc in range(KC):
                P.mm(psX[0:NS, 256:384], h[:, kc, TP:W], wt[:, kc, 128:256], kc == 0, kc == KC - 1)
            P.copy(vnew32[:], psX[0:NS, 256:384], q='act')
            P.copy(vnew[:], vnew32[:])
            for b in range(NSEQ):
                P.dma('sp', nv_s[l, b, 124:128, :], vnew32[b * 4:(b + 1) * 4, :], is_out=True)
            P.dma('sp', nv_s[l, :, 0:124, :], cv[l, :, 4:128, :], is_out=True)
            P.dma('sp', nk_s[l, :, 0:124, :], ck[l, :, 4:128, :], is_out=True)
            P.mm(psX[0:NS, 384:512], kr32[:, TP:W], ident[:], True, True)
            P.copy(otok[0:NS, 256:384], psX[0:NS, 384:512])
            for b in range(NSEQ):
                P.dma('sp', nk_s[l, b, 124:128, :], otok[b * 4:(b + 1) * 4, 256:384], is_out=True)
            P.dma('pool', cv_tok[:], cv[l].rearrange("b k f -> k b f"))
            for hh in range(2):
                P.dma('pool', ck_tok[:], ck[l, hh * 8:(hh + 1) * 8].rearrange("b k f -> k b f"))
                pst_ = PS[3][:, 512:1024].bitcast(BF16)
                for b8 in range(8):
                    P.tr(pst_[:, b8 * 128:(b8 + 1) * 128], ck_tok[:, b8, :], identb[:])
                P.copy(kcT[:, hh * 8:(hh + 1) * 8, :], pst_[:, :].rearrange("p (b f) -> p b f", b=8))
        P.copy(kcar[:], kT[:, TP:TP + 128])
        P.copy(vcar[:], vtok[:, 4, :])
        LS('attnin')

        def attn_unit(u, nq, g_, q_fn, k_parts, mask, nk, norm_fn, v_parts, out_fn):
            psS = PS[u]
            psTO = PS[2 + u]
            psT_ = psTO[:, 0:512].bitcast(BF16)
            psO = psTO[:, 512:1024]
            sm = smal[0:nq, u, :, :]
            sk = sinkb[0:nq, g_ * 4:(g_ + 1) * 4]
            S3 = psS[0:nq, :].rearrange("p (c k) -> p c k", c=4)[:, :, 0:nk]
            rows0 = v_parts[0][0]
            steps = []

            def s_qk():
                for c_ in range(4):
                    for ki, (k0, kn, kap) in enumerate(k_parts):
                        P.mm(psS[0:nq, c_ * 256 + k0:c_ * 256 + k0 + kn], q_fn(c_), kap, ki == 0, False)
                    P.mm(psS[0:nq, c_ * 256:c_ * 256 + nk], identb[0:nq, 0:nq], mask[0:nq, 0:nk], False, True)
            steps.append(s_qk)
            steps.append(lambda: P.rmax(sm[:, 0, :], S3))
            steps.append(lambda: P.ts(sm[:, 0, :], sm[:, 0, :], 0.125, None, ALU.mult))
            steps.append(lambda: P.tt(sm[:, 1, :], sm[:, 0, :], sk, ALU.max))
            steps.append(lambda: P.ts(sm[:, 2, :], sm[:, 1, :], -1.0, None, ALU.mult))
            steps.append(lambda: P.tt(sm[:, 4, :], sk, sm[:, 1, :], ALU.subtract))

            def s_exp():
                for c_ in range(4):
                    P.act(pbuf[0:nq, u, c_, 0:nk], psS[0:nq, c_ * 256:c_ * 256 + nk], AF.Exp, scale=0.125,
                          bias=sm[:, 2, c_:c_ + 1], accum_out=sm[:, 3, c_:c_ + 1])
                P.act(sm[:, 4, :], sm[:, 4, :], AF.Exp)
            steps.append(s_exp)
            steps.append(lambda: P.tt(sm[:, 5, :], sm[:, 3, :], sm[:, 4, :], ALU.add))
            steps.append(lambda: P.recip(sm[:, 6, :], sm[:, 5, :]))
            steps.append(lambda: norm_fn(u, sm[:, 6, :]))

            def s_tr():
                for c_ in range(4):
                    for vi, (rows, vap, src_fn) in enumerate(v_parts):
                        P.tr(psT_[0:rows, (c_ * 2 + vi) * 128:(c_ * 2 + vi) * 128 + nq], src_fn(u, c_), identb[0:nq, 0:nq])
            steps.append(s_tr)
            steps.append(lambda: P.copy(pT[0:rows0, u, :, 0:nq], psT_[0:rows0, :].rearrange("p (s q) -> p s q", s=8)[:, :, 0:nq], q='act'))

            def s_pv():
                for c_ in range(4):
                    for vi, (rows, vap, src_fn) in enumerate(v_parts):
                        P.mm(psO[:, c_ * 128:c_ * 128 + nq], vap, pT[0:rows, u, c_ * 2 + vi, 0:nq], vi == 0, vi == len(v_parts) - 1)
            steps.append(s_pv)
            steps.append(lambda: out_fn(psO))
            return steps

        def run_pair(ua, ub_):
            for fa, fb in zip(ua, ub_):
                fa()
                fb()

        for bi in range(4):
            gb = t * 4 + bi
            units = []
            for g_ in range(2):
                gs = slice(g_ * 64, (g_ + 1) * 64)

                def outp(psO, bi=bi, gs=gs):
                    P.copy(yb[gs, :, bi * 128:(bi + 1) * 128], psO[gs, :].rearrange("p (c q) -> p c q", c=4))

                def normp(u, rinv):
                    P.tt(pn[:, u, :, :], pbuf[:, u, :, :], bc(rinv.unsqueeze(2), [128, 4, 256]), ALU.mult)
                units.append(attn_unit(g_, 128, g_, lambda c_, bi=bi, gs=gs: qT[gs, c_, bi * 128:(bi + 1) * 128],
                                       [(0, 256, kT[gs, bi * 128:bi * 128 + 256])],
                                       maskB if gb == 0 else maskA, 256, normp,
                                       [(128, vtok[:, bi, :], lambda u, c_: pn[:, u, c_, 0:128]),
                                        (128, vtok[:, bi + 1, :], lambda u, c_: pn[:, u, c_, 128:256])], outp))
            run_pair(units[0], units[1])
        LS('attnp')
        if last:
            for b in range(NSEQ):
                units = []
                for g_ in range(2):
                    gs = slice(g_ * 64, (g_ + 1) * 64)
                    cs_ = slice(TP + 4 * b, TP + 4 * b + 4)

                    def outp(psO, gs=gs, cs_=cs_):
                        P.copy(yb[gs, :, cs_], psO[gs, :].rearrange("p (c q) -> p c q", c=4)[:, :, 0:4])

                    def norms(u, rinv, b=b):
                        P.tt(pn[0:4, u, :, 0:128], pbuf[0:4, u, :, 0:128], bc(rinv.unsqueeze(2), [4, 4, 128]), ALU.mult)
                        P.memset(pz[0:4, u, :, :], 0.0, q='dve')
                        P.tt(pz[0:4, u, :, 4 * b:4 * b + 4], pbuf[0:4, u, :, 128:132], bc(rinv.unsqueeze(2), [4, 4, 4]), ALU.mult)
                    units.append(attn_unit(g_, 4, g_, lambda c_, gs=gs, cs_=cs_: qT[gs, c_, cs_],
                                           [(0, 128, kcT[gs, b, :]), (128, 4, kT[gs, 128 + TP + 4 * b:128 + TP + 4 * b + 4])],
                                           maskA, 132, norms,
                                           [(128, cv_tok[:, b, :], lambda u, c_: pn[0:4, u, c_, 0:128]),
                                            (64, vnew[:, :], lambda u, c_: pz[0:4, u, c_, :])], outp))
                run_pair(units[0], units[1])
        if dbg and l == 0 and t == 3:
            dump('yb', yb[:], [128, 4, WMAX], BF16)
            dump('qT', qT[:], [128, 4, WMAX], BF16)
        merge(1, yb)

        cfg['_stopfn']('attn')
        LS('attn')
        wt = wv(W_.get(wload_std(w_in[l][:, 1280:1792], 8, 512)), 8, 512)
        cfg['_stopfn']('ssm_a')
        for j in range(4):
            ps = nextps()
            lin(ps, lambda kc: wt[:, kc, j * 128:(j + 1) * 128], range(8), lambda kc, s0, sn: h[:, kc, s0:s0 + sn])
            P.copy(u32[:, j, 0:W], ps[:, 0:W], q='act')
            P.copy(ub[:, j, 0:W], ps[:, 0:W])
        cfg['_stopfn']('ssm_b')
        LS('s0')
        P.ts(ctmp[:], c_g64[:], float(8 * t), None, ALU.mult)
        P.copy(ctmpi[:], ctmp[:])
        P.tt(c_ct[:], ctmp[:], ctmpi[:], ALU.subtract)
        cfg['_stopfn']('ssm_u')
        psYs = None
        if last:
            psdr = nextps(); psdi = nextps()
            for hf in range(2):
                P.ts(ubs[:, hf, :, :], ub[:, :, TP:W], sel[:, 4 + hf:5 + hf], None, ALU.mult)
            for sc in range(16):
                j, m = sc // 4, sc % 4
                P.mm(psdr[:, sc * NS:(sc + 1) * NS], BT_re[:, j, m % 2, :], ubs[:, m // 2, j, :], True, True)
                P.mm(psdi[:, sc * NS:(sc + 1) * NS], BT_im[:, j, m % 2, :], ubs[:, m // 2, j, :], True, True)
            LS('s0b')
            P.copy(dsr[:].rearrange("p a b c -> p (a b c)"), psdr[:, :], q='act')
            LS('s0c')
            P.copy(dsi[:].rearrange("p a b c -> p (a b c)"), psdi[:, :])
            LS('s1')
            for (src_, dstq) in ((sre, hq_re), (sim, hq_im)):
                for r4 in range(4):
                    P.dma('sp', htok[:], src_[l][:, r4 * 512:(r4 + 1) * 512])
                    for s_ in range(4):
                        sc = r4 * 4 + s_
                        P.mm(psX[:, sc * 16:(sc + 1) * 16], htok[0:16, s_ * 128:(s_ + 1) * 128], ident[0:16, 0:16], True, True)
                P.copy(dstq[:, :, :, 0], psX[:, 0:256].rearrange("p (a b) -> p a b", a=16), q='act')
            LS('s2')
            ar_b = bc(c_abr[:].unsqueeze(2), [128, 16, NSEQ])
            ai_b = bc(c_abi[:].unsqueeze(2), [128, 16, NSEQ])
            for tt_ in range(4):
                pr = hq_re[:, :, :, tt_]; pi_ = hq_im[:, :, :, tt_]
                P.tt(st1[:], ar_b, pr, ALU.mult)
                P.tt(st2[:], ai_b, pi_, ALU.mult)
                P.tt(st1[:], st1[:], st2[:], ALU.subtract)
                P.tt(hq_re[:, :, :, tt_ + 1], st1[:], dsr[:, :, :, tt_], ALU.add)
                P.tt(st1[:], ar_b, pi_, ALU.mult)
                P.tt(st2[:], ai_b, pr, ALU.mult)
                P.tt(st1[:], st1[:], st2[:], ALU.add)
                P.tt(hq_im[:, :, :, tt_ + 1], st1[:], dsi[:, :, :, tt_], ALU.add)
            LS('s3')
            P.copy(hsb_re[:].rearrange("p a (b t) -> p a b t", t=4), hq_re[:, :, :, 1:5])
            P.copy(hsb_im[:].rearrange("p a (b t) -> p a b t", t=4), hq_im[:, :, :, 1:5])
            LS('s4')
            for (srcq, dsto) in ((hq_re, nre_s), (hq_im, nim_s)):
                for r4 in range(4):
                    for s_ in range(4):
                        sc = r4 * 4 + s_
                        P.mm(psX[0:16, s_ * 128:(s_ + 1) * 128], srcq[:, sc, :, 4], ident[:], True, True)
                    P.copy(htok[0:16, :], psX[0:16, 0:512], q='act')
                    P.dma('sp', dsto[l][:, r4 * 512:(r4 + 1) * 512], htok[:], is_out=True)
        LS('ssms')
        for j in range(4):
            psY = PS[3]

            def chain(m, ci):
                sc = 4 * j + m
                u = ci
                hs_ = slice((m // 2) * 64, (m // 2) * 64 + 64)
                A_, B_, C_, D_ = pl[4 * ci][:, :], pl[4 * ci + 1][:, :], pl[4 * ci + 2][:, :], pl[4 * ci + 3][:, :]
                pi_ = pli[:, ci, :]
                sn_ = tb_s[:, u, :]; cs_ = tb_c[:, u, :]
                st = {}
                rho_b = bc(c_rho[:, sc:sc + 1], [128, TP])
                steps = []

                def s_drive():
                    st['psD'] = nextps()
                    P.mm(st['psD'][:, 0:TP], BT_re[hs_, j, m % 2, :], ub[hs_, j, 0:TP], True, True)
                    P.mm(st['psD'][:, TP:2 * TP], BT_im[hs_, j, m % 2, :], ub[hs_, j, 0:TP], True, True)
                steps.append(s_drive)
                steps.append(lambda: P.act(A_, thl[:, 1, :], AF.Identity, scale=c_ft[:, sc:sc + 1], bias=c_ct[:, sc:sc + 1]))
                steps.append(lambda: P.stt(A_, thl[:, 0, :], c_g64[:, sc:sc + 1], A_, ALU.mult, ALU.add))
                steps.append(lambda: P.copy(pi_, A_))
                steps.append(lambda: P.tt(A_, A_, pi_, ALU.subtract))
                steps.append(lambda: P.stt(B_, A_, -1.0, A_, ALU.mult, ALU.max))

                def s_sin():
                    P.act(sn_, A_, AF.Sin, scale=TWO_PI_S)
                    P.act(cs_, B_, AF.Sin, scale=-TWO_PI_S, bias=halfpi[:, 0:1])
                steps.append(s_sin)
                dre = lambda: st['psD'][:, 0:TP]
                dim_ = lambda: st['psD'][:, TP:2 * TP]
                steps.append(lambda: P.tt(A_, dre(), cs_, ALU.mult))
                steps.append(lambda: P.tt(B_, dim_(), sn_, ALU.mult))
                steps.append(lambda: P.tt(A_, A_, B_, ALU.add))
                steps.append(lambda: P.tt(B_, dim_(), cs_, ALU.mult))
                steps.append(lambda: P.tt(C_, dre(), sn_, ALU.mult))
                steps.append(lambda: P.tt(B_, B_, C_, ALU.subtract))
                steps.append(lambda: P.scan(C_, rho_b, A_, gcar[:, sc, 0:1]))
                steps.append(lambda: P.scan(D_, rho_b, B_, gcar[:, sc, 1:2]))

                def s_carry():
                    P.copy(gcar[:, sc, 0:1], C_[:, TP - 1:TP], q='act')
                    P.copy(gcar[:, sc, 1:2], D_[:, TP - 1:TP], q='act')
                steps.append(s_carry)
                steps.append(lambda: P.tt(A_, cs_, C_, ALU.mult))
                steps.append(lambda: P.tt(B_, sn_, D_, ALU.mult))

                def s_hre():
                    P.tt(hre[:, u, :], A_, B_, ALU.subtract)
                    if last:
                        P.tt(hl[:, sc, 0:1], A_[:, TP - 1:TP], B_[:, TP - 1:TP], ALU.subtract)
                steps.append(s_hre)
                steps.append(lambda: P.tt(A_, cs_, D_, ALU.mult))
                steps.append(lambda: P.tt(B_, sn_, C_, ALU.mult))

                def s_him():
                    P.tt(him[:, u, :], A_, B_, ALU.add)
                    if last:
                        P.tt(hl[:, sc, 1:2], A_[:, TP - 1:TP], B_[:, TP - 1:TP], ALU.add)
                steps.append(s_him)

                def s_cmm():
                    P.mm(psY[:, 0:TP], CT_re[:, sc, :], hre[:, u, :], m == 0, False)
                    P.mm(psY[:, 0:TP], CT_imn[:, sc, :], him[:, u, :], False, m == 3)
                    if last:
                        P.mm(psY[:, TP:W], CT_re[:, sc, :], hsb_re[:, sc, :], m == 0, False)
                        P.mm(psY[:, TP:W], CT_imn[:, sc, :], hsb_im[:, sc, :], False, m == 3)
                steps.append(s_cmm)
                return steps
            for pr in range(2):
                ca = chain(2 * pr, 0)
                cb = chain(2 * pr + 1, 1)
                for fa, fb in zip(ca, cb):
                    fa()
                    fb()
            cfg['_stopfn']('ssm_post')
            Y1 = ysm[:, 0, 0:W]; Y2 = ysm[:, 1, 0:W]; SG = ysm[:, 2, 0:W]
            P.stt(Y1, u32[:, j, 0:W], v_ssmd[:, j:j + 1], psY[:, 0:W], ALU.mult, ALU.add)
            P.tt(Y2, Y1, Y1, ALU.mult)
            P.ts(Y2, Y2, 0.044715, 1.0, ALU.mult, ALU.add)
            P.tt(Y2, Y2, Y1, ALU.mult)
            P.act(SG, Y2, AF.Sigmoid, scale=1.5957691216)
            P.tt(yg[:, j, 0:W], Y1, SG, ALU.mult)
        if last:
            for (cidx, dsto) in ((0, nre_p), (1, nim_p)):
                P.mm(psX[0:16, 0:128], hl[:, :, cidx], ident[:], True, True)
                P.copy(otok[0:16, cidx * 128:(cidx + 1) * 128], psX[0:16, 0:128])
                P.dma('sp', dsto[l], otok[0:16, cidx * 128:(cidx + 1) * 128], is_out=True)
        wt = wv(W_.get(wload_std(w_glu[l], 4, 512)), 4, 512)
        for oc in range(4):
            ps = nextps()
            lin(ps, lambda kc: wt[:, kc, oc * 128:(oc + 1) * 128], range(4), lambda kc, s0, sn: yg[:, kc, s0:s0 + sn])
            P.act(ysm[:, oc % 2, 0:W], ps[:, 0:W], AF.Sigmoid)
            P.tt(yc[:, oc, 0:W], yg[:, oc, 0:W], ysm[:, oc % 2, 0:W], ALU.mult)
        if dbg and l == 0 and t == 3:
            dump('yc', yc[:], [128, 4, WMAX], BF16)
        merge(2, yc)

        cfg['_stopfn']('ssm')
        LS('ssm')
        for hb in range(2):
            wt = wv(W_.get(wload_std(w_out[l][:, hb * 512:(hb + 1) * 512], 8, 512)), 8, 512)
            for o in range(4):
                oc = hb * 4 + o
                ps = nps()
                lin(ps, lambda kc: wt[:, kc, o * 128:(o + 1) * 128], range(8), lambda kc, s0, sn: mb[:, kc, s0:s0 + sn])
                resid(ps, oc, 16)
        cfg['_stopfn']('outp')
        norm(A2, lambda kc: modT[:, 24 + kc, :])
        for hp in range(HC // 2):
            def issue(buf, hp=hp):
                dst = buf[:, 0:4096].rearrange("p (k a m) -> p k a m", k=8, a=2)
                for a_ in range(2):
                    cw = a_ * DFF + hp * 256
                    P.dma('pool', dst[:, :, a_, :], w_ffn_in[l][:, cw:cw + 256].rearrange("(k p) n -> p k n", p=128))
            wt = W_.get(issue)[:, 0:4096].rearrange("p (k a m) -> p k a m", k=8, a=2)
            for o in range(2):
                hc = hp * 2 + o
                psA = nps(); psB = nps()
                lin(psA, lambda kc: wt[:, kc, 0, o * 128:(o + 1) * 128], range(8), lambda kc, s0, sn: h[:, kc, s0:s0 + sn])
                lin(psB, lambda kc: wt[:, kc, 1, o * 128:(o + 1) * 128], range(8), lambda kc, s0, sn: h[:, kc, s0:s0 + sn])
                P.act(sa[:, hc % 2, 0:W], psA[:, 0:W], AF.Silu)
                P.tt(hid[:, hc, 0:W], sa[:, hc % 2, 0:W], psB[:, 0:W], ALU.mult)
        for oc in range(KC):
            wt = wv(W_.get(wload_std(w_ffn_out[l][:, oc * 128:(oc + 1) * 128], HC, 128)), HC, 128)
            ps = nps()
            lin(ps, lambda kc: wt[:, kc, :], range(HC), lambda kc, s0, sn: hid[:, kc, s0:s0 + sn])
            resid(ps, oc, 40)
        cfg['_stopfn']('ffn')
        LS('end')

    def final_tile(t):
        c0 = t * TP
        last = (t == NTILE - 1)
        W = WMAX if last else TP
        segs = [(0, TP)] + ([(TP, NS)] if last else [])
        for kc in range(KC):
            P.act(sq[:, kc % 2, 0:W], x[:, kc, c0:c0 + W], AF.Square)
            for (s0, sn) in segs:
                P.mm(psX[:, s0:s0 + sn], onesb[:], sq[:, kc % 2, s0:s0 + sn], kc == 0, kc == KC - 1)
        P.ts(rb[:, 0:W], psX[:, 0:W], 1.0 / D, EPS, ALU.mult, ALU.add)
        P.recip(rb[:, 0:W], rb[:, 0:W])
        P.act(rb[:, 0:W], rb[:, 0:W], AF.Sqrt)
        for kc in range(KC):
            P.stt(yf[:, kc, 0:W], x[:, kc, c0:c0 + W], v_fng[:, kc:kc + 1], rb[:, 0:W], ALU.mult, ALU.mult)
        nblk = 5 if last else 4
        for blk in range(nblk):
            nr = 128 if blk < 4 else NS
            ps = nextps()
            for kc in range(KC):
                P.mm(ps[0:nr, kc * 128:(kc + 1) * 128], yf[:, kc, blk * 128:blk * 128 + nr], ident[:], True, True)
            P.copy(iotok[0:nr, :], ps[0:nr, :], q='act')
            if blk < 4:
                P.dma('sp', y_p[c0 + blk * 128:c0 + (blk + 1) * 128, :], iotok[0:nr, :], is_out=True)
            else:
                P.dma('sp', y_s, iotok[0:nr, :], is_out=True)

    def stop(name):
        if cfg.get('stop') == name:
            raise StopBuild()
    cfg['_stopfn'] = stop
    for dry in (True, False):
        P.dry = dry
        psrr[0] = 0
        try:
            body()
        except StopBuild:
            pass
    return P


def _consts():
    ident = np.eye(128, dtype=np.float32)
    rotm = np.zeros((128, 128), np.float32)
    for d in range(128):
        dd = d % 64
        if dd < 8:
            rotm[d + 8, d] = -1.0
        elif dd < 16:
            rotm[d - 8, d] = 1.0
    qi = np.arange(128)[:, None]
    kj = np.arange(256)[None, :]
    diff = 128 + qi - kj
    band = (diff >= 0) & (diff < 128)
    mA = np.where(band, 0.0, -30000.0).astype(np.float32)
    mB = np.where(band & (kj >= 128), 0.0, -30000.0).astype(np.float32)
    mask = np.stack([mA, mB])
    pos = np.concatenate([np.arange(T), np.tile(PAST + np.arange(4), NSEQ)]).astype(np.float32)
    inv = (500000.0 ** (-np.arange(0, 16, 2, dtype=np.float32) / 16)).astype(np.float32)
    ang = pos[:, None] * inv[None, :]
    cos = np.cos(ang).astype(np.float32)
    sin = np.sin(ang).astype(np.float32)
    rope = np.zeros((2, 128, TOT), np.float32)
    rope[0] = 1.0
    for p in range(128):
        dd = p % 64
        if dd < 16:
            rope[0, p] = cos[:, dd % 8]
            rope[1, p] = sin[:, dd % 8]
    sel = np.zeros((128, 8), np.float32)
    for p in range(128):
        for mm_ in range(2):
            for gq in range(2):
                sel[p, mm_ * 2 + gq] = 1.0 if ((p % 64) // 32 == mm_ and (p % 32) // 16 == gq) else 0.0
        for gq in range(2):
            sel[p, 4 + gq] = 1.0 if p // 64 == gq else 0.0
            sel[p, 6 + gq] = -sel[p, 4 + gq]
    invc = np.zeros((128, 4, 16), np.float32)
    for g in range(4):
        w = 2 << g
        invc[:, g, :] = 1.0 / np.minimum(np.arange(16) + 1, w)
    tt_ = np.arange(512)
    thl = np.concatenate([(tt_ // 64), (tt_ % 64)]).astype(np.float32)[None, :]
    kprep = np.zeros((32, 800), np.float32)
    for g in range(32):
        if g % 2 == 0:
            kprep[g, g // 2] = 1.0
        else:
            kprep[g, 16 + g // 2] = 1.0
        kprep[g, 32:96] = 1.0
        kprep[g, 160 + 64:160 + 128] = 1.0
        j, mg = g // 8, g % 8
        kprep[g, 288 + j * 128 + mg * 16:288 + j * 128 + mg * 16 + 16] = 1.0
    return dict(k_prep=kprep, k_ident=ident, k_rotm=rotm, k_mask=mask, k_rope=rope, k_sel=sel,
                k_invc=invc.reshape(128, 64), k_thl=thl)


_WNAMES = ['norm1_g', 'norm2_g', 'w_ada', 'b_ada', 'w_in', 'pool_w', 'pool_scale', 'attn_sinks', 'ssm_a_re',
           'ssm_a_im', 'ssm_log_dt', 'ssm_b_re', 'ssm_b_im', 'ssm_c_re', 'ssm_c_im', 'ssm_d', 'w_glu',
           'w_branch_a', 'w_branch_b', 'w_branch_c', 'w_out', 'w_ffn_in', 'w_ffn_out', 'final_norm_g']


def make_in_maps(inputs, ncores=8):
    f = lambda a: np.ascontiguousarray(np.asarray(a, dtype=np.float32))
    shared = {n: f(inputs[n]) for n in _WNAMES}
    shared.update(_consts())
    maps = []
    for c in range(ncores):
        sl = slice(c * NSEQ, (c + 1) * NSEQ)
        m = dict(shared)
        m['xp'] = f(inputs['x_prompt'][c])
        m['xs'] = f(np.asarray(inputs['x_sample'])[sl].reshape(NS, D))
        m['ck'] = f(np.asarray(inputs['cache_win_k'])[:, sl].reshape(NL, NSEQ, 128, 128))
        m['cv'] = f(np.asarray(inputs['cache_win_v'])[:, sl].reshape(NL, NSEQ, 128, 128))
        m['spool'] = f(np.asarray(inputs['state_pool'])[:, sl])
        m['sre'] = f(np.asarray(inputs['state_ssm_re'])[:, sl].reshape(NL, NSEQ, 2048))
        m['sim'] = f(np.asarray(inputs['state_ssm_im'])[:, sl].reshape(NL, NSEQ, 2048))
        m['c17'] = f(np.concatenate([np.asarray(inputs['c_prompt'])[c:c + 1], np.asarray(inputs['c_sample'])[sl]], 0))
        maps.append(m)
    return maps


def assemble(results):
    R = results
    n = len(R)
    cat = lambda k, ax: np.concatenate([r[k] for r in R], axis=ax)
    y_prompt = np.stack([r['y_p'] for r in R])
    y_sample = cat('y_s', 0).reshape(n * NSEQ, 4, D)
    nk_p = np.stack([r['nk_p'] for r in R], 1).reshape(NL, n, 128, 2, 64)
    nv_p = np.stack([r['nv_p'] for r in R], 1).reshape(NL, n, 128, 2, 64)
    npool_p = np.stack([r['npool_p'] for r in R], 1)
    nre_p = np.stack([r['nre_p'] for r in R], 1).reshape(NL, n, 32, 64)
    nim_p = np.stack([r['nim_p'] for r in R], 1).reshape(NL, n, 32, 64)
    nk_s = cat('nk_s', 1).reshape(NL, n * NSEQ, 128, 2, 64)
    nv_s = cat('nv_s', 1).reshape(NL, n * NSEQ, 128, 2, 64)
    npool_s = cat('npool_s', 1)
    nre_s = cat('nre_s', 1).reshape(NL, n * NSEQ, 32, 64)
    nim_s = cat('nim_s', 1).reshape(NL, n * NSEQ, 32, 64)
    outs = (y_prompt, y_sample, nk_p, nv_p, npool_p, nre_p, nim_p, nk_s, nv_s, npool_s, nre_s, nim_s)
    return tuple(np.ascontiguousarray(o, dtype=np.float32) for o in outs)


def kernel(**inputs):
    from contextlib import ExitStack
    nc = bass.Bass("TRN2", target_bir_lowering=False)
    cfg = {}
    with ExitStack() as es:
        P = build(nc, cfg)
        P.finish(es)
    in_maps = make_in_maps(inputs, 8)
    res = run_bass_kernel_spmd(nc, in_maps, core_ids=list(range(8)))
    return assemble(res.results)
```
